# Optimizing a Trainium2 kernel written in Bass

```python
import math
import jax, jax.numpy as jnp
from jax import lax
import numpy as np

D_MODEL = 2048
BATCH = 4
SEQ = 8192
DEPTH = 2

MIX_WIDTH = D_MODEL
S5_WIDTH = MIX_WIDTH // 4
SGU_WIDTH = MIX_WIDTH // 2
POOL_WIDTH = MIX_WIDTH - S5_WIDTH - SGU_WIDTH
S5_GROUP_CH = 16
S5_GROUPS = S5_WIDTH // S5_GROUP_CH
S5_STATE = 64
DT_MIN = 0.001
DT_MAX = 0.1
CHUNK = 128
SGU_HEAD_DIM = 128
SGU_HEADS = SGU_WIDTH // SGU_HEAD_DIM
POOL_WINDOWS = (2, 4, 8, 16)
POOL_GROUPS = len(POOL_WINDOWS)
POOL_GROUP_CH = POOL_WIDTH // POOL_GROUPS
SPLIT_SIZES = (S5_WIDTH, SGU_WIDTH, SGU_WIDTH, POOL_WIDTH, S5_WIDTH, SGU_WIDTH, POOL_WIDTH)
IN_COLS = sum(SPLIT_SIZES)
SPLIT_POINTS = tuple(int(s) for s in np.cumsum(SPLIT_SIZES)[:-1])
RMS_EPS = 1e-6
LN_EPS = 1e-5

kernel_name = 'hymba_style_s5_gmlp_pool_hybrid'


def rms_norm(x, g):
    xf = x.astype(jnp.float32)
    y = xf * lax.rsqrt(jnp.mean(xf * xf, axis=-1, keepdims=True) + RMS_EPS)
    return (y * g.astype(jnp.float32)).astype(x.dtype)


def s5_mixer(xa, lam_re, lam_im, b_re, b_im, c_re, c_im, d_skip, log_dt, w_glu, b_glu):
    bsz, seq, _ = xa.shape
    f32 = jnp.float32
    xg = xa.astype(f32).reshape(bsz, seq, S5_GROUPS, S5_GROUP_CH)
    lam = lax.complex(lam_re.astype(f32), lam_im.astype(f32))
    dt = jnp.exp(log_dt.astype(f32))[:, None]
    lam_bar = jnp.exp(lam * dt)
    b = lax.complex(b_re.astype(f32), b_im.astype(f32))
    b_bar = ((lam_bar - 1.0) / lam)[..., None] * b
    c = lax.complex(c_re.astype(f32), c_im.astype(f32))
    bu = jnp.einsum('blgh,gph->blgp', xg.astype(jnp.complex64), b_bar)
    a = jnp.broadcast_to(lam_bar, (1, seq) + lam_bar.shape)

    def combine(left, right):
        a_l, b_l = left
        a_r, b_r = right
        return a_r * a_l, a_r * b_l + b_r

    _, states = lax.associative_scan(combine, (a, bu), axis=1)
    y = jnp.einsum('blgp,ghp->blgh', states, c).real + d_skip.astype(f32) * xg
    y = jax.nn.gelu(y.reshape(bsz, seq, S5_WIDTH)).astype(xa.dtype)
    return y * jax.nn.sigmoid(y @ w_glu + b_glu)


def sgu_mixer(u, v, ln_g, ln_b, w_s, b_s):
    bsz, seq, _ = v.shape
    u = jax.nn.gelu(u)
    vf = jax.nn.gelu(v).astype(jnp.float32)
    mu = jnp.mean(vf, axis=-1, keepdims=True)
    var = jnp.mean(jnp.square(vf - mu), axis=-1, keepdims=True)
    vn = ((vf - mu) * lax.rsqrt(var + LN_EPS) * ln_g.astype(jnp.float32)
          + ln_b.astype(jnp.float32)).astype(v.dtype)
    vn = vn.reshape(bsz, seq // CHUNK, CHUNK, SGU_HEADS, SGU_HEAD_DIM)
    causal = jnp.tril(jnp.ones((CHUNK, CHUNK), dtype=bool))
    ws = jnp.where(causal[None], w_s, jnp.zeros_like(w_s))
    s = jnp.einsum('hts,bcshd->bcthd', ws, vn) + jnp.transpose(b_s)[:, :, None]
    return u * s.reshape(bsz, seq, SGU_WIDTH)


def pool_mixer(xc, w_pool, pool_scale):
    bsz, seq, _ = xc.shape
    xg = xc.astype(jnp.float32).reshape(bsz, seq, POOL_GROUPS, POOL_GROUP_CH)
    cs = jnp.cumsum(xg, axis=1)
    pos = jnp.arange(1, seq + 1, dtype=jnp.float32)[None, :, None]
    outs = []
    for g, w in enumerate(POOL_WINDOWS):
        c = cs[:, :, g]
        lagged = jnp.pad(c[:, :seq - w], ((0, 0), (w, 0), (0, 0)))
        mean = (c - lagged) / jnp.minimum(pos, float(w))
        outs.append(mean - xg[:, :, g])
    p = jnp.stack(outs, axis=2).astype(xc.dtype)
    y = jnp.einsum('blgc,gcd->blgd', p, w_pool).reshape(bsz, seq, POOL_WIDTH)
    return y * pool_scale


def setup_inputs(seed: int = 0) -> dict:
    key = jax.random.key(seed)
    ks = jax.random.split(key, 24)
    f32 = jnp.float32
    nrm = lambda k, shape: jax.random.normal(k, shape, dtype=f32)
    L, D = DEPTH, D_MODEL
    G, P, H = S5_GROUPS, S5_STATE, S5_GROUP_CH
    x = nrm(ks[0], (BATCH, SEQ, D))
    norm_g = 1.0 + 0.02 * nrm(ks[1], (L, D))
    w_in = nrm(ks[2], (L, D, IN_COLS)) * D ** -0.5
    lam_re = -0.5 + 0.01 * nrm(ks[3], (L, G, P))
    lam_im = math.pi * jnp.arange(P, dtype=f32)[None, None, :] + 0.01 * nrm(ks[4], (L, G, P))
    b_re = nrm(ks[5], (L, G, P, H)) * (2.0 * H) ** -0.5
    b_im = nrm(ks[6], (L, G, P, H)) * (2.0 * H) ** -0.5
    c_re = nrm(ks[7], (L, G, H, P)) * P ** -0.5
    c_im = nrm(ks[8], (L, G, H, P)) * P ** -0.5
    d_skip = nrm(ks[9], (L, G, H))
    log_dt = jax.random.uniform(ks[10], (L, G), dtype=f32,
                                minval=math.log(DT_MIN), maxval=math.log(DT_MAX))
    w_glu = nrm(ks[11], (L, S5_WIDTH, S5_WIDTH)) * S5_WIDTH ** -0.5
    b_glu = 0.01 * nrm(ks[12], (L, S5_WIDTH))
    ln_g = 1.0 + 0.02 * nrm(ks[13], (L, SGU_WIDTH))
    ln_b = 0.02 * nrm(ks[14], (L, SGU_WIDTH))
    w_s = nrm(ks[15], (L, SGU_HEADS, CHUNK, CHUNK)) * CHUNK ** -0.5
    b_s = 1.0 + 0.02 * nrm(ks[16], (L, SGU_HEADS, CHUNK))
    w_pool = nrm(ks[17], (L, POOL_GROUPS, POOL_GROUP_CH, POOL_GROUP_CH)) * POOL_GROUP_CH ** -0.5
    pool_scale = 1.0 + 0.1 * nrm(ks[18], (L, POOL_WIDTH))
    w_out = nrm(ks[19], (L, MIX_WIDTH, D)) * MIX_WIDTH ** -0.5
    final_g = 1.0 + 0.02 * nrm(ks[20], (D,))
    return {'x': x, 'norm_g': norm_g, 'w_in': w_in, 'lam_re': lam_re, 'lam_im': lam_im,
            'b_re': b_re, 'b_im': b_im, 'c_re': c_re, 'c_im': c_im, 'd_skip': d_skip,
            'log_dt': log_dt, 'w_glu': w_glu, 'b_glu': b_glu, 'ln_g': ln_g, 'ln_b': ln_b,
            'w_s': w_s, 'b_s': b_s, 'w_pool': w_pool, 'pool_scale': pool_scale,
            'w_out': w_out, 'final_g': final_g}


def reference(x, norm_g, w_in, lam_re, lam_im, b_re, b_im, c_re, c_im, d_skip, log_dt,
              w_glu, b_glu, ln_g, ln_b, w_s, b_s, w_pool, pool_scale, w_out, final_g):
    for i in range(DEPTH):
        h = rms_norm(x, norm_g[i])
        z = h @ w_in[i]
        xa, u, v, xc, ga, gb, gc = jnp.split(z, SPLIT_POINTS, axis=-1)
        ya = s5_mixer(xa, lam_re[i], lam_im[i], b_re[i], b_im[i], c_re[i], c_im[i],
                      d_skip[i], log_dt[i], w_glu[i], b_glu[i]) * jax.nn.silu(ga)
        yb = sgu_mixer(u, v, ln_g[i], ln_b[i], w_s[i], b_s[i]) * jax.nn.silu(gb)
        yc = pool_mixer(xc, w_pool[i], pool_scale[i]) * jax.nn.silu(gc)
        y = jnp.concatenate([ya.astype(x.dtype), yb.astype(x.dtype), yc.astype(x.dtype)], axis=-1)
        x = x + y @ w_out[i]
    return rms_norm(x, final_g)
```

```python
import numpy as np
import ml_dtypes
from contextlib import ExitStack
import concourse.bass as bass
import concourse.mybir as mybir
from concourse.bass_utils import run_bass_kernel_spmd

F32 = mybir.dt.float32
BF16 = mybir.dt.bfloat16
AF = mybir.ActivationFunctionType
ALU = mybir.AluOpType

D = 2048
NK = 16
INC = 5120
TT = 256
NS = 2
PI = float(np.pi)
CFG = {"seq": 8192, "depth": 2, "batch": 4, "gelu": "tanh"}
I32 = mybir.dt.int32


class _Stop(Exception):
    pass


def stop_at(n):
    if CFG.get("stop") == n:
        raise _Stop()


class Res:
    __slots__ = ("w", "r", "excl")

    def __init__(self, excl=False):
        self.w = None
        self.r = {}
        self.excl = excl


class Eng:
    def __init__(self, e, sem, key, is_pe=False):
        self.e = e
        self.sem = sem
        self.key = key
        self.is_pe = is_pe
        self.n = 0
        self.seen = {}

    def need(self, st):
        if st is None:
            return
        key, sem, val = st
        if key == self.key and self.is_pe:
            return
        if self.seen.get(key, 0) >= val:
            return
        self.e.wait_ge(sem, val)
        self.seen[key] = val


class Slot:
    def __init__(self, sem, key):
        self.sem = sem
        self.key = key
        self.n = 0


def _split(reads, writes):
    ex = [r for r in reads if r.excl]
    if ex:
        reads = [r for r in reads if not r.excl]
        writes = list(writes) + ex
    return reads, writes


def _deps(E, reads, writes):
    reads, writes = _split(reads, writes)
    for r in reads:
        E.need(r.w)
    for w in writes:
        E.need(w.w)
        for st in w.r.values():
            E.need(st)


def _upd(st, reads, writes):
    reads, writes = _split(reads, writes)
    for w in writes:
        w.w = st
        w.r = {}
    for r in reads:
        r.r[st[0]] = st


def op(E, fn, reads=(), writes=()):
    _deps(E, reads, writes)
    ins = fn()
    E.n += 1
    ins.then_inc(E.sem, 1)
    _upd((E.key, E.sem, E.n), reads, writes)


def grp(E, fns, reads=(), writes=()):
    _deps(E, reads, writes)
    ins = None
    for fn in fns:
        ins = fn()
    E.n += 1
    ins.then_inc(E.sem, 1)
    _upd((E.key, E.sem, E.n), reads, writes)


def dma(Q, slot, fns, reads=(), writes=()):
    _deps(Q, reads, writes)
    if slot.n:
        Q.need((slot.key, slot.sem, slot.n))
    for fn in fns:
        fn().then_inc(slot.sem, 16)
        slot.n += 16
    _upd((slot.key, slot.sem, slot.n), reads, writes)


def build(seq, depth_unused=1):
    depth = 1
    ntile = seq // TT
    nstep = ntile + 2
    nc = bass.Bass("TRN2", target_bir_lowering=False)

    def din(name, shape, dt=F32):
        return nc.dram_tensor(name, list(shape), dt, kind="ExternalInput").ap()

    x_in = din("x", [seq, D])
    norm_g = din("norm_g", [depth, D])
    w_in = din("w_in", [depth, D, INC])
    lam_re = din("lam_re", [depth, 32, 64])
    lam_im = din("lam_im", [depth, 32, 64])
    b_re = din("b_re", [depth, 32, 64, 16])
    b_im = din("b_im", [depth, 32, 64, 16])
    c_re = din("c_re", [depth, 32, 16, 64])
    c_im = din("c_im", [depth, 32, 16, 64])
    d_skip = din("d_skip", [depth, 32, 16])
    log_dt = din("log_dt", [depth, 32])
    w_glu = din("w_glu", [depth, 512, 512])
    b_glu = din("b_glu", [depth, 512])
    ln_g = din("ln_g", [depth, 1024])
    ln_b = din("ln_b", [depth, 1024])
    w_s = din("w_s", [depth, 8, 128, 128])
    b_s = din("b_s", [depth, 8, 128])
    w_pool = din("w_pool", [depth, 4, 128, 128])
    pool_scale = din("pool_scale", [depth, 512])
    w_out = din("w_out", [depth, D, D])
    final_g = din("final_g", [1, D])
    c_identb = din("c_identb", [128, 128], BF16)
    c_identf = din("c_identf", [128, 128])
    c_tril = din("c_tril", [128, 128])
    c_tio = din("c_tio", [1, 256])
    c_pdiv0 = din("c_pdiv0", [1, 4 * 16])
    c_pm = din("c_pm", [128, 4])
    c_pdiv2 = din("c_pdiv2", [1, 4 * 16])
    c_sel = din("c_sel", [1, 1], I32)
    out = nc.dram_tensor("out", [seq, D], F32, kind="ExternalOutput").ap()
    wsc_in = nc.dram_tensor("wsc_in", [depth, 40, 128, NK, 128], BF16, kind="Internal").ap()
    wsc_out = nc.dram_tensor("wsc_out", [depth, 16, 128, NK, 128], BF16, kind="Internal").ap()
    sendb = [nc.dram_tensor("sendb%d" % i, [TT, D], F32, kind="Internal") for i in range(2)]
    Ub = [nc.dram_tensor("Ub%d" % i, [2, TT, D], F32, kind="Internal") for i in range(2)]

    es = ExitStack()
    with es:
        def sb(name, shape, dt=F32):
            return es.enter_context(nc.sbuf_tensor(name, list(shape), dt))

        def pst(name, shape, dt=F32):
            return es.enter_context(nc.psum_tensor(name, list(shape), dt))

        def sem(name):
            return es.enter_context(nc.semaphore(name))

        P = Eng(nc.tensor, sem("s_pe"), "pe", True)
        A = Eng(nc.scalar, sem("s_act"), "act")
        V = Eng(nc.vector, sem("s_dve"), "dve")
        G = Eng(nc.gpsimd, sem("s_pool"), "pool")
        SY = Eng(nc.sync, sem("s_sy"), "sy")
        nslot = [0]

        def slot():
            nslot[0] += 1
            return Slot(sem("s_d%d" % nslot[0]), "d%d" % nslot[0])

        identb = sb("identb", [128, 128], BF16)
        identf = sb("identf", [128, 128])
        tril = sb("tril", [128, 128])
        tio = sb("tio", [128, 256])
        pdiv0 = sb("pdiv0", [128, 4, 16])
        pdiv2 = sb("pdiv2", [128, 4, 16])
        selt = sb("selt", [1, 1], I32)
        pm = sb("pm", [128, 4])
        fg_rep = sb("fg_rep", [128, D])
        xt = [sb("xt%d" % i, [128, D]) for i in range(NS)]
        hp = sb("hp", [128, D], BF16)
        hTs = [sb("hT%d" % i, [128, NK, TT], BF16) for i in range(2)]
        NW = 5
        W = [sb("w%d" % i, [128, NK, 128], BF16) for i in range(NW)]
        gv = [sb("gv%d" % i, [128, 1024]) for i in range(NS)]
        vn = [sb("vn%d" % i, [128, 1024], BF16) for i in range(NS)]
        tA = [sb("tA%d" % i, [128, TT]) for i in range(2)]
        tG = [sb("tG%d" % i, [128, TT]) for i in range(2)]
        tS = [sb("tS%d" % i, [128, TT]) for i in range(2)]
        tP = [sb("tP%d" % i, [128, TT]) for i in range(2)]
        tW = sb("tW", [128, 2 * TT])
        bglh = sb("bglh", [128, 4])
        yT = sb("yT", [128, NK, TT], BF16)
        xaf = sb("xaf", [128, 4, TT])
        xab = sb("xab", [128, 4, TT], BF16)
        sga = sb("sga", [128, 4, TT], BF16)
        sgc = sb("sgc", [128, 4, TT], BF16)
        xc = sb("xc", [128, 4, 16 + TT])
        pa = sb("pa", [128, 16 + TT])
        pb = sb("pb", [128, 16 + TT])
        pp = sb("pp", [128, 4, TT], BF16)
        cosT = sb("cosT", [128, 16, TT])
        sinT = sb("sinT", [128, 16, TT])
        qa = [sb("qa%d" % i, [128, 2, TT]) for i in range(2)]
        qb = [sb("qb%d" % i, [128, 2, TT]) for i in range(2)]
        qu = [sb("qu%d" % i, [128, 2, TT]) for i in range(2)]
        qt = [sb("qt%d" % i, [128, 2, TT]) for i in range(2)]
        t1, t2, t3, t4, Tre, Tim = qa[0], qa[1], qb[0], qb[1], qu[0], qu[1]
        Sl = sb("Sl", [128, 2, 16])
        kit = qt[1][:].bitcast(mybir.dt.int32)
        Sb = sb("Sb", [128, 4, 4, TT], BF16)
        Cpad = sb("Cpad", [128, 16, 3, 128], BF16)
        BTr = sb("BTr", [128, 4, 2, 128], BF16)
        BTi = sb("BTi", [128, 4, 2, 128], BF16)
        ypre = sb("ypre", [128, TT])
        yg = sb("yg", [128, 4, TT])
        ygb = sb("ygb", [128, 4, TT], BF16)
        wglu = sb("wglu", [128, 4, 512], BF16)
        wpool = sb("wpool", [128, 4, 128], BF16)
        wsT = sb("wsT", [128, 8, 128], BF16)
        Bt = sb("Bt", [128, 8, 128])
        lng = sb("lng", [128, 8])
        lnb = sb("lnb", [128, 8])
        psc = sb("psc", [128, 4])
        bgl = sb("bgl", [128, 4])
        dsk = sb("dsk", [128, 4])
        gk = sb("gk", [128, NK])
        Tin_re = sb("Tin_re", [128, 16])
        Tin_im = sb("Tin_im", [128, 16])
        rr = sb("rr", [128, 16])
        th = sb("th", [128, 16])
        sp = [sb("sp%d" % i, [128, 16]) for i in range(12)]
        Bre = t3[:].rearrange("p a t -> p (a t)")[:, 0:256].rearrange("p (g h) -> p g h", g=16)
        Bim = t4[:].rearrange("p a t -> p (a t)")[:, 0:256].rearrange("p (g h) -> p g h", g=16)
        BBr = Tre[:].rearrange("p a t -> p (a t)")[:, 0:256].rearrange("p (g h) -> p g h", g=16)
        BBi = Tim[:].rearrange("p a t -> p (a t)")[:, 0:256].rearrange("p (g h) -> p g h", g=16)
        in3r = t1[:].rearrange("p a t -> p (a t)")
        in3i = t2[:].rearrange("p a t -> p (a t)")
        Cre = gv[0][:, 0:256].rearrange("p (g h) -> p g h", g=16)
        Cim = gv[0][:, 256:512].rearrange("p (g h) -> p g h", g=16)
        ss = sb("ss", [128, 4])
        rstd = sb("rstd", [128, 4])
        bst = [sb("bst%d" % i, [128, 2, 6]) for i in range(NS)]
        mv = sb("mv", [128, NS, 2])
        cr = sb("cr", [128, 4, 16])
        cthk = sb("cthk", [128, 16])
        sthk = sb("sthk", [128, 16])
        ki16 = sb("ki16", [128, 16], mybir.dt.int32)
        cvi = [xt[i][:].rearrange("p (k c) -> p k c", k=NK) for i in range(2)]
        cvo = [yT[:, 8 * i:8 * i + 8, :].rearrange("p a (b c) -> p (a b) c", c=128) for i in range(2)]
        ones_b = sb("ones_b", [128, 128], BF16)

        pA = [pst("pA%d" % i, [128, 512]) for i in range(2)]
        pV = [pst("pV%d" % i, [128, 512]) for i in range(2)]
        pS = [pst("pS%d" % i, [128, 512]) for i in range(2)]
        pY = pst("pY", [128, 512])
        pT = pst("pT", [128, 1024], BF16)

        blk = es.enter_context(nc.Block())

        R = {}

        def res(name):
            if name not in R:
                R[name] = Res(excl=name in ("pA0", "pA1", "pV0", "pV1", "pS0", "pS1", "pY", "pT"))
            return R[name]

        sl_c = slot()
        sl_x = [slot() for _ in range(NS)]
        sl_w = [slot() for _ in range(NW)]
        sl_st = [slot() for _ in range(NS)]
        sl_sd = [slot() for _ in range(NS)]
        sl_cv = [slot() for _ in range(2)]
        sl_cvo = [slot() for _ in range(2)]
        sl_p = slot()

        gelu_f = AF.Gelu_apprx_tanh if CFG["gelu"] == "tanh" else AF.Gelu

        try:
            dma(SY, sl_c, [
                lambda: nc.sync.dma_start(out=identb[:], in_=c_identb[:, :]),
                lambda: nc.sync.dma_start(out=identf[:], in_=c_identf[:, :]),
                lambda: nc.sync.dma_start(out=tril[:], in_=c_tril[:, :]),
                lambda: nc.sync.dma_start(out=tio[:], in_=c_tio[0:1, :].broadcast_to([128, 256])),
                lambda: nc.sync.dma_start(out=pdiv0[:].rearrange("p g t -> p (g t)"),
                                          in_=c_pdiv0[0:1, :].broadcast_to([128, 64])),
                lambda: nc.sync.dma_start(out=pm[:], in_=c_pm[:, :]),
                lambda: nc.sync.dma_start(out=pdiv2[:].rearrange("p g t -> p (g t)"),
                                          in_=c_pdiv2[0:1, :].broadcast_to([128, 64])),
                lambda: nc.sync.dma_start(out=selt[:], in_=c_sel[:, :]),
                lambda: nc.sync.dma_start(out=fg_rep[:], in_=final_g[0:1, :].broadcast_to([128, D])),
            ], writes=[res("const")])
            RC = [res("const")]

            selr = es.enter_context(nc.gpsimd.register("selr"))
            G.need(res("const").w)
            nc.gpsimd.reg_load(selr, selt[:1, :1])
            SELIDX = nc.gpsimd.snap(selr, min_val=0, max_val=1)
            sl_cc = Slot(sem("s_cc"), "cc")
            sl_u = slot()
            op(G, lambda: nc.gpsimd.memset(xt[0][:], 0.0), writes=[res("xt0")])
            for par in range(2):
                dma(G, sl_u, [
                    (lambda par=par, r=r: nc.gpsimd.dma_start(out=Ub[par].ap()[0][r * 128:(r + 1) * 128, :], in_=xt[0][:]))
                    for r in range(2)] + [
                    lambda par=par: nc.gpsimd.dma_start(out=Ub[par].ap()[1], in_=x_in[par * TT:(par + 1) * TT, :])],
                    reads=[res("xt0")], writes=[res("U%d" % par)])

            def usrc(step, ts_):
                return Ub[step % 2].ap()[SELIDX][ts_ * 128:(ts_ + 1) * 128, :]

            cnt = 0
            for l in range(depth):
                dma(SY, sl_p, [lambda l=l: nc.sync.dma_start(
                    out=gk[:], in_=norm_g[l].rearrange("(k p) -> p k", p=128),
                    allow_slow_non_contiguous=True)], writes=[res("gk")])
                for m in range(40 + 16):
                    s = cnt % 2
                    cnt += 1
                    if m < 40:
                        src = w_in[l][:, m * 128:(m + 1) * 128].rearrange("(k p) c -> p k c", p=128)
                        dst = wsc_in[l, m]
                    else:
                        src = w_out[l][:, (m - 40) * 128:(m - 39) * 128].rearrange("(k p) c -> p k c", p=128)
                        dst = wsc_out[l, m - 40]
                    dma(SY, sl_cv[s], [lambda s=s, src=src: nc.sync.dma_start(out=cvi[s][:], in_=src)],
                        writes=[res("xt%d" % s)])
                    E = V if (cnt % 2 == 0) else G
                    if m < 40:
                        op(E, lambda E=E, s=s: E.e.tensor_tensor(
                            out=cvo[s][:], in0=cvi[s][:],
                            in1=gk[:].unsqueeze(2).broadcast_to([128, NK, 128]), op=ALU.mult),
                           reads=[res("xt%d" % s), res("gk")], writes=[res("cvo%d" % s)])
                    else:
                        op(E, lambda E=E, s=s: E.e.tensor_copy(out=cvo[s][:], in_=cvi[s][:]),
                           reads=[res("xt%d" % s)], writes=[res("cvo%d" % s)])
                    dma(SY, sl_cvo[s], [lambda s=s, dst=dst: nc.sync.dma_start(out=dst, in_=cvo[s][:])],
                        reads=[res("cvo%d" % s)], writes=[res("wsc%d" % l)])

            for nm_ in ("cvo0", "cvo1"):
                R_ = res(nm_)
                res("yT").r.update(R_.r)
                if R_.w is not None:
                    res("yT").r[R_.w[0]] = R_.w
            stop_at(1)
            w_i = [0]

            def load_w(src_ap, rname):
                s = w_i[0] % NW
                w_i[0] += 1
                dma(SY, sl_w[s], [lambda: nc.sync.dma_start(out=W[s][:], in_=src_ap)],
                    reads=[res(rname)], writes=[res("w%d" % s)])
                return s

            pA_i = [0]
            pV_i = [0]
            CUR = {"hT": hTs[0], "hi": 0}

            def zcols(l, ms, evac, pre=None):
                b = pA_i[0] % 2
                pA_i[0] += 1
                hT = CUR["hT"]
                rh = res("hT%d" % CUR["hi"])
                for h, m in enumerate(ms):
                    s = load_w(wsc_in[l, m], "wsc%d" % l)
                    grp(P, [(lambda k=k, s=s, h=h, b=b: nc.tensor.matmul(
                        pA[b][:, h * TT:(h + 1) * TT], W[s][:, k, :], hT[:, k, :],
                        start=(k == 0), stop=(k == NK - 1))) for k in range(NK)],
                        reads=[res("w%d" % s), rh], writes=[res("pA%d" % b)])
                if pre is not None:
                    pre()
                evac(pA[b], res("pA%d" % b))

            def tokmaj(srcs, rname, lhs_fn, rlhs, evac, after=None):
                slots = [load_w(a_, rname) for a_ in srcs]
                for ts_ in range(NS):
                    b = pV_i[0] % 2
                    pV_i[0] += 1
                    for h, s in enumerate(slots):
                        grp(P, [(lambda k=k, s=s, h=h, b=b, ts_=ts_: nc.tensor.matmul(
                            pV[b][:, h * 128:(h + 1) * 128], lhs_fn(k, ts_), W[s][:, k, :],
                            start=(k == 0), stop=(k == NK - 1))) for k in range(NK)],
                            reads=[res("w%d" % s)] + rlhs, writes=[res("pV%d" % b)])
                    if after is not None:
                        after()
                    evac(ts_, pV[b], res("pV%d" % b))

            def prefetch_gen(srcx, tn, hi):
                hTn = hTs[hi]
                for ts_ in range(NS):
                    dma(G, sl_x[ts_], [lambda ts_=ts_: nc.gpsimd.dma_start(
                        out=xt[ts_][:], in_=usrc(tn, ts_))],
                        reads=[res("U%d" % (tn % 2))], writes=[res("xt%d" % ts_)])
                yield
                for ts_ in range(NS):
                    op(A, lambda ts_=ts_: nc.scalar.activation(
                        out=hp[:], in_=xt[ts_][:], func=AF.Square, accum_out=ss[:, ts_:ts_ + 1]),
                       reads=[res("xt%d" % ts_)], writes=[res("hp"), res("ss")])
                op(V, lambda: nc.vector.tensor_scalar(
                    out=rstd[:, 0:2], in0=ss[:, 0:2], scalar1=1.0 / D, scalar2=1e-6,
                    op0=ALU.mult, op1=ALU.add), reads=[res("ss")], writes=[res("rstd")])
                op(A, lambda: nc.scalar.activation(out=rstd[:, 0:2], in_=rstd[:, 0:2], func=AF.Sqrt),
                   reads=[res("rstd")], writes=[res("rstd")])
                op(V, lambda: nc.vector.reciprocal(out=rstd[:, 0:2], in_=rstd[:, 0:2]),
                   reads=[res("rstd")], writes=[res("rstd")])
                yield
                for ts_ in range(NS):
                    op(A, lambda ts_=ts_: nc.scalar.activation(
                        out=hp[:], in_=xt[ts_][:], func=AF.Copy, scale=rstd[:, ts_:ts_ + 1]),
                       reads=[res("xt%d" % ts_), res("rstd")], writes=[res("hp")])
                    yield
                    for kb in range(2):
                        grp(P, [(lambda k=k, kb=kb: nc.tensor.transpose(
                            out=pT[:, (k - kb * 8) * 128:(k - kb * 8 + 1) * 128],
                            in_=hp[:, k * 128:(k + 1) * 128], identity=identb[:]))
                            for k in range(kb * 8, kb * 8 + 8)],
                            reads=[res("hp")] + RC, writes=[res("pT")])
                        if kb == 0:
                            op(V, lambda ts_=ts_, kb=kb: nc.vector.tensor_copy(
                                out=hTn[:, kb * 8:kb * 8 + 8, ts_ * 128:(ts_ + 1) * 128],
                                in_=pT[:].rearrange("p (a b) -> p a b", a=8)),
                               reads=[res("pT")], writes=[res("hT%d" % hi)])
                        else:
                            op(A, lambda ts_=ts_, kb=kb: nc.scalar.copy(
                                out=hTn[:, kb * 8:kb * 8 + 8, ts_ * 128:(ts_ + 1) * 128],
                                in_=pT[:].rearrange("p (a b) -> p a b", a=8)),
                               reads=[res("pT")], writes=[res("hT%d" % hi)])
                    yield

            gtile = [0]
            for l in range(depth):
                src = None
                last = (l == depth - 1)
                RL = res("layerw")
                cv0f = cvi[0][:].rearrange("p k c -> p (k c)")
                cv1f = cvi[1][:].rearrange("p k c -> p (k c)")
                dma(SY, sl_p, [lambda: nc.sync.dma_start(
                    out=cv0f[:, 0:512].rearrange("p (g d) -> p g d", g=4),
                    in_=w_pool[l].rearrange("g c d -> c g d"))], writes=[res("xt0")])
                op(V, lambda: nc.vector.tensor_copy(
                    out=wpool[:], in_=cv0f[:, 0:512].rearrange("p (g d) -> p g d", g=4)),
                   reads=[res("xt0")], writes=[RL])
                dma(SY, sl_p, [lambda: nc.sync.dma_start(
                    out=cv1f[:, 0:2048].rearrange("p (k n) -> p k n", k=4),
                    in_=w_glu[l].rearrange("(k p) n -> p k n", p=128))], writes=[res("xt1")])
                op(V, lambda: nc.vector.tensor_copy(
                    out=wglu[:], in_=cv1f[:, 0:2048].rearrange("p (k n) -> p k n", k=4)),
                   reads=[res("xt1")], writes=[RL])
                dma(SY, sl_p, [lambda: nc.sync.dma_start(
                    out=cvi[0][:, 0:8, :], in_=w_s[l].rearrange("h t s -> t h s"))], writes=[res("xt0")])
                op(V, lambda: nc.vector.tensor_tensor(
                    out=cvi[0][:, 0:8, :], in0=cvi[0][:, 0:8, :],
                    in1=tril[:].unsqueeze(1).broadcast_to([128, 8, 128]), op=ALU.mult),
                   reads=[res("xt0")] + RC, writes=[res("xt0")])
                for hh in range(8):
                    half = hh % 4
                    if half == 0:
                        pass
                    grp(P, [lambda hh=hh, half=half: nc.tensor.transpose(
                        out=pY[:, half * 128:(half + 1) * 128], in_=cvi[0][:, hh, :], identity=identf[:])],
                        reads=[res("xt0")] + RC, writes=[res("pY")])
                    if half == 3:
                        op(V, lambda hh=hh: nc.vector.tensor_copy(
                            out=wsT[:, hh - 3:hh + 1, :], in_=pY[:].rearrange("p (a b) -> p a b", a=4)),
                           reads=[res("pY")], writes=[RL])
                op(G, lambda: nc.gpsimd.memset(ones_b[:], 1.0), writes=[res("ones")])
                dma(SY, sl_p, [
                    lambda: nc.sync.dma_start(out=Bt[:].rearrange("p h t -> p (h t)"),
                                              in_=b_s[l:l + 1].rearrange("o h t -> o (h t)").broadcast_to([128, 1024])),
                    lambda: nc.sync.dma_start(out=lng[:], in_=ln_g[l].rearrange("(h d) -> d h", d=128),
                                              allow_slow_non_contiguous=True),
                    lambda: nc.sync.dma_start(out=lnb[:], in_=ln_b[l].rearrange("(h d) -> d h", d=128),
                                              allow_slow_non_contiguous=True),
                    lambda: nc.sync.dma_start(out=psc[:], in_=pool_scale[l].rearrange("(g d) -> d g", d=128),
                                              allow_slow_non_contiguous=True),
                    lambda: nc.sync.dma_start(out=bgl[:], in_=b_glu[l].rearrange("(g d) -> d g", d=128),
                                              allow_slow_non_contiguous=True),
                    lambda: nc.sync.dma_start(out=dsk[:], in_=d_skip[l].rearrange("(c g) h -> (g h) c", g=8),
                                              allow_slow_non_contiguous=True),
                ], writes=[res("lsmall")])
                op(V, lambda: nc.vector.tensor_scalar(out=bglh[:], in0=bgl[:], scalar1=0.5, scalar2=0.0, op0=ALU.mult, op1=ALU.add),
                   reads=[res("lsmall")], writes=[res("lsmall")])
                for hh in range(8):
                    grp(P, [lambda hh=hh: nc.tensor.matmul(
                        pY[:, 0:128], ones_b[:], wsT[:, hh, :], start=True, stop=True)],
                        reads=[res("ones"), RL], writes=[res("pY")])
                    op(V, lambda hh=hh: nc.vector.scalar_tensor_tensor(
                        out=Bt[:, hh, :], in0=pY[:, 0:128], scalar=lnb[:, hh:hh + 1], in1=Bt[:, hh, :],
                        op0=ALU.mult, op1=ALU.add),
                       reads=[res("pY"), res("lsmall")], writes=[res("lsmall")])

                stop_at(2)
                RS = res("s5w")
                ALIAS = [res("qa0"), res("qa1"), res("qb0"), res("qb1"), res("qu0"), res("qu1")]
                fns = []
                for gl in range(2):
                    ps_ = slice(gl * 64, (gl + 1) * 64)
                    fns += [
                        lambda gl=gl, ps_=ps_: nc.sync.dma_start(out=sp[0][ps_, :], in_=lam_re[l].rearrange("(gh gl) p -> gl p gh", gl=2)[gl], allow_slow_non_contiguous=True),
                        lambda gl=gl, ps_=ps_: nc.sync.dma_start(out=sp[1][ps_, :], in_=lam_im[l].rearrange("(gh gl) p -> gl p gh", gl=2)[gl], allow_slow_non_contiguous=True),
                        lambda gl=gl, ps_=ps_: nc.sync.dma_start(out=sp[2][ps_, :], in_=log_dt[l:l + 1].rearrange("o (gh gl) -> gl o gh", gl=2)[gl].broadcast_to([64, 16]), allow_slow_non_contiguous=True),
                        lambda gl=gl, ps_=ps_: nc.sync.dma_start(out=Bre[ps_], in_=b_re[l].rearrange("(gh gl) p h -> gl p gh h", gl=2)[gl], allow_slow_non_contiguous=True),
                        lambda gl=gl, ps_=ps_: nc.sync.dma_start(out=Bim[ps_], in_=b_im[l].rearrange("(gh gl) p h -> gl p gh h", gl=2)[gl], allow_slow_non_contiguous=True),
                    ]
                    for gh in range(16):
                        fns += [
                            lambda gl=gl, ps_=ps_, gh=gh: nc.sync.dma_start(out=Cre[ps_, gh, :], in_=c_re[l, 2 * gh + gl].rearrange("h p -> p h"), allow_slow_non_contiguous=True),
                            lambda gl=gl, ps_=ps_, gh=gh: nc.sync.dma_start(out=Cim[ps_, gh, :], in_=c_im[l, 2 * gh + gl].rearrange("h p -> p h"), allow_slow_non_contiguous=True),
                        ]
                dma(SY, sl_p, fns, writes=[res("s5raw"), res("gv0")] + ALIAS)
                RR = [res("s5raw")]
                lre, lim, ldt = sp[0], sp[1], sp[2]
                dt_, are, a1, sth, cth, lbr, lbi, den, qr, qi = sp[3], sp[4], sp[5], sp[6], sp[7], sp[8], sp[9], sp[10], sp[11], sp[2]
                RP = res("s5p")
                op(A, lambda: nc.scalar.activation(out=dt_[:], in_=ldt[:], func=AF.Exp), reads=RR, writes=[RP])
                op(V, lambda: nc.vector.tensor_tensor(out=are[:], in0=lre[:], in1=dt_[:], op=ALU.mult), reads=RR + [RP], writes=[RP])
                op(V, lambda: nc.vector.tensor_tensor(out=th[:], in0=lim[:], in1=dt_[:], op=ALU.mult), reads=RR + [RP], writes=[RP, RS])
                op(A, lambda: nc.scalar.activation(out=rr[:], in_=are[:], func=AF.Exp), reads=[RP], writes=[RP, RS])
                def sincos_small(dst, off, mul=1.0):
                    op(V, lambda: nc.vector.tensor_scalar(out=a1[:], in0=th[:], scalar1=mul / (2 * PI), scalar2=off / (2 * PI), op0=ALU.mult, op1=ALU.add), reads=[RP], writes=[RP])
                    op(V, lambda: nc.vector.tensor_copy(out=ki16[:], in_=a1[:]), reads=[RP], writes=[RP])
                    op(V, lambda: nc.vector.tensor_copy(out=den[:], in_=ki16[:]), reads=[RP], writes=[RP])
                    op(V, lambda: nc.vector.tensor_tensor(out=a1[:], in0=a1[:], in1=den[:], op=ALU.subtract), reads=[RP], writes=[RP])
                    op(V, lambda: nc.vector.tensor_scalar(out=den[:], in0=a1[:], scalar1=0.5, scalar2=1.0, op0=ALU.is_gt, op1=ALU.mult), reads=[RP], writes=[RP])
                    op(V, lambda: nc.vector.tensor_tensor(out=a1[:], in0=a1[:], in1=den[:], op=ALU.subtract), reads=[RP], writes=[RP])
                    op(V, lambda: nc.vector.tensor_scalar(out=den[:], in0=a1[:], scalar1=-0.5, scalar2=1.0, op0=ALU.is_lt, op1=ALU.mult), reads=[RP], writes=[RP])
                    op(V, lambda: nc.vector.tensor_tensor(out=a1[:], in0=a1[:], in1=den[:], op=ALU.add), reads=[RP], writes=[RP])
                    op(A, lambda: nc.scalar.activation(out=dst[:], in_=a1[:], func=AF.Sin, scale=2 * PI), reads=[RP], writes=[RP])
                sincos_small(sth, 0.0)
                sincos_small(cth, 0.5 * PI)
                MARK_CARRY_TABLES = True
                op(V, lambda: nc.vector.tensor_tensor(out=lbr[:], in0=rr[:], in1=cth[:], op=ALU.mult), reads=[RP], writes=[RP])
                op(V, lambda: nc.vector.tensor_tensor(out=lbi[:], in0=rr[:], in1=sth[:], op=ALU.mult), reads=[RP], writes=[RP])
                op(V, lambda: nc.vector.tensor_scalar(out=lbr[:], in0=lbr[:], scalar1=-1.0, scalar2=0.0, op0=ALU.add, op1=ALU.add), reads=[RP], writes=[RP])
                op(V, lambda: nc.vector.tensor_tensor(out=den[:], in0=lre[:], in1=lre[:], op=ALU.mult), reads=RR + [RP], writes=[RP])
                op(V, lambda: nc.vector.tensor_tensor(out=a1[:], in0=lim[:], in1=lim[:], op=ALU.mult), reads=RR + [RP], writes=[RP])
                op(V, lambda: nc.vector.tensor_tensor(out=den[:], in0=den[:], in1=a1[:], op=ALU.add), reads=[RP], writes=[RP])
                op(V, lambda: nc.vector.reciprocal(out=den[:], in_=den[:]), reads=[RP], writes=[RP])
                op(V, lambda: nc.vector.tensor_tensor(out=qr[:], in0=lbr[:], in1=lre[:], op=ALU.mult), reads=RR + [RP], writes=[RP])
                op(V, lambda: nc.vector.tensor_tensor(out=a1[:], in0=lbi[:], in1=lim[:], op=ALU.mult), reads=RR + [RP], writes=[RP])
                op(V, lambda: nc.vector.tensor_tensor(out=qr[:], in0=qr[:], in1=a1[:], op=ALU.add), reads=[RP], writes=[RP])
                op(V, lambda: nc.vector.tensor_tensor(out=qr[:], in0=qr[:], in1=den[:], op=ALU.mult), reads=[RP], writes=[RP])
                op(V, lambda: nc.vector.tensor_tensor(out=qi[:], in0=lbi[:], in1=lre[:], op=ALU.mult), reads=RR + [RP], writes=[RP])
                op(V, lambda: nc.vector.tensor_tensor(out=a1[:], in0=lbr[:], in1=lim[:], op=ALU.mult), reads=RR + [RP], writes=[RP])
                op(V, lambda: nc.vector.tensor_tensor(out=qi[:], in0=qi[:], in1=a1[:], op=ALU.subtract), reads=[RP], writes=[RP])
                op(V, lambda: nc.vector.tensor_tensor(out=qi[:], in0=qi[:], in1=den[:], op=ALU.mult), reads=[RP], writes=[RP])
                sincos_small(sth, 0.0, float(TT))
                op(V, lambda: nc.vector.tensor_copy(out=sthk[:], in_=sth[:]), reads=[RP], writes=[RS])
                sincos_small(cth, 0.5 * PI, float(TT))
                op(V, lambda: nc.vector.tensor_copy(out=cthk[:], in_=cth[:]), reads=[RP], writes=[RS])
                qrb = qr[:].unsqueeze(2).broadcast_to([128, 16, 16])
                qib = qi[:].unsqueeze(2).broadcast_to([128, 16, 16])
                i3r = in3r[:].rearrange("p (g a h) -> p g a h", g=16, a=2)
                i3i = in3i[:].rearrange("p (g a h) -> p g a h", g=16, a=2)
                op(V, lambda: nc.vector.tensor_tensor(out=BBr[:], in0=Bre[:], in1=qrb, op=ALU.mult), reads=RR + [RP], writes=[RP] + ALIAS)
                op(V, lambda: nc.vector.tensor_tensor(out=i3r[:, :, 0, :], in0=Bim[:], in1=qib, op=ALU.mult), reads=RR + [RP], writes=[RP] + ALIAS)
                op(V, lambda: nc.vector.tensor_tensor(out=BBr[:], in0=BBr[:], in1=i3r[:, :, 0, :], op=ALU.subtract), reads=[RP], writes=[RP] + ALIAS)
                op(V, lambda: nc.vector.tensor_tensor(out=BBi[:], in0=Bim[:], in1=qrb, op=ALU.mult), reads=RR + [RP], writes=[RP] + ALIAS)
                op(V, lambda: nc.vector.tensor_tensor(out=i3r[:, :, 0, :], in0=Bre[:], in1=qib, op=ALU.mult), reads=RR + [RP], writes=[RP] + ALIAS)
                op(V, lambda: nc.vector.tensor_tensor(out=BBi[:], in0=BBi[:], in1=i3r[:, :, 0, :], op=ALU.add), reads=[RP], writes=[RP] + ALIAS)
                for a in range(2):
                    op(V, lambda a=a: nc.vector.tensor_scalar(out=i3r[:, :, a, :], in0=BBr[:], scalar1=pm[:, a:a + 1], scalar2=0.0, op0=ALU.mult, op1=ALU.add), reads=[RP] + RC, writes=[RP] + ALIAS)
                    op(V, lambda a=a: nc.vector.tensor_scalar(out=i3i[:, :, a, :], in0=BBi[:], scalar1=pm[:, a:a + 1], scalar2=0.0, op0=ALU.mult, op1=ALU.add), reads=[RP] + RC, writes=[RP] + ALIAS)
                for (i3, BT) in ((in3r, BTr), (in3i, BTi)):
                    for ct in range(4):
                        grp(P, [lambda ct=ct, i3=i3: nc.tensor.transpose(
                            out=pY[:, ct * 128:(ct + 1) * 128], in_=i3[:, ct * 128:(ct + 1) * 128], identity=identf[:])],
                            reads=[RP] + RC + ALIAS, writes=[res("pY")])
                    for jj in range(2):
                        op(V, lambda BT=BT, jj=jj: nc.vector.tensor_scalar(
                            out=BT[:, :, jj, :], in0=pY[:].rearrange("p (a b) -> p a b", a=4),
                            scalar1=pm[:, 2 + jj:3 + jj], scalar2=0.0, op0=ALU.mult, op1=ALU.add),
                           reads=[res("pY")] + RC, writes=[RS])
                op(G, lambda: nc.gpsimd.memset(Cpad[:], 0.0), writes=[RS])
                Cp6 = Cpad[:].rearrange("p (a q) r (qq g h) -> p a q r qq g h", q=4, qq=4, g=2)
                for q in range(4):
                    for ri, Cs, sgn in ((0, Cre, 1.0), (1, Cim, -1.0), (2, Cre, -1.0)):
                        Csv = Cs[:].rearrange("p (a q) h -> p a q h", q=4)
                        for a in range(2):
                            op(V, lambda q=q, ri=ri, Csv=Csv, a=a, sgn=sgn: nc.vector.tensor_scalar(
                                out=Cp6[:, :, q, ri, q, a, :], in0=Csv[:, :, q, :], scalar1=pm[:, a:a + 1],
                                scalar2=sgn, op0=ALU.mult, op1=ALU.mult),
                               reads=RR + RC, writes=[RS])
                for (tab, off) in ((sinT, 0.0), (cosT, 0.5 * PI)):
                    for c in range(8):
                        W3 = [res("qa0"), res("qb0"), res("qt1")]
                        op(V, lambda c=c: nc.vector.tensor_tensor(
                            out=t1[:], in0=th[:, 2 * c:2 * c + 2].unsqueeze(2).broadcast_to([128, 2, TT]),
                            in1=tio[:].unsqueeze(1).broadcast_to([128, 2, TT]), op=ALU.mult),
                           reads=[RP, RS] + RC, writes=W3)
                        op(V, lambda off=off: nc.vector.tensor_scalar(out=t1[:], in0=t1[:], scalar1=off, scalar2=1.0 / (2 * PI), op0=ALU.add, op1=ALU.mult), reads=[], writes=W3)
                        op(V, lambda: nc.vector.tensor_copy(out=kit[:], in_=t1[:]), writes=W3)
                        op(V, lambda: nc.vector.tensor_copy(out=t3[:], in_=kit[:]), writes=W3)
                        op(V, lambda: nc.vector.tensor_tensor(out=t1[:], in0=t1[:], in1=t3[:], op=ALU.subtract), writes=W3)
                        op(V, lambda: nc.vector.tensor_scalar(out=t3[:], in0=t1[:], scalar1=0.5, scalar2=1.0, op0=ALU.is_gt, op1=ALU.mult), writes=W3)
                        op(V, lambda: nc.vector.tensor_tensor(out=t1[:], in0=t1[:], in1=t3[:], op=ALU.subtract), writes=W3)
                        op(V, lambda: nc.vector.tensor_scalar(out=t3[:], in0=t1[:], scalar1=-0.5, scalar2=1.0, op0=ALU.is_lt, op1=ALU.mult), writes=W3)
                        op(V, lambda: nc.vector.tensor_tensor(out=t1[:], in0=t1[:], in1=t3[:], op=ALU.add), writes=W3)
                        op(A, lambda tab=tab, c=c: nc.scalar.activation(out=tab[:, 2 * c:2 * c + 2, :], in_=t1[:], func=AF.Sin, scale=2 * PI),
                           reads=W3, writes=[RS])
                op(G, lambda: nc.gpsimd.memset(Tin_re[:], 0.0), writes=[res("Tin")])
                op(G, lambda: nc.gpsimd.memset(Tin_im[:], 0.0), writes=[res("Tin")])
                op(G, lambda: nc.gpsimd.memset(xc[:], 0.0), writes=[res("xc")])

                stop_at(3)
                def emit_xa():
                    for j in range(2):
                        def ev_xa(bank, rb, j=j):
                            op(A, lambda: nc.scalar.copy(
                                out=xaf[:, 2 * j:2 * j + 2, :], in_=bank[:].rearrange("p (a b) -> p a b", a=2)),
                               reads=[rb], writes=[res("xaf")])
                            op(V, lambda: nc.vector.tensor_copy(
                                out=xab[:, 2 * j:2 * j + 2, :], in_=xaf[:, 2 * j:2 * j + 2, :]),
                               reads=[res("xaf")], writes=[res("xab")])
                        zcols(l, [2 * j, 2 * j + 1], ev_xa)

                def s5_gen():
                    for gh in range(16):
                        pr = gh % 2
                        ct = gh // 4
                        q = gh % 4
                        hb = 64 * (q // 2)
                        jz = q % 2
                        bank = pS[pr]
                        rbank = res("pS%d" % pr)
                        grp(P, [lambda bank=bank, hb=hb, ct=ct, jz=jz: nc.tensor.matmul(
                                    bank[:, 0:TT], BTr[hb:hb + 64, ct, jz, :], xab[hb:hb + 64, ct, :], start=True, stop=True),
                                lambda bank=bank, hb=hb, ct=ct, jz=jz: nc.tensor.matmul(
                                    bank[:, TT:2 * TT], BTi[hb:hb + 64, ct, jz, :], xab[hb:hb + 64, ct, :], start=True, stop=True)],
                            reads=[RS, res("xab")], writes=[rbank])
                        u2 = bank[:].rearrange("p (a b) -> p a b", a=2)
                        cs = cosT[:, gh:gh + 1, :].broadcast_to([128, 2, TT])
                        sn = sinT[:, gh:gh + 1, :].broadcast_to([128, 2, TT])
                        ra, rb_, ru, rt = (res("qa%d" % pr), res("qb%d" % pr), res("qu%d" % pr), res("qt%d" % pr))
                        op(V, lambda u2=u2, cs=cs, pr=pr: nc.vector.tensor_tensor(out=qa[pr][:], in0=u2, in1=cs, op=ALU.mult), reads=[rbank, RS], writes=[ra])
                        op(V, lambda u2=u2, sn=sn, pr=pr: nc.vector.tensor_tensor(out=qb[pr][:], in0=u2, in1=sn, op=ALU.mult), reads=[rbank, RS], writes=[rb_])
                        op(V, lambda pr=pr: nc.vector.tensor_tensor(out=qu[pr][:, 0, :], in0=qa[pr][:, 0, :], in1=qb[pr][:, 1, :], op=ALU.add), reads=[ra, rb_], writes=[ru])
                        op(V, lambda pr=pr: nc.vector.tensor_tensor(out=qu[pr][:, 1, :], in0=qa[pr][:, 1, :], in1=qb[pr][:, 0, :], op=ALU.subtract), reads=[ra, rb_], writes=[ru])
                        op(V, lambda gh=gh, pr=pr: nc.vector.tensor_tensor_scan(
                            out=qt[pr][:, 0, :], data0=rr[:, gh:gh + 1].broadcast_to([128, TT]), data1=qu[pr][:, 0, :],
                            initial=Tin_re[:, gh:gh + 1], op0=ALU.mult, op1=ALU.add), reads=[ru, RS, res("Tin")], writes=[rt])
                        op(V, lambda gh=gh, pr=pr: nc.vector.tensor_tensor_scan(
                            out=qt[pr][:, 1, :], data0=rr[:, gh:gh + 1].broadcast_to([128, TT]), data1=qu[pr][:, 1, :],
                            initial=Tin_im[:, gh:gh + 1], op0=ALU.mult, op1=ALU.add), reads=[ru, RS, res("Tin")], writes=[rt])
                        op(V, lambda cs=cs, pr=pr, q=q: nc.vector.tensor_tensor(out=Sb[:, 0:2, q, :], in0=qt[pr][:], in1=cs, op=ALU.mult), reads=[rt, RS], writes=[res("Sb")])
                        op(V, lambda sn=sn, pr=pr, q=q: nc.vector.tensor_tensor(out=Sb[:, 2:4, q, :], in0=qt[pr][:], in1=sn, op=ALU.mult), reads=[rt, RS], writes=[res("Sb")])
                        op(V, lambda gh=gh, pr=pr: nc.vector.tensor_copy(out=Sl[:, :, gh], in_=qt[pr][:, :, TT - 1]), reads=[rt], writes=[res("Sl")])
                        if q == 3:
                            fl = []
                            n = 0
                            for qq in range(4):
                                for prod, var in ((0, 0), (1, 1), (2, 1), (3, 2)):
                                    fl.append(lambda qq=qq, prod=prod, var=var, ct=ct, n=n: nc.tensor.matmul(
                                        pY[:, 0:TT], Cpad[:, 4 * ct + qq, var, :], Sb[:, prod, qq, :],
                                        start=(n == 0), stop=(n == 15)))
                                    n += 1
                            grp(P, fl, reads=[RS, res("Sb")], writes=[res("pY")])
                            op(V, lambda ct=ct: nc.vector.scalar_tensor_tensor(
                                out=ypre[:], in0=xaf[:, ct, :], scalar=dsk[:, ct:ct + 1], in1=pY[:, 0:TT],
                                op0=ALU.mult, op1=ALU.add),
                               reads=[res("pY"), res("xaf"), res("lsmall")], writes=[res("ypre")])
                            op(A, lambda ct=ct: nc.scalar.activation(out=yg[:, ct, :], in_=ypre[:], func=gelu_f),
                               reads=[res("ypre")], writes=[res("yg")])
                            op(G, lambda ct=ct: nc.gpsimd.tensor_copy(out=ygb[:, ct, :], in_=yg[:, ct, :]),
                               reads=[res("yg")], writes=[res("ygb")])
                        if gh % 2 == 1:
                            yield
                    op(G, lambda: nc.gpsimd.tensor_tensor(out=cr[:, 0, :], in0=Sl[:, 0, :], in1=cthk[:], op=ALU.mult), reads=[res("Sl"), RS], writes=[res("cr")])
                    op(G, lambda: nc.gpsimd.tensor_tensor(out=cr[:, 1, :], in0=Sl[:, 1, :], in1=sthk[:], op=ALU.mult), reads=[res("Sl"), RS], writes=[res("cr")])
                    op(G, lambda: nc.gpsimd.tensor_tensor(out=cr[:, 2, :], in0=Sl[:, 1, :], in1=cthk[:], op=ALU.mult), reads=[res("Sl"), RS], writes=[res("cr")])
                    op(G, lambda: nc.gpsimd.tensor_tensor(out=cr[:, 3, :], in0=Sl[:, 0, :], in1=sthk[:], op=ALU.mult), reads=[res("Sl"), RS], writes=[res("cr")])
                    op(G, lambda: nc.gpsimd.tensor_tensor(out=Tin_re[:], in0=cr[:, 0, :], in1=cr[:, 1, :], op=ALU.subtract), reads=[res("cr")], writes=[res("Tin")])
                    op(G, lambda: nc.gpsimd.tensor_tensor(out=Tin_im[:], in0=cr[:, 2, :], in1=cr[:, 3, :], op=ALU.add), reads=[res("cr")], writes=[res("Tin")])
                    for c2 in range(4):
                        grp(P, [(lambda k=k, c2=c2: nc.tensor.matmul(
                            pY[:, 0:TT], wglu[:, k, c2 * 128:(c2 + 1) * 128], ygb[:, k, :],
                            start=(k == 0), stop=(k == 3))) for k in range(4)],
                            reads=[RL, res("ygb")], writes=[res("pY")])
                        op(A, lambda c2=c2: nc.scalar.activation(
                            out=tA[0][:], in_=pY[:, 0:TT], func=AF.Tanh, bias=bglh[:, c2:c2 + 1], scale=0.5),
                           reads=[res("pY"), res("lsmall")], writes=[res("tA0")])
                        op(V, lambda c2=c2: nc.vector.scalar_tensor_tensor(
                            out=tP[0][:], in0=tA[0][:], scalar=1.0, in1=yg[:, c2, :], op0=ALU.add, op1=ALU.mult),
                           reads=[res("tA0"), res("yg")], writes=[res("tP0")])
                        op(V, lambda c2=c2: nc.vector.scalar_tensor_tensor(
                            out=yT[:, c2, :], in0=tP[0][:], scalar=0.5, in1=sga[:, c2, :], op0=ALU.mult, op1=ALU.mult),
                           reads=[res("tP0"), res("sga")], writes=[res("yT")])
                        yield


                for it in range(nstep):
                    t0 = (it - 2) * TT
                    par = it % 2
                    gi = gtile[0]
                    gtile[0] += 1
                    hi = gi % 2
                    if gi == 0:
                        for _ in prefetch_gen(None, 0, 0):
                            pass
                    CUR["hT"] = hTs[hi]
                    CUR["hi"] = hi
                    hT = hTs[hi]
                    if it == 0:
                        emit_xa()
                        s5it = s5_gen()
                    next(s5it, None)
                    if it + 1 < nstep:
                        pf = prefetch_gen(None, it + 1, 1 - hi)
                    else:
                        pf = iter(())

                    for j in range(2):
                        def ev_xc(bank, rb, j=j):
                            op(A, lambda: nc.scalar.copy(
                                out=xc[:, 2 * j:2 * j + 2, 16:16 + TT], in_=bank[:].rearrange("p (a b) -> p a b", a=2)),
                               reads=[rb], writes=[res("xc")])
                        zcols(l, [20 + 2 * j, 21 + 2 * j], ev_xc)
                    next(s5it, None)

                    def ev_silu(dst, rname):
                        def f(bank, rb):
                            op(A, lambda: nc.scalar.activation(
                                out=dst, in_=bank[:].rearrange("p (a b) -> p a b", a=2), func=AF.Silu),
                               reads=[rb], writes=[res(rname)])
                        return f
                    for j in range(2):
                        zcols(l, [36 + 2 * j, 37 + 2 * j], ev_silu(sgc[:, 2 * j:2 * j + 2, :], "sgc"))
                    for g in range(4):
                        srcb = xc[:, g, :]
                        cur = None
                        bufs = [pa, pb]
                        rn = ["pa", "pb"]
                        sh = 1
                        for lev in range(g + 1):
                            o = bufs[lev % 2]
                            lo = 2 * sh - 1
                            i_ap = srcb if cur is None else cur[:]
                            rsrc = res("xc") if cur is None else res(rn[(lev - 1) % 2])
                            op(G, lambda o=o, i_ap=i_ap, lo=lo, sh=sh: nc.gpsimd.tensor_tensor(
                                out=o[:, lo:16 + TT], in0=i_ap[:, lo:16 + TT], in1=i_ap[:, lo - sh:16 + TT - sh], op=ALU.add),
                               reads=[rsrc], writes=[res(rn[lev % 2])])
                            cur = o
                            sh *= 2
                        rcur = res(rn[g % 2])
                        op(V, lambda cur=cur, g=g: nc.vector.scalar_tensor_tensor(
                            out=pp[:, g, :], in0=cur[:, 16:16 + TT], scalar=1.0 / (2 ** (g + 1)),
                            in1=xc[:, g, 16:16 + TT], op0=ALU.mult, op1=ALU.subtract),
                           reads=[rcur, res("xc")], writes=[res("pp")])
                        if it in (0, 2):
                            pdv_ = pdiv0 if it == 0 else pdiv2
                            op(V, lambda cur=cur, g=g, pdv_=pdv_: nc.vector.tensor_tensor(
                                out=cur[:, 16:32], in0=cur[:, 16:32], in1=pdv_[:, g, :], op=ALU.mult),
                               reads=[rcur] + RC, writes=[rcur])
                            op(V, lambda cur=cur, g=g: nc.vector.tensor_tensor(
                                out=pp[:, g, 0:16], in0=cur[:, 16:32], in1=xc[:, g, 16:32], op=ALU.subtract),
                               reads=[rcur, res("xc")], writes=[res("pp")])
                    op(G, lambda: nc.gpsimd.tensor_copy(out=xc[:, :, 0:16], in_=xc[:, :, TT:TT + 16]),
                       reads=[res("xc")], writes=[res("xc")])
                    for j in range(2):
                        b = pA_i[0] % 2
                        pA_i[0] += 1
                        for h in range(2):
                            g = 2 * j + h
                            grp(P, [lambda g=g, h=h, b=b: nc.tensor.matmul(
                                pA[b][:, h * TT:(h + 1) * TT], wpool[:, g, :], pp[:, g, :], start=True, stop=True)],
                                reads=[RL, res("pp")], writes=[res("pA%d" % b)])
                            op(A, lambda g=g, h=h, b=b: nc.scalar.activation(
                                out=tA[g % 2][:], in_=pA[b][:, h * TT:(h + 1) * TT], func=AF.Copy, scale=psc[:, g:g + 1]),
                               reads=[res("pA%d" % b), res("lsmall")], writes=[res("tA%d" % (g % 2))])
                            op(G, lambda g=g: nc.gpsimd.tensor_tensor(
                                out=yT[:, 12 + g, :], in0=tA[g % 2][:], in1=sgc[:, g, :], op=ALU.mult),
                               reads=[res("tA%d" % (g % 2)), res("sgc")], writes=[res("yT")])

                    next(s5it, None)
                    for j in range(2):
                        zcols(l, [24 + 2 * j, 25 + 2 * j], ev_silu(sga[:, 2 * j:2 * j + 2, :], "sga"))
                    next(s5it, None)

                    for c in range(2):
                        def ev_v(ts_, bank, rb, c=c):
                            op(A, lambda: nc.scalar.activation(
                                out=gv[ts_][:, c * 512:(c + 1) * 512], in_=bank[:], func=gelu_f),
                               reads=[rb], writes=[res("gv%d" % ts_)])
                            op(V, lambda: nc.vector.bn_stats(
                                out=bst[ts_][:, c, :], in_=gv[ts_][:, c * 512:(c + 1) * 512]),
                               reads=[res("gv%d" % ts_)], writes=[res("bst%d" % ts_)])
                        tokmaj([wsc_in[l, 12 + 4 * c + h] for h in range(4)], "wsc%d" % l,
                               lambda k, ts_: hT[:, k, ts_ * 128:(ts_ + 1) * 128], [res("hT%d" % hi)], ev_v,
                               after=lambda: next(s5it, None))
                    for ts_ in range(NS):
                        op(V, lambda ts_=ts_: nc.vector.bn_aggr(out=mv[:, ts_, :], in_=bst[ts_][:].rearrange("p a b -> p (a b)")),
                           reads=[res("bst%d" % ts_)], writes=[res("mv")])
                    op(V, lambda: nc.vector.tensor_scalar(
                        out=mv[:, :, 1], in0=mv[:, :, 1], scalar1=1e-5, scalar2=0.0, op0=ALU.add, op1=ALU.add),
                       reads=[res("mv")], writes=[res("mv")])
                    op(A, lambda: nc.scalar.activation(out=mv[:, :, 1], in_=mv[:, :, 1], func=AF.Sqrt),
                       reads=[res("mv")], writes=[res("mv")])
                    op(V, lambda: nc.vector.reciprocal(out=mv[:, :, 1], in_=mv[:, :, 1]),
                       reads=[res("mv")], writes=[res("mv")])
                    for ts_ in range(NS):
                        op(V, lambda ts_=ts_: nc.vector.tensor_scalar(
                            out=vn[ts_][:], in0=gv[ts_][:], scalar1=mv[:, ts_, 0:1], scalar2=mv[:, ts_, 1:2],
                            op0=ALU.subtract, op1=ALU.mult),
                           reads=[res("gv%d" % ts_), res("mv")], writes=[res("vn%d" % ts_)])

                    next(pf, None)
                    for hh in range(8):
                        e = hh % 2
                        def pre_s(hh=hh, e=e):
                            grp(P, [(lambda ts_=ts_, hh=hh: nc.tensor.matmul(
                                pY[:, ts_ * 128:(ts_ + 1) * 128], vn[ts_][:, hh * 128:(hh + 1) * 128], wsT[:, hh, :],
                                start=True, stop=True)) for ts_ in range(NS)],
                                reads=[res("vn0"), res("vn1"), RL], writes=[res("pY")])
                            op(A, lambda hh=hh, e=e: nc.scalar.activation(
                                out=tS[e][:], in_=pY[:, 0:256], func=AF.Copy, scale=lng[:, hh:hh + 1]),
                               reads=[res("pY"), res("lsmall")], writes=[res("tS%d" % e)])
                            op(G, lambda hh=hh, e=e: nc.gpsimd.tensor_tensor(
                                out=tS[e][:].rearrange("p (a b) -> p a b", a=2), in0=tS[e][:].rearrange("p (a b) -> p a b", a=2),
                                in1=Bt[:, hh:hh + 1, :].broadcast_to([128, 2, 128]), op=ALU.add),
                               reads=[res("tS%d" % e), res("lsmall")], writes=[res("tS%d" % e)])

                        def ev_sgu(bank, rb, hh=hh, e=e):
                            op(A, lambda: nc.scalar.activation(out=tA[e][:], in_=bank[:, 0:TT], func=gelu_f),
                               reads=[rb], writes=[res("tA%d" % e)])
                            op(A, lambda: nc.scalar.activation(out=tG[e][:], in_=bank[:, TT:2 * TT], func=AF.Tanh, scale=0.5),
                               reads=[rb], writes=[res("tG%d" % e)])
                            op(A, lambda: nc.scalar.copy(out=tW[:, e * TT:(e + 1) * TT], in_=bank[:, TT:2 * TT]),
                               reads=[rb], writes=[res("tW%d" % e)])
                            op(V, lambda: nc.vector.scalar_tensor_tensor(
                                out=tG[e][:], in0=tG[e][:], scalar=1.0, in1=tW[:, e * TT:(e + 1) * TT], op0=ALU.add, op1=ALU.mult),
                               reads=[res("tW%d" % e), res("tG%d" % e)], writes=[res("tG%d" % e)])
                            op(G, lambda: nc.gpsimd.tensor_tensor(out=tP[e][:], in0=tA[e][:], in1=tG[e][:], op=ALU.mult),
                               reads=[res("tA%d" % e), res("tG%d" % e)], writes=[res("tP%d" % e)])
                            op(V, lambda: nc.vector.scalar_tensor_tensor(
                                out=yT[:, 4 + hh, :], in0=tP[e][:], scalar=0.5, in1=tS[e][:], op0=ALU.mult, op1=ALU.mult),
                               reads=[res("tP%d" % e), res("tS%d" % e)], writes=[res("yT")])
                        def pre_all(pre_s=pre_s):
                            next(s5it, None)
                            next(pf, None)
                            pre_s()
                        zcols(l, [4 + hh, 28 + hh], ev_sgu, pre=pre_all)
                    for _ in s5it:
                        pass
                    for _ in pf:
                        pass

                    for ts_ in range(NS):
                        dma(G, sl_x[ts_], [lambda ts_=ts_: nc.gpsimd.dma_start(
                            out=xt[ts_][:], in_=usrc(it, ts_))],
                            reads=[res("U%d" % par)], writes=[res("xt%d" % ts_)])
                    if it + 1 < nstep:
                        CUR["hT"] = hTs[1 - hi]
                        CUR["hi"] = 1 - hi
                        emit_xa()
                        CUR["hT"] = hTs[hi]
                        CUR["hi"] = hi
                        s5it = s5_gen()
                    else:
                        s5it = iter(())
                    wo_cnt = [0]

                    def after_o():
                        wo_cnt[0] += 1
                        if wo_cnt[0] % 2 == 0:
                            next(s5it, None)
                    for c in range(4):
                        def ev_o(ts_, bank, rb, c=c):
                            op(V, lambda: nc.vector.tensor_tensor(
                                out=xt[ts_][:, c * 512:(c + 1) * 512], in0=xt[ts_][:, c * 512:(c + 1) * 512],
                                in1=bank[:], op=ALU.add),
                               reads=[rb], writes=[res("xt%d" % ts_)])
                        tokmaj([wsc_out[l, 4 * c + h] for h in range(4)], "wsc%d" % l,
                               lambda k, ts_: yT[:, k, ts_ * 128:(ts_ + 1) * 128], [res("yT")], ev_o, after=after_o)
                    if it < ntile:
                        for ts_ in range(NS):
                            dma(G, sl_sd[ts_], [lambda ts_=ts_: nc.gpsimd.dma_start(
                                out=sendb[par].ap()[ts_ * 128:(ts_ + 1) * 128, :], in_=xt[ts_][:])],
                                reads=[res("xt%d" % ts_)], writes=[res("send%d" % par)])
                        _deps(G, [res("send%d" % par)], [res("U%d" % par)])
                        ins_ = nc.gpsimd.collective_compute(
                            "AllGather", ALU.bypass, replica_groups=[[0, 1], [2, 3], [4, 5], [6, 7]],
                            ins=[sendb[par].ap().opt()], outs=[Ub[par].ap().rearrange("a t d -> (a t) d").opt()])
                        ins_.then_inc(sl_cc.sem)
                        sl_cc.n += 1
                        _upd((sl_cc.key, sl_cc.sem, sl_cc.n), [res("send%d" % par)], [res("U%d" % par)])
                        tn_ = min(it + 2, ntile - 1)
                        dma(G, sl_u, [lambda tn_=tn_: nc.gpsimd.dma_start(
                            out=Ub[par].ap()[1], in_=x_in[tn_ * TT:(tn_ + 1) * TT, :])],
                            writes=[res("U%d" % par)])
                    for ts_ in range(NS):
                        op(A, lambda ts_=ts_: nc.scalar.activation(
                            out=hp[:], in_=xt[ts_][:], func=AF.Square, accum_out=ss[:, 2 + ts_:3 + ts_]),
                           reads=[res("xt%d" % ts_)], writes=[res("hp"), res("ss2")])
                    op(V, lambda: nc.vector.tensor_scalar(
                        out=rstd[:, 2:4], in0=ss[:, 2:4], scalar1=1.0 / D, scalar2=1e-6,
                        op0=ALU.mult, op1=ALU.add), reads=[res("ss2")], writes=[res("rstd2")])
                    op(A, lambda: nc.scalar.activation(out=rstd[:, 2:4], in_=rstd[:, 2:4], func=AF.Sqrt),
                       reads=[res("rstd2")], writes=[res("rstd2")])
                    op(V, lambda: nc.vector.reciprocal(out=rstd[:, 2:4], in_=rstd[:, 2:4]),
                       reads=[res("rstd2")], writes=[res("rstd2")])
                    for ts_ in range(NS):
                        rx = res("xt%d" % ts_)
                        op(A, lambda ts_=ts_: nc.scalar.activation(
                            out=xt[ts_][:], in_=xt[ts_][:], func=AF.Copy, scale=rstd[:, 2 + ts_:3 + ts_]),
                           reads=[rx, res("rstd2")], writes=[rx])
                        op(V, lambda ts_=ts_: nc.vector.tensor_tensor(
                            out=xt[ts_][:], in0=xt[ts_][:], in1=fg_rep[:], op=ALU.mult),
                           reads=[rx] + RC, writes=[rx])
                        if it >= 2:
                            dma(G, sl_st[ts_], [lambda ts_=ts_: nc.gpsimd.dma_start(
                                out=out[t0 + ts_ * 128:t0 + (ts_ + 1) * 128, :], in_=xt[ts_][:])],
                                reads=[rx], writes=[res("outd")])
        except _Stop:
            pass
        for s in sl_st + sl_sd:
            if s.n:
                nc.gpsimd.wait_ge(s.sem, s.n)
    return nc


vslots = {}
_CACHE = {}


def _consts():
    w = [2, 4, 8, 16]
    pos = np.arange(1, 17, dtype=np.float32)
    pdiv0 = np.stack([1.0 / np.minimum(pos, float(wi)) for wi in w]).astype(np.float32).reshape(1, -1)
    pdivc = np.stack([np.full(256, 1.0 / wi, np.float32) for wi in w]).reshape(1, -1)
    pm = np.zeros((128, 4), np.float32)
    pm[:64, 0] = 1.0
    pm[64:, 1] = 1.0
    for p in range(128):
        pm[p, 2 + ((p // 32) % 2)] = 1.0
    return {
        "c_identb": np.eye(128, dtype=np.float32).astype(ml_dtypes.bfloat16),
        "c_identf": np.eye(128, dtype=np.float32),
        "c_tril": np.tril(np.ones((128, 128), np.float32)),
        "c_tio": np.arange(256, dtype=np.float32).reshape(1, 256),
        "c_pdiv0": pdiv0, "c_pdiv2": np.ascontiguousarray(pdivc.reshape(4, 256)[:, :16]).reshape(1, -1), "c_pm": pm,
    }


def kernel(**inputs):
    seq, batch = CFG["seq"], CFG["batch"]
    key = (seq, CFG["gelu"], CFG.get("stop"))
    if key not in _CACHE:
        _CACHE[key] = build(seq)
    nc = _CACHE[key]
    x = np.ascontiguousarray(np.asarray(inputs["x"], dtype=np.float32))
    cst = _consts()
    per_layer = {}
    for k, v in inputs.items():
        if k in ("x", "final_g"):
            continue
        v = np.asarray(v, dtype=np.float32)
        per_layer[k] = [np.ascontiguousarray(v[r:r + 1]) for r in range(2)]
    fg = np.ascontiguousarray(np.asarray(inputs["final_g"], dtype=np.float32).reshape(1, D))
    zeros_x = np.zeros((seq, D), np.float32)
    in_maps = []
    for c in range(2 * batch):
        b, role = c // 2, c % 2
        m = {k: v[role] for k, v in per_layer.items()}
        m["final_g"] = fg
        m.update(cst)
        m["x"] = x[b] if role == 0 else zeros_x
        m["c_sel"] = np.array([[1 - role]], np.int32)
        if role == 1:
            m["c_pdiv0"], m["c_pdiv2"] = cst["c_pdiv2"], cst["c_pdiv0"]
        in_maps.append(m)
    res = run_bass_kernel_spmd(nc, in_maps, core_ids=list(range(2 * batch)))
    return np.stack([np.asarray(res.results[2 * b + 1]["out"]) for b in range(batch)], axis=0).astype(np.float32)
```

```python
import numpy as np
import ml_dtypes
from contextlib import ExitStack
import concourse.bass as bass
import concourse.mybir as mybir
from concourse.bass_utils import run_bass_kernel_spmd

F32 = mybir.dt.float32
BF16 = mybir.dt.bfloat16
AF = mybir.ActivationFunctionType
ALU = mybir.AluOpType

D = 2048
NK = 16
INC = 5120
TT = 256
NS = 2
PI = float(np.pi)
CFG = {"seq": 8192, "depth": 2, "batch": 4, "gelu": "tanh"}
I32 = mybir.dt.int32


class _Stop(Exception):
    pass


def stop_at(n):
    if CFG.get("stop") == n:
        raise _Stop()


class Res:
    __slots__ = ("w", "r", "excl")

    def __init__(self, excl=False):
        self.w = None
        self.r = {}
        self.excl = excl


class Eng:
    def __init__(self, e, sem, key, is_pe=False):
        self.e = e
        self.sem = sem
        self.key = key
        self.is_pe = is_pe
        self.n = 0
        self.seen = {}

    def need(self, st):
        if st is None:
            return
        key, sem, val = st
        if key == self.key and self.is_pe:
            return
        if self.seen.get(key, 0) >= val:
            return
        self.e.wait_ge(sem, val)
        self.seen[key] = val


class Slot:
    def __init__(self, sem, key):
        self.sem = sem
        self.key = key
        self.n = 0


def _split(reads, writes):
    ex = [r for r in reads if r.excl]
    if ex:
        reads = [r for r in reads if not r.excl]
        writes = list(writes) + ex
    return reads, writes


def _deps(E, reads, writes):
    reads, writes = _split(reads, writes)
    for r in reads:
        E.need(r.w)
    for w in writes:
        E.need(w.w)
        for st in w.r.values():
            E.need(st)


def _upd(st, reads, writes):
    reads, writes = _split(reads, writes)
    for w in writes:
        w.w = st
        w.r = {}
    for r in reads:
        r.r[st[0]] = st


def op(E, fn, reads=(), writes=()):
    _deps(E, reads, writes)
    ins = fn()
    E.n += 1
    ins.then_inc(E.sem, 1)
    _upd((E.key, E.sem, E.n), reads, writes)


def grp(E, fns, reads=(), writes=()):
    _deps(E, reads, writes)
    ins = None
    for fn in fns:
        ins = fn()
    E.n += 1
    ins.then_inc(E.sem, 1)
    _upd((E.key, E.sem, E.n), reads, writes)


def dma(Q, slot, fns, reads=(), writes=()):
    _deps(Q, reads, writes)
    if slot.n:
        Q.need((slot.key, slot.sem, slot.n))
    for fn in fns:
        fn().then_inc(slot.sem, 16)
        slot.n += 16
    _upd((slot.key, slot.sem, slot.n), reads, writes)


def build(seq, depth_unused=1):
    depth = 1
    ntile = seq // TT
    nstep = ntile + 2
    nc = bass.Bass("TRN2", target_bir_lowering=False)

    def din(name, shape, dt=F32):
        return nc.dram_tensor(name, list(shape), dt, kind="ExternalInput").ap()

    x_in = din("x", [seq, D])
    norm_g = din("norm_g", [depth, D])
    w_in = din("w_in", [depth, D, INC])
    lam_re = din("lam_re", [depth, 32, 64])
    lam_im = din("lam_im", [depth, 32, 64])
    b_re = din("b_re", [depth, 32, 64, 16])
    b_im = din("b_im", [depth, 32, 64, 16])
    c_re = din("c_re", [depth, 32, 16, 64])
    c_im = din("c_im", [depth, 32, 16, 64])
    d_skip = din("d_skip", [depth, 32, 16])
    log_dt = din("log_dt", [depth, 32])
    w_glu = din("w_glu", [depth, 512, 512])
    b_glu = din("b_glu", [depth, 512])
    ln_g = din("ln_g", [depth, 1024])
    ln_b = din("ln_b", [depth, 1024])
    w_s = din("w_s", [depth, 8, 128, 128])
    b_s = din("b_s", [depth, 8, 128])
    w_pool = din("w_pool", [depth, 4, 128, 128])
    pool_scale = din("pool_scale", [depth, 512])
    w_out = din("w_out", [depth, D, D])
    final_g = din("final_g", [1, D])
    c_identb = din("c_identb", [128, 128], BF16)
    c_identf = din("c_identf", [128, 128])
    c_tril = din("c_tril", [128, 128])
    c_tio = din("c_tio", [1, 256])
    c_pdiv0 = din("c_pdiv0", [1, 4 * 16])
    c_pm = din("c_pm", [128, 4])
    c_pdiv2 = din("c_pdiv2", [1, 4 * 16])
    c_sel = din("c_sel", [1, 1], I32)
    out = nc.dram_tensor("out", [seq, D], F32, kind="ExternalOutput").ap()
    wsc_in = nc.dram_tensor("wsc_in", [depth, 40, 128, NK, 128], BF16, kind="Internal").ap()
    wsc_out = nc.dram_tensor("wsc_out", [depth, 16, 128, NK, 128], BF16, kind="Internal").ap()
    sendb = [nc.dram_tensor("sendb%d" % i, [TT, D], F32, kind="Internal") for i in range(2)]
    Ub = [nc.dram_tensor("Ub%d" % i, [2, TT, D], F32, kind="Internal") for i in range(2)]

    es = ExitStack()
    with es:
        def sb(name, shape, dt=F32):
            return es.enter_context(nc.sbuf_tensor(name, list(shape), dt))

        def pst(name, shape, dt=F32):
            return es.enter_context(nc.psum_tensor(name, list(shape), dt))

        def sem(name):
            return es.enter_context(nc.semaphore(name))

        P = Eng(nc.tensor, sem("s_pe"), "pe", True)
        A = Eng(nc.scalar, sem("s_act"), "act")
        V = Eng(nc.vector, sem("s_dve"), "dve")
        G = Eng(nc.gpsimd, sem("s_pool"), "pool")
        SY = Eng(nc.sync, sem("s_sy"), "sy")
        nslot = [0]

        def slot():
            nslot[0] += 1
            return Slot(sem("s_d%d" % nslot[0]), "d%d" % nslot[0])

        identb = sb("identb", [128, 128], BF16)
        identf = sb("identf", [128, 128])
        tril = sb("tril", [128, 128])
        tio = sb("tio", [128, 256])
        pdiv0 = sb("pdiv0", [128, 4, 16])
        pdiv2 = sb("pdiv2", [128, 4, 16])
        selt = sb("selt", [1, 1], I32)
        pm = sb("pm", [128, 4])
        fg_rep = sb("fg_rep", [128, D])
        xt = [sb("xt%d" % i, [128, D]) for i in range(NS)]
        hp = sb("hp", [128, D], BF16)
        hTs = [sb("hT%d" % i, [128, NK, TT], BF16) for i in range(2)]
        NW = 5
        W = [sb("w%d" % i, [128, NK, 128], BF16) for i in range(NW)]
        gv = [sb("gv%d" % i, [128, 1024]) for i in range(NS)]
        vn = [sb("vn%d" % i, [128, 1024], BF16) for i in range(NS)]
        tA = [sb("tA%d" % i, [128, TT]) for i in range(2)]
        tG = [sb("tG%d" % i, [128, TT]) for i in range(2)]
        tS = [sb("tS%d" % i, [128, TT]) for i in range(2)]
        tP = [sb("tP%d" % i, [128, TT]) for i in range(2)]
        tW = sb("tW", [128, 2 * TT])
        bglh = sb("bglh", [128, 4])
        yT = sb("yT", [128, NK, TT], BF16)
        xaf = sb("xaf", [128, 4, TT])
        xab = sb("xab", [128, 4, TT], BF16)
        sga = sb("sga", [128, 4, TT], BF16)
        sgc = sb("sgc", [128, 4, TT], BF16)
        xc = sb("xc", [128, 4, 16 + TT])
        pa = sb("pa", [128, 16 + TT])
        pb = sb("pb", [128, 16 + TT])
        pp = sb("pp", [128, 4, TT], BF16)
        cosT = sb("cosT", [128, 16, TT])
        sinT = sb("sinT", [128, 16, TT])
        qa = [sb("qa%d" % i, [128, 2, TT]) for i in range(2)]
        qb = [sb("qb%d" % i, [128, 2, TT]) for i in range(2)]
        qu = [sb("qu%d" % i, [128, 2, TT]) for i in range(2)]
        qt = [sb("qt%d" % i, [128, 2, TT]) for i in range(2)]
        t1, t2, t3, t4, Tre, Tim = qa[0], qa[1], qb[0], qb[1], qu[0], qu[1]
        Sl = sb("Sl", [128, 2, 16])
        kit = qt[1][:].bitcast(mybir.dt.int32)
        Sb = sb("Sb", [128, 4, 4, TT], BF16)
        Cpad = sb("Cpad", [128, 16, 3, 128], BF16)
        BTr = sb("BTr", [128, 4, 2, 128], BF16)
        BTi = sb("BTi", [128, 4, 2, 128], BF16)
        ypre = sb("ypre", [128, TT])
        yg = sb("yg", [128, 4, TT])
        ygb = sb("ygb", [128, 4, TT], BF16)
        wglu = sb("wglu", [128, 4, 512], BF16)
        wpool = sb("wpool", [128, 4, 128], BF16)
        wsT = sb("wsT", [128, 8, 128], BF16)
        Bt = sb("Bt", [128, 8, 128])
        lng = sb("lng", [128, 8])
        lnb = sb("lnb", [128, 8])
        psc = sb("psc", [128, 4])
        bgl = sb("bgl", [128, 4])
        dsk = sb("dsk", [128, 4])
        gk = sb("gk", [128, NK])
        Tin_re = sb("Tin_re", [128, 16])
        Tin_im = sb("Tin_im", [128, 16])
        rr = sb("rr", [128, 16])
        th = sb("th", [128, 16])
        sp = [sb("sp%d" % i, [128, 16]) for i in range(12)]
        Bre = t3[:].rearrange("p a t -> p (a t)")[:, 0:256].rearrange("p (g h) -> p g h", g=16)
        Bim = t4[:].rearrange("p a t -> p (a t)")[:, 0:256].rearrange("p (g h) -> p g h", g=16)
        BBr = Tre[:].rearrange("p a t -> p (a t)")[:, 0:256].rearrange("p (g h) -> p g h", g=16)
        BBi = Tim[:].rearrange("p a t -> p (a t)")[:, 0:256].rearrange("p (g h) -> p g h", g=16)
        in3r = t1[:].rearrange("p a t -> p (a t)")
        in3i = t2[:].rearrange("p a t -> p (a t)")
        Cre = gv[0][:, 0:256].rearrange("p (g h) -> p g h", g=16)
        Cim = gv[0][:, 256:512].rearrange("p (g h) -> p g h", g=16)
        ss = sb("ss", [128, 4])
        rstd = sb("rstd", [128, 4])
        bst = [sb("bst%d" % i, [128, 2, 6]) for i in range(NS)]
        mv = sb("mv", [128, NS, 2])
        cr = sb("cr", [128, 4, 16])
        cthk = sb("cthk", [128, 16])
        sthk = sb("sthk", [128, 16])
        ki16 = sb("ki16", [128, 16], mybir.dt.int32)
        cvi = [xt[i][:].rearrange("p (k c) -> p k c", k=NK) for i in range(2)]
        cvo = [yT[:, 8 * i:8 * i + 8, :].rearrange("p a (b c) -> p (a b) c", c=128) for i in range(2)]
        ones_b = sb("ones_b", [128, 128], BF16)

        pA = [pst("pA%d" % i, [128, 512]) for i in range(2)]
        pV = [pst("pV%d" % i, [128, 512]) for i in range(2)]
        pS = [pst("pS%d" % i, [128, 512]) for i in range(2)]
        pY = pst("pY", [128, 512])
        pT = pst("pT", [128, 1024], BF16)

        blk = es.enter_context(nc.Block())

        R = {}

        def res(name):
            if name not in R:
                R[name] = Res(excl=name in ("pA0", "pA1", "pV0", "pV1", "pS0", "pS1", "pY", "pT"))
            return R[name]

        sl_c = slot()
        sl_x = [slot() for _ in range(NS)]
        sl_w = [slot() for _ in range(NW)]
        sl_st = [slot() for _ in range(NS)]
        sl_sd = [slot() for _ in range(NS)]
        sl_cv = [slot() for _ in range(2)]
        sl_cvo = [slot() for _ in range(2)]
        sl_p = slot()

        gelu_f = AF.Gelu_apprx_tanh if CFG["gelu"] == "tanh" else AF.Gelu

        try:
            dma(SY, sl_c, [
                lambda: nc.sync.dma_start(out=identb[:], in_=c_identb[:, :]),
                lambda: nc.sync.dma_start(out=identf[:], in_=c_identf[:, :]),
                lambda: nc.sync.dma_start(out=tril[:], in_=c_tril[:, :]),
                lambda: nc.sync.dma_start(out=tio[:], in_=c_tio[0:1, :].broadcast_to([128, 256])),
                lambda: nc.sync.dma_start(out=pdiv0[:].rearrange("p g t -> p (g t)"),
                                          in_=c_pdiv0[0:1, :].broadcast_to([128, 64])),
                lambda: nc.sync.dma_start(out=pm[:], in_=c_pm[:, :]),
                lambda: nc.sync.dma_start(out=pdiv2[:].rearrange("p g t -> p (g t)"),
                                          in_=c_pdiv2[0:1, :].broadcast_to([128, 64])),
                lambda: nc.sync.dma_start(out=selt[:], in_=c_sel[:, :]),
                lambda: nc.sync.dma_start(out=fg_rep[:], in_=final_g[0:1, :].broadcast_to([128, D])),
            ], writes=[res("const")])
            RC = [res("const")]

            selr = es.enter_context(nc.gpsimd.register("selr"))
            G.need(res("const").w)
            nc.gpsimd.reg_load(selr, selt[:1, :1])
            SELIDX = nc.gpsimd.snap(selr, min_val=0, max_val=1)
            sl_cc = Slot(sem("s_cc"), "cc")
            sl_u = slot()
            op(G, lambda: nc.gpsimd.memset(xt[0][:], 0.0), writes=[res("xt0")])
            for par in range(2):
                dma(G, sl_u, [
                    (lambda par=par, r=r: nc.gpsimd.dma_start(out=Ub[par].ap()[0][r * 128:(r + 1) * 128, :], in_=xt[0][:]))
                    for r in range(2)] + [
                    lambda par=par: nc.gpsimd.dma_start(out=Ub[par].ap()[1], in_=x_in[par * TT:(par + 1) * TT, :])],
                    reads=[res("xt0")], writes=[res("U%d" % par)])

            def usrc(step, ts_):
                return Ub[step % 2].ap()[SELIDX][ts_ * 128:(ts_ + 1) * 128, :]

            cnt = 0
            for l in range(depth):
                dma(SY, sl_p, [lambda l=l: nc.sync.dma_start(
                    out=gk[:], in_=norm_g[l].rearrange("(k p) -> p k", p=128),
                    allow_slow_non_contiguous=True)], writes=[res("gk")])
                for m in range(40 + 16):
                    s = cnt % 2
                    cnt += 1
                    if m < 40:
                        src = w_in[l][:, m * 128:(m + 1) * 128].rearrange("(k p) c -> p k c", p=128)
                        dst = wsc_in[l, m]
                    else:
                        src = w_out[l][:, (m - 40) * 128:(m - 39) * 128].rearrange("(k p) c -> p k c", p=128)
                        dst = wsc_out[l, m - 40]
                    dma(SY, sl_cv[s], [lambda s=s, src=src: nc.sync.dma_start(out=cvi[s][:], in_=src)],
                        writes=[res("xt%d" % s)])
                    E = V if (cnt % 2 == 0) else G
                    if m < 40:
                        op(E, lambda E=E, s=s: E.e.tensor_tensor(
                            out=cvo[s][:], in0=cvi[s][:],
                            in1=gk[:].unsqueeze(2).broadcast_to([128, NK, 128]), op=ALU.mult),
                           reads=[res("xt%d" % s), res("gk")], writes=[res("cvo%d" % s)])
                    else:
                        op(E, lambda E=E, s=s: E.e.tensor_copy(out=cvo[s][:], in_=cvi[s][:]),
                           reads=[res("xt%d" % s)], writes=[res("cvo%d" % s)])
                    dma(SY, sl_cvo[s], [lambda s=s, dst=dst: nc.sync.dma_start(out=dst, in_=cvo[s][:])],
                        reads=[res("cvo%d" % s)], writes=[res("wsc%d" % l)])

            for nm_ in ("cvo0", "cvo1"):
                R_ = res(nm_)
                res("yT").r.update(R_.r)
                if R_.w is not None:
                    res("yT").r[R_.w[0]] = R_.w
            stop_at(1)
            w_i = [0]

            def load_w(src_ap, rname):
                s = w_i[0] % NW
                w_i[0] += 1
                dma(SY, sl_w[s], [lambda: nc.sync.dma_start(out=W[s][:], in_=src_ap)],
                    reads=[res(rname)], writes=[res("w%d" % s)])
                return s

            pA_i = [0]
            pV_i = [0]
            CUR = {"hT": hTs[0], "hi": 0}

            def zcols(l, ms, evac, pre=None):
                b = pA_i[0] % 2
                pA_i[0] += 1
                hT = CUR["hT"]
                rh = res("hT%d" % CUR["hi"])
                for h, m in enumerate(ms):
                    s = load_w(wsc_in[l, m], "wsc%d" % l)
                    grp(P, [(lambda k=k, s=s, h=h, b=b: nc.tensor.matmul(
                        pA[b][:, h * TT:(h + 1) * TT], W[s][:, k, :], hT[:, k, :],
                        start=(k == 0), stop=(k == NK - 1))) for k in range(NK)],
                        reads=[res("w%d" % s), rh], writes=[res("pA%d" % b)])
                if pre is not None:
                    pre()
                evac(pA[b], res("pA%d" % b))

            def tokmaj(srcs, rname, lhs_fn, rlhs, evac, after=None):
                slots = [load_w(a_, rname) for a_ in srcs]
                for ts_ in range(NS):
                    b = pV_i[0] % 2
                    pV_i[0] += 1
                    for h, s in enumerate(slots):
                        grp(P, [(lambda k=k, s=s, h=h, b=b, ts_=ts_: nc.tensor.matmul(
                            pV[b][:, h * 128:(h + 1) * 128], lhs_fn(k, ts_), W[s][:, k, :],
                            start=(k == 0), stop=(k == NK - 1))) for k in range(NK)],
                            reads=[res("w%d" % s)] + rlhs, writes=[res("pV%d" % b)])
                    if after is not None:
                        after()
                    evac(ts_, pV[b], res("pV%d" % b))

            def prefetch_gen(srcx, tn, hi):
                hTn = hTs[hi]
                for ts_ in range(NS):
                    dma(G, sl_x[ts_], [lambda ts_=ts_: nc.gpsimd.dma_start(
                        out=xt[ts_][:], in_=usrc(tn, ts_))],
                        reads=[res("U%d" % (tn % 2))], writes=[res("xt%d" % ts_)])
                yield
                for ts_ in range(NS):
                    op(A, lambda ts_=ts_: nc.scalar.activation(
                        out=hp[:], in_=xt[ts_][:], func=AF.Square, accum_out=ss[:, ts_:ts_ + 1]),
                       reads=[res("xt%d" % ts_)], writes=[res("hp"), res("ss")])
                op(V, lambda: nc.vector.tensor_scalar(
                    out=rstd[:, 0:2], in0=ss[:, 0:2], scalar1=1.0 / D, scalar2=1e-6,
                    op0=ALU.mult, op1=ALU.add), reads=[res("ss")], writes=[res("rstd")])
                op(A, lambda: nc.scalar.activation(out=rstd[:, 0:2], in_=rstd[:, 0:2], func=AF.Sqrt),
                   reads=[res("rstd")], writes=[res("rstd")])
                op(V, lambda: nc.vector.reciprocal(out=rstd[:, 0:2], in_=rstd[:, 0:2]),
                   reads=[res("rstd")], writes=[res("rstd")])
                yield
                for ts_ in range(NS):
                    op(A, lambda ts_=ts_: nc.scalar.activation(
                        out=hp[:], in_=xt[ts_][:], func=AF.Copy, scale=rstd[:, ts_:ts_ + 1]),
                       reads=[res("xt%d" % ts_), res("rstd")], writes=[res("hp")])
                    yield
                    for kb in range(2):
                        grp(P, [(lambda k=k, kb=kb: nc.tensor.transpose(
                            out=pT[:, (k - kb * 8) * 128:(k - kb * 8 + 1) * 128],
                            in_=hp[:, k * 128:(k + 1) * 128], identity=identb[:]))
                            for k in range(kb * 8, kb * 8 + 8)],
                            reads=[res("hp")] + RC, writes=[res("pT")])
                        if kb == 0:
                            op(V, lambda ts_=ts_, kb=kb: nc.vector.tensor_copy(
                                out=hTn[:, kb * 8:kb * 8 + 8, ts_ * 128:(ts_ + 1) * 128],
                                in_=pT[:].rearrange("p (a b) -> p a b", a=8)),
                               reads=[res("pT")], writes=[res("hT%d" % hi)])
                        else:
                            op(A, lambda ts_=ts_, kb=kb: nc.scalar.copy(
                                out=hTn[:, kb * 8:kb * 8 + 8, ts_ * 128:(ts_ + 1) * 128],
                                in_=pT[:].rearrange("p (a b) -> p a b", a=8)),
                               reads=[res("pT")], writes=[res("hT%d" % hi)])
                    yield

            gtile = [0]
            for l in range(depth):
                src = None
                last = (l == depth - 1)
                RL = res("layerw")
                cv0f = cvi[0][:].rearrange("p k c -> p (k c)")
                cv1f = cvi[1][:].rearrange("p k c -> p (k c)")
                dma(SY, sl_p, [lambda: nc.sync.dma_start(
                    out=cv0f[:, 0:512].rearrange("p (g d) -> p g d", g=4),
                    in_=w_pool[l].rearrange("g c d -> c g d"))], writes=[res("xt0")])
                op(V, lambda: nc.vector.tensor_copy(
                    out=wpool[:], in_=cv0f[:, 0:512].rearrange("p (g d) -> p g d", g=4)),
                   reads=[res("xt0")], writes=[RL])
                dma(SY, sl_p, [lambda: nc.sync.dma_start(
                    out=cv1f[:, 0:2048].rearrange("p (k n) -> p k n", k=4),
                    in_=w_glu[l].rearrange("(k p) n -> p k n", p=128))], writes=[res("xt1")])
                op(V, lambda: nc.vector.tensor_copy(
                    out=wglu[:], in_=cv1f[:, 0:2048].rearrange("p (k n) -> p k n", k=4)),
                   reads=[res("xt1")], writes=[RL])
                dma(SY, sl_p, [lambda: nc.sync.dma_start(
                    out=cvi[0][:, 0:8, :], in_=w_s[l].rearrange("h t s -> t h s"))], writes=[res("xt0")])
                op(V, lambda: nc.vector.tensor_tensor(
                    out=cvi[0][:, 0:8, :], in0=cvi[0][:, 0:8, :],
                    in1=tril[:].unsqueeze(1).broadcast_to([128, 8, 128]), op=ALU.mult),
                   reads=[res("xt0")] + RC, writes=[res("xt0")])
                for hh in range(8):
                    half = hh % 4
                    if half == 0:
                        pass
                    grp(P, [lambda hh=hh, half=half: nc.tensor.transpose(
                        out=pY[:, half * 128:(half + 1) * 128], in_=cvi[0][:, hh, :], identity=identf[:])],
                        reads=[res("xt0")] + RC, writes=[res("pY")])
                    if half == 3:
                        op(V, lambda hh=hh: nc.vector.tensor_copy(
                            out=wsT[:, hh - 3:hh + 1, :], in_=pY[:].rearrange("p (a b) -> p a b", a=4)),
                           reads=[res("pY")], writes=[RL])
                op(G, lambda: nc.gpsimd.memset(ones_b[:], 1.0), writes=[res("ones")])
                dma(SY, sl_p, [
                    lambda: nc.sync.dma_start(out=Bt[:].rearrange("p h t -> p (h t)"),
                                              in_=b_s[l:l + 1].rearrange("o h t -> o (h t)").broadcast_to([128, 1024])),
                    lambda: nc.sync.dma_start(out=lng[:], in_=ln_g[l].rearrange("(h d) -> d h", d=128),
                                              allow_slow_non_contiguous=True),
                    lambda: nc.sync.dma_start(out=lnb[:], in_=ln_b[l].rearrange("(h d) -> d h", d=128),
                                              allow_slow_non_contiguous=True),
                    lambda: nc.sync.dma_start(out=psc[:], in_=pool_scale[l].rearrange("(g d) -> d g", d=128),
                                              allow_slow_non_contiguous=True),
                    lambda: nc.sync.dma_start(out=bgl[:], in_=b_glu[l].rearrange("(g d) -> d g", d=128),
                                              allow_slow_non_contiguous=True),
                    lambda: nc.sync.dma_start(out=dsk[:], in_=d_skip[l].rearrange("(c g) h -> (g h) c", g=8),
                                              allow_slow_non_contiguous=True),
                ], writes=[res("lsmall")])
                op(V, lambda: nc.vector.tensor_scalar(out=bglh[:], in0=bgl[:], scalar1=0.5, scalar2=0.0, op0=ALU.mult, op1=ALU.add),
                   reads=[res("lsmall")], writes=[res("lsmall")])
                for hh in range(8):
                    grp(P, [lambda hh=hh: nc.tensor.matmul(
                        pY[:, 0:128], ones_b[:], wsT[:, hh, :], start=True, stop=True)],
                        reads=[res("ones"), RL], writes=[res("pY")])
                    op(V, lambda hh=hh: nc.vector.scalar_tensor_tensor(
                        out=Bt[:, hh, :], in0=pY[:, 0:128], scalar=lnb[:, hh:hh + 1], in1=Bt[:, hh, :],
                        op0=ALU.mult, op1=ALU.add),
                       reads=[res("pY"), res("lsmall")], writes=[res("lsmall")])

                stop_at(2)
                RS = res("s5w")
                ALIAS = [res("qa0"), res("qa1"), res("qb0"), res("qb1"), res("qu0"), res("qu1")]
                fns = []
                for gl in range(2):
                    ps_ = slice(gl * 64, (gl + 1) * 64)
                    fns += [
                        lambda gl=gl, ps_=ps_: nc.sync.dma_start(out=sp[0][ps_, :], in_=lam_re[l].rearrange("(gh gl) p -> gl p gh", gl=2)[gl], allow_slow_non_contiguous=True),
                        lambda gl=gl, ps_=ps_: nc.sync.dma_start(out=sp[1][ps_, :], in_=lam_im[l].rearrange("(gh gl) p -> gl p gh", gl=2)[gl], allow_slow_non_contiguous=True),
                        lambda gl=gl, ps_=ps_: nc.sync.dma_start(out=sp[2][ps_, :], in_=log_dt[l:l + 1].rearrange("o (gh gl) -> gl o gh", gl=2)[gl].broadcast_to([64, 16]), allow_slow_non_contiguous=True),
                        lambda gl=gl, ps_=ps_: nc.sync.dma_start(out=Bre[ps_], in_=b_re[l].rearrange("(gh gl) p h -> gl p gh h", gl=2)[gl], allow_slow_non_contiguous=True),
                        lambda gl=gl, ps_=ps_: nc.sync.dma_start(out=Bim[ps_], in_=b_im[l].rearrange("(gh gl) p h -> gl p gh h", gl=2)[gl], allow_slow_non_contiguous=True),
                    ]
                    for gh in range(16):
                        fns += [
                            lambda gl=gl, ps_=ps_, gh=gh: nc.sync.dma_start(out=Cre[ps_, gh, :], in_=c_re[l, 2 * gh + gl].rearrange("h p -> p h"), allow_slow_non_contiguous=True),
                            lambda gl=gl, ps_=ps_, gh=gh: nc.sync.dma_start(out=Cim[ps_, gh, :], in_=c_im[l, 2 * gh + gl].rearrange("h p -> p h"), allow_slow_non_contiguous=True),
                        ]
                dma(SY, sl_p, fns, writes=[res("s5raw"), res("gv0")] + ALIAS)
                RR = [res("s5raw")]
                lre, lim, ldt = sp[0], sp[1], sp[2]
                dt_, are, a1, sth, cth, lbr, lbi, den, qr, qi = sp[3], sp[4], sp[5], sp[6], sp[7], sp[8], sp[9], sp[10], sp[11], sp[2]
                RP = res("s5p")
                op(A, lambda: nc.scalar.activation(out=dt_[:], in_=ldt[:], func=AF.Exp), reads=RR, writes=[RP])
                op(V, lambda: nc.vector.tensor_tensor(out=are[:], in0=lre[:], in1=dt_[:], op=ALU.mult), reads=RR + [RP], writes=[RP])
                op(V, lambda: nc.vector.tensor_tensor(out=th[:], in0=lim[:], in1=dt_[:], op=ALU.mult), reads=RR + [RP], writes=[RP, RS])
                op(A, lambda: nc.scalar.activation(out=rr[:], in_=are[:], func=AF.Exp), reads=[RP], writes=[RP, RS])
                def sincos_small(dst, off, mul=1.0):
                    op(V, lambda: nc.vector.tensor_scalar(out=a1[:], in0=th[:], scalar1=mul / (2 * PI), scalar2=off / (2 * PI), op0=ALU.mult, op1=ALU.add), reads=[RP], writes=[RP])
                    op(V, lambda: nc.vector.tensor_copy(out=ki16[:], in_=a1[:]), reads=[RP], writes=[RP])
                    op(V, lambda: nc.vector.tensor_copy(out=den[:], in_=ki16[:]), reads=[RP], writes=[RP])
                    op(V, lambda: nc.vector.tensor_tensor(out=a1[:], in0=a1[:], in1=den[:], op=ALU.subtract), reads=[RP], writes=[RP])
                    op(V, lambda: nc.vector.tensor_scalar(out=den[:], in0=a1[:], scalar1=0.5, scalar2=1.0, op0=ALU.is_gt, op1=ALU.mult), reads=[RP], writes=[RP])
                    op(V, lambda: nc.vector.tensor_tensor(out=a1[:], in0=a1[:], in1=den[:], op=ALU.subtract), reads=[RP], writes=[RP])
                    op(V, lambda: nc.vector.tensor_scalar(out=den[:], in0=a1[:], scalar1=-0.5, scalar2=1.0, op0=ALU.is_lt, op1=ALU.mult), reads=[RP], writes=[RP])
                    op(V, lambda: nc.vector.tensor_tensor(out=a1[:], in0=a1[:], in1=den[:], op=ALU.add), reads=[RP], writes=[RP])
                    op(A, lambda: nc.scalar.activation(out=dst[:], in_=a1[:], func=AF.Sin, scale=2 * PI), reads=[RP], writes=[RP])
                sincos_small(sth, 0.0)
                sincos_small(cth, 0.5 * PI)
                MARK_CARRY_TABLES = True
                op(V, lambda: nc.vector.tensor_tensor(out=lbr[:], in0=rr[:], in1=cth[:], op=ALU.mult), reads=[RP], writes=[RP])
                op(V, lambda: nc.vector.tensor_tensor(out=lbi[:], in0=rr[:], in1=sth[:], op=ALU.mult), reads=[RP], writes=[RP])
                op(V, lambda: nc.vector.tensor_scalar(out=lbr[:], in0=lbr[:], scalar1=-1.0, scalar2=0.0, op0=ALU.add, op1=ALU.add), reads=[RP], writes=[RP])
                op(V, lambda: nc.vector.tensor_tensor(out=den[:], in0=lre[:], in1=lre[:], op=ALU.mult), reads=RR + [RP], writes=[RP])
                op(V, lambda: nc.vector.tensor_tensor(out=a1[:], in0=lim[:], in1=lim[:], op=ALU.mult), reads=RR + [RP], writes=[RP])
                op(V, lambda: nc.vector.tensor_tensor(out=den[:], in0=den[:], in1=a1[:], op=ALU.add), reads=[RP], writes=[RP])
                op(V, lambda: nc.vector.reciprocal(out=den[:], in_=den[:]), reads=[RP], writes=[RP])
                op(V, lambda: nc.vector.tensor_tensor(out=qr[:], in0=lbr[:], in1=lre[:], op=ALU.mult), reads=RR + [RP], writes=[RP])
                op(V, lambda: nc.vector.tensor_tensor(out=a1[:], in0=lbi[:], in1=lim[:], op=ALU.mult), reads=RR + [RP], writes=[RP])
                op(V, lambda: nc.vector.tensor_tensor(out=qr[:], in0=qr[:], in1=a1[:], op=ALU.add), reads=[RP], writes=[RP])
                op(V, lambda: nc.vector.tensor_tensor(out=qr[:], in0=qr[:], in1=den[:], op=ALU.mult), reads=[RP], writes=[RP])
                op(V, lambda: nc.vector.tensor_tensor(out=qi[:], in0=lbi[:], in1=lre[:], op=ALU.mult), reads=RR + [RP], writes=[RP])
                op(V, lambda: nc.vector.tensor_tensor(out=a1[:], in0=lbr[:], in1=lim[:], op=ALU.mult), reads=RR + [RP], writes=[RP])
                op(V, lambda: nc.vector.tensor_tensor(out=qi[:], in0=qi[:], in1=a1[:], op=ALU.subtract), reads=[RP], writes=[RP])
                op(V, lambda: nc.vector.tensor_tensor(out=qi[:], in0=qi[:], in1=den[:], op=ALU.mult), reads=[RP], writes=[RP])
                sincos_small(sth, 0.0, float(TT))
                op(V, lambda: nc.vector.tensor_copy(out=sthk[:], in_=sth[:]), reads=[RP], writes=[RS])
                sincos_small(cth, 0.5 * PI, float(TT))
                op(V, lambda: nc.vector.tensor_copy(out=cthk[:], in_=cth[:]), reads=[RP], writes=[RS])
                qrb = qr[:].unsqueeze(2).broadcast_to([128, 16, 16])
                qib = qi[:].unsqueeze(2).broadcast_to([128, 16, 16])
                i3r = in3r[:].rearrange("p (g a h) -> p g a h", g=16, a=2)
                i3i = in3i[:].rearrange("p (g a h) -> p g a h", g=16, a=2)
                op(V, lambda: nc.vector.tensor_tensor(out=BBr[:], in0=Bre[:], in1=qrb, op=ALU.mult), reads=RR + [RP], writes=[RP] + ALIAS)
                op(V, lambda: nc.vector.tensor_tensor(out=i3r[:, :, 0, :], in0=Bim[:], in1=qib, op=ALU.mult), reads=RR + [RP], writes=[RP] + ALIAS)
                op(V, lambda: nc.vector.tensor_tensor(out=BBr[:], in0=BBr[:], in1=i3r[:, :, 0, :], op=ALU.subtract), reads=[RP], writes=[RP] + ALIAS)
                op(V, lambda: nc.vector.tensor_tensor(out=BBi[:], in0=Bim[:], in1=qrb, op=ALU.mult), reads=RR + [RP], writes=[RP] + ALIAS)
                op(V, lambda: nc.vector.tensor_tensor(out=i3r[:, :, 0, :], in0=Bre[:], in1=qib, op=ALU.mult), reads=RR + [RP], writes=[RP] + ALIAS)
                op(V, lambda: nc.vector.tensor_tensor(out=BBi[:], in0=BBi[:], in1=i3r[:, :, 0, :], op=ALU.add), reads=[RP], writes=[RP] + ALIAS)
                for a in range(2):
                    op(V, lambda a=a: nc.vector.tensor_scalar(out=i3r[:, :, a, :], in0=BBr[:], scalar1=pm[:, a:a + 1], scalar2=0.0, op0=ALU.mult, op1=ALU.add), reads=[RP] + RC, writes=[RP] + ALIAS)
                    op(V, lambda a=a: nc.vector.tensor_scalar(out=i3i[:, :, a, :], in0=BBi[:], scalar1=pm[:, a:a + 1], scalar2=0.0, op0=ALU.mult, op1=ALU.add), reads=[RP] + RC, writes=[RP] + ALIAS)
                for (i3, BT) in ((in3r, BTr), (in3i, BTi)):
                    for ct in range(4):
                        grp(P, [lambda ct=ct, i3=i3: nc.tensor.transpose(
                            out=pY[:, ct * 128:(ct + 1) * 128], in_=i3[:, ct * 128:(ct + 1) * 128], identity=identf[:])],
                            reads=[RP] + RC + ALIAS, writes=[res("pY")])
                    for jj in range(2):
                        op(V, lambda BT=BT, jj=jj: nc.vector.tensor_scalar(
                            out=BT[:, :, jj, :], in0=pY[:].rearrange("p (a b) -> p a b", a=4),
                            scalar1=pm[:, 2 + jj:3 + jj], scalar2=0.0, op0=ALU.mult, op1=ALU.add),
                           reads=[res("pY")] + RC, writes=[RS])
                op(G, lambda: nc.gpsimd.memset(Cpad[:], 0.0), writes=[RS])
                Cp6 = Cpad[:].rearrange("p (a q) r (qq g h) -> p a q r qq g h", q=4, qq=4, g=2)
                for q in range(4):
                    for ri, Cs, sgn in ((0, Cre, 1.0), (1, Cim, -1.0), (2, Cre, -1.0)):
                        Csv = Cs[:].rearrange("p (a q) h -> p a q h", q=4)
                        for a in range(2):
                            op(V, lambda q=q, ri=ri, Csv=Csv, a=a, sgn=sgn: nc.vector.tensor_scalar(
                                out=Cp6[:, :, q, ri, q, a, :], in0=Csv[:, :, q, :], scalar1=pm[:, a:a + 1],
                                scalar2=sgn, op0=ALU.mult, op1=ALU.mult),
                               reads=RR + RC, writes=[RS])
                for (tab, off) in ((sinT, 0.0), (cosT, 0.5 * PI)):
                    for c in range(8):
                        W3 = [res("qa0"), res("qb0"), res("qt1")]
                        op(V, lambda c=c: nc.vector.tensor_tensor(
                            out=t1[:], in0=th[:, 2 * c:2 * c + 2].unsqueeze(2).broadcast_to([128, 2, TT]),
                            in1=tio[:].unsqueeze(1).broadcast_to([128, 2, TT]), op=ALU.mult),
                           reads=[RP, RS] + RC, writes=W3)
                        op(V, lambda off=off: nc.vector.tensor_scalar(out=t1[:], in0=t1[:], scalar1=off, scalar2=1.0 / (2 * PI), op0=ALU.add, op1=ALU.mult), reads=[], writes=W3)
                        op(V, lambda: nc.vector.tensor_copy(out=kit[:], in_=t1[:]), writes=W3)
                        op(V, lambda: nc.vector.tensor_copy(out=t3[:], in_=kit[:]), writes=W3)
                        op(V, lambda: nc.vector.tensor_tensor(out=t1[:], in0=t1[:], in1=t3[:], op=ALU.subtract), writes=W3)
                        op(V, lambda: nc.vector.tensor_scalar(out=t3[:], in0=t1[:], scalar1=0.5, scalar2=1.0, op0=ALU.is_gt, op1=ALU.mult), writes=W3)
                        op(V, lambda: nc.vector.tensor_tensor(out=t1[:], in0=t1[:], in1=t3[:], op=ALU.subtract), writes=W3)
                        op(V, lambda: nc.vector.tensor_scalar(out=t3[:], in0=t1[:], scalar1=-0.5, scalar2=1.0, op0=ALU.is_lt, op1=ALU.mult), writes=W3)
                        op(V, lambda: nc.vector.tensor_tensor(out=t1[:], in0=t1[:], in1=t3[:], op=ALU.add), writes=W3)
                        op(A, lambda tab=tab, c=c: nc.scalar.activation(out=tab[:, 2 * c:2 * c + 2, :], in_=t1[:], func=AF.Sin, scale=2 * PI),
                           reads=W3, writes=[RS])
                op(G, lambda: nc.gpsimd.memset(Tin_re[:], 0.0), writes=[res("Tin")])
                op(G, lambda: nc.gpsimd.memset(Tin_im[:], 0.0), writes=[res("Tin")])
                op(G, lambda: nc.gpsimd.memset(xc[:], 0.0), writes=[res("xc")])

                stop_at(3)
                def emit_xa():
                    for j in range(2):
                        def ev_xa(bank, rb, j=j):
                            op(A, lambda: nc.scalar.copy(
                                out=xaf[:, 2 * j:2 * j + 2, :], in_=bank[:].rearrange("p (a b) -> p a b", a=2)),
                               reads=[rb], writes=[res("xaf")])
                            op(V, lambda: nc.vector.tensor_copy(
                                out=xab[:, 2 * j:2 * j + 2, :], in_=xaf[:, 2 * j:2 * j + 2, :]),
                               reads=[res("xaf")], writes=[res("xab")])
                        zcols(l, [2 * j, 2 * j + 1], ev_xa)

                def s5_gen():
                    for gh in range(16):
                        pr = gh % 2
                        ct = gh // 4
                        q = gh % 4
                        hb = 64 * (q // 2)
                        jz = q % 2
                        bank = pS[pr]
                        rbank = res("pS%d" % pr)
                        grp(P, [lambda bank=bank, hb=hb, ct=ct, jz=jz: nc.tensor.matmul(
                                    bank[:, 0:TT], BTr[hb:hb + 64, ct, jz, :], xab[hb:hb + 64, ct, :], start=True, stop=True),
                                lambda bank=bank, hb=hb, ct=ct, jz=jz: nc.tensor.matmul(
                                    bank[:, TT:2 * TT], BTi[hb:hb + 64, ct, jz, :], xab[hb:hb + 64, ct, :], start=True, stop=True)],
                            reads=[RS, res("xab")], writes=[rbank])
                        u2 = bank[:].rearrange("p (a b) -> p a b", a=2)
                        cs = cosT[:, gh:gh + 1, :].broadcast_to([128, 2, TT])
                        sn = sinT[:, gh:gh + 1, :].broadcast_to([128, 2, TT])
                        ra, rb_, ru, rt = (res("qa%d" % pr), res("qb%d" % pr), res("qu%d" % pr), res("qt%d" % pr))
                        op(V, lambda u2=u2, cs=cs, pr=pr: nc.vector.tensor_tensor(out=qa[pr][:], in0=u2, in1=cs, op=ALU.mult), reads=[rbank, RS], writes=[ra])
                        op(V, lambda u2=u2, sn=sn, pr=pr: nc.vector.tensor_tensor(out=qb[pr][:], in0=u2, in1=sn, op=ALU.mult), reads=[rbank, RS], writes=[rb_])
                        op(V, lambda pr=pr: nc.vector.tensor_tensor(out=qu[pr][:, 0, :], in0=qa[pr][:, 0, :], in1=qb[pr][:, 1, :], op=ALU.add), reads=[ra, rb_], writes=[ru])
                        op(V, lambda pr=pr: nc.vector.tensor_tensor(out=qu[pr][:, 1, :], in0=qa[pr][:, 1, :], in1=qb[pr][:, 0, :], op=ALU.subtract), reads=[ra, rb_], writes=[ru])
                        op(V, lambda gh=gh, pr=pr: nc.vector.tensor_tensor_scan(
                            out=qt[pr][:, 0, :], data0=rr[:, gh:gh + 1].broadcast_to([128, TT]), data1=qu[pr][:, 0, :],
                            initial=Tin_re[:, gh:gh + 1], op0=ALU.mult, op1=ALU.add), reads=[ru, RS, res("Tin")], writes=[rt])
                        op(V, lambda gh=gh, pr=pr: nc.vector.tensor_tensor_scan(
                            out=qt[pr][:, 1, :], data0=rr[:, gh:gh + 1].broadcast_to([128, TT]), data1=qu[pr][:, 1, :],
                            initial=Tin_im[:, gh:gh + 1], op0=ALU.mult, op1=ALU.add), reads=[ru, RS, res("Tin")], writes=[rt])
                        op(V, lambda cs=cs, pr=pr, q=q: nc.vector.tensor_tensor(out=Sb[:, 0:2, q, :], in0=qt[pr][:], in1=cs, op=ALU.mult), reads=[rt, RS], writes=[res("Sb")])
                        op(V, lambda sn=sn, pr=pr, q=q: nc.vector.tensor_tensor(out=Sb[:, 2:4, q, :], in0=qt[pr][:], in1=sn, op=ALU.mult), reads=[rt, RS], writes=[res("Sb")])
                        op(V, lambda gh=gh, pr=pr: nc.vector.tensor_copy(out=Sl[:, :, gh], in_=qt[pr][:, :, TT - 1]), reads=[rt], writes=[res("Sl")])
                        if q == 3:
                            fl = []
                            n = 0
                            for qq in range(4):
                                for prod, var in ((0, 0), (1, 1), (2, 1), (3, 2)):
                                    fl.append(lambda qq=qq, prod=prod, var=var, ct=ct, n=n: nc.tensor.matmul(
                                        pY[:, 0:TT], Cpad[:, 4 * ct + qq, var, :], Sb[:, prod, qq, :],
                                        start=(n == 0), stop=(n == 15)))
                                    n += 1
                            grp(P, fl, reads=[RS, res("Sb")], writes=[res("pY")])
                            op(V, lambda ct=ct: nc.vector.scalar_tensor_tensor(
                                out=ypre[:], in0=xaf[:, ct, :], scalar=dsk[:, ct:ct + 1], in1=pY[:, 0:TT],
                                op0=ALU.mult, op1=ALU.add),
                               reads=[res("pY"), res("xaf"), res("lsmall")], writes=[res("ypre")])
                            op(A, lambda ct=ct: nc.scalar.activation(out=yg[:, ct, :], in_=ypre[:], func=gelu_f),
                               reads=[res("ypre")], writes=[res("yg")])
                            op(G, lambda ct=ct: nc.gpsimd.tensor_copy(out=ygb[:, ct, :], in_=yg[:, ct, :]),
                               reads=[res("yg")], writes=[res("ygb")])
                        if gh % 2 == 1:
                            yield
                    op(G, lambda: nc.gpsimd.tensor_tensor(out=cr[:, 0, :], in0=Sl[:, 0, :], in1=cthk[:], op=ALU.mult), reads=[res("Sl"), RS], writes=[res("cr")])
                    op(G, lambda: nc.gpsimd.tensor_tensor(out=cr[:, 1, :], in0=Sl[:, 1, :], in1=sthk[:], op=ALU.mult), reads=[res("Sl"), RS], writes=[res("cr")])
                    op(G, lambda: nc.gpsimd.tensor_tensor(out=cr[:, 2, :], in0=Sl[:, 1, :], in1=cthk[:], op=ALU.mult), reads=[res("Sl"), RS], writes=[res("cr")])
                    op(G, lambda: nc.gpsimd.tensor_tensor(out=cr[:, 3, :], in0=Sl[:, 0, :], in1=sthk[:], op=ALU.mult), reads=[res("Sl"), RS], writes=[res("cr")])
                    op(G, lambda: nc.gpsimd.tensor_tensor(out=Tin_re[:], in0=cr[:, 0, :], in1=cr[:, 1, :], op=ALU.subtract), reads=[res("cr")], writes=[res("Tin")])
                    op(G, lambda: nc.gpsimd.tensor_tensor(out=Tin_im[:], in0=cr[:, 2, :], in1=cr[:, 3, :], op=ALU.add), reads=[res("cr")], writes=[res("Tin")])
                    for c2 in range(4):
                        grp(P, [(lambda k=k, c2=c2: nc.tensor.matmul(
                            pY[:, 0:TT], wglu[:, k, c2 * 128:(c2 + 1) * 128], ygb[:, k, :],
                            start=(k == 0), stop=(k == 3))) for k in range(4)],
                            reads=[RL, res("ygb")], writes=[res("pY")])
                        op(A, lambda c2=c2: nc.scalar.activation(
                            out=tA[0][:], in_=pY[:, 0:TT], func=AF.Tanh, bias=bglh[:, c2:c2 + 1], scale=0.5),
                           reads=[res("pY"), res("lsmall")], writes=[res("tA0")])
                        op(V, lambda c2=c2: nc.vector.scalar_tensor_tensor(
                            out=tP[0][:], in0=tA[0][:], scalar=1.0, in1=yg[:, c2, :], op0=ALU.add, op1=ALU.mult),
                           reads=[res("tA0"), res("yg")], writes=[res("tP0")])
                        op(V, lambda c2=c2: nc.vector.scalar_tensor_tensor(
                            out=yT[:, c2, :], in0=tP[0][:], scalar=0.5, in1=sga[:, c2, :], op0=ALU.mult, op1=ALU.mult),
                           reads=[res("tP0"), res("sga")], writes=[res("yT")])
                        yield


                for it in range(nstep):
                    t0 = (it - 2) * TT
                    par = it % 2
                    gi = gtile[0]
                    gtile[0] += 1
                    hi = gi % 2
                    if gi == 0:
                        for _ in prefetch_gen(None, 0, 0):
                            pass
                    CUR["hT"] = hTs[hi]
                    CUR["hi"] = hi
                    hT = hTs[hi]
                    if it == 0:
                        emit_xa()
                        s5it = s5_gen()
                    next(s5it, None)
                    if it + 1 < nstep:
                        pf = prefetch_gen(None, it + 1, 1 - hi)
                    else:
                        pf = iter(())

                    for j in range(2):
                        def ev_xc(bank, rb, j=j):
                            op(A, lambda: nc.scalar.copy(
                                out=xc[:, 2 * j:2 * j + 2, 16:16 + TT], in_=bank[:].rearrange("p (a b) -> p a b", a=2)),
                               reads=[rb], writes=[res("xc")])
                        zcols(l, [20 + 2 * j, 21 + 2 * j], ev_xc)
                    next(s5it, None)

                    def ev_silu(dst, rname):
                        def f(bank, rb):
                            op(A, lambda: nc.scalar.activation(
                                out=dst, in_=bank[:].rearrange("p (a b) -> p a b", a=2), func=AF.Silu),
                               reads=[rb], writes=[res(rname)])
                        return f
                    for j in range(2):
                        zcols(l, [36 + 2 * j, 37 + 2 * j], ev_silu(sgc[:, 2 * j:2 * j + 2, :], "sgc"))
                    for g in range(4):
                        srcb = xc[:, g, :]
                        cur = None
                        bufs = [pa, pb]
                        rn = ["pa", "pb"]
                        sh = 1
                        for lev in range(g + 1):
                            o = bufs[lev % 2]
                            lo = 2 * sh - 1
                            i_ap = srcb if cur is None else cur[:]
                            rsrc = res("xc") if cur is None else res(rn[(lev - 1) % 2])
                            op(G, lambda o=o, i_ap=i_ap, lo=lo, sh=sh: nc.gpsimd.tensor_tensor(
                                out=o[:, lo:16 + TT], in0=i_ap[:, lo:16 + TT], in1=i_ap[:, lo - sh:16 + TT - sh], op=ALU.add),
                               reads=[rsrc], writes=[res(rn[lev % 2])])
                            cur = o
                            sh *= 2
                        rcur = res(rn[g % 2])
                        op(V, lambda cur=cur, g=g: nc.vector.scalar_tensor_tensor(
                            out=pp[:, g, :], in0=cur[:, 16:16 + TT], scalar=1.0 / (2 ** (g + 1)),
                            in1=xc[:, g, 16:16 + TT], op0=ALU.mult, op1=ALU.subtract),
                           reads=[rcur, res("xc")], writes=[res("pp")])
                        if it in (0, 2):
                            pdv_ = pdiv0 if it == 0 else pdiv2
                            op(V, lambda cur=cur, g=g, pdv_=pdv_: nc.vector.tensor_tensor(
                                out=cur[:, 16:32], in0=cur[:, 16:32], in1=pdv_[:, g, :], op=ALU.mult),
                               reads=[rcur] + RC, writes=[rcur])
                            op(V, lambda cur=cur, g=g: nc.vector.tensor_tensor(
                                out=pp[:, g, 0:16], in0=cur[:, 16:32], in1=xc[:, g, 16:32], op=ALU.subtract),
                               reads=[rcur, res("xc")], writes=[res("pp")])
                    op(G, lambda: nc.gpsimd.tensor_copy(out=xc[:, :, 0:16], in_=xc[:, :, TT:TT + 16]),
                       reads=[res("xc")], writes=[res("xc")])
                    def pool_mm():
                        for j in range(2):
                            b = pA_i[0] % 2
                            pA_i[0] += 1
                            for h in range(2):
                                g = 2 * j + h
                                grp(P, [lambda g=g, h=h, b=b: nc.tensor.matmul(
                                    pA[b][:, h * TT:(h + 1) * TT], wpool[:, g, :], pp[:, g, :], start=True, stop=True)],
                                    reads=[RL, res("pp")], writes=[res("pA%d" % b)])
                                op(A, lambda g=g, h=h, b=b: nc.scalar.activation(
                                    out=tA[g % 2][:], in_=pA[b][:, h * TT:(h + 1) * TT], func=AF.Copy, scale=psc[:, g:g + 1]),
                                   reads=[res("pA%d" % b), res("lsmall")], writes=[res("tA%d" % (g % 2))])
                                op(G, lambda g=g: nc.gpsimd.tensor_tensor(
                                    out=yT[:, 12 + g, :], in0=tA[g % 2][:], in1=sgc[:, g, :], op=ALU.mult),
                                   reads=[res("tA%d" % (g % 2)), res("sgc")], writes=[res("yT")])


                    next(s5it, None)
                    for j in range(2):
                        zcols(l, [24 + 2 * j, 25 + 2 * j], ev_silu(sga[:, 2 * j:2 * j + 2, :], "sga"))
                    next(s5it, None)

                    for c in range(2):
                        def ev_v(ts_, bank, rb, c=c):
                            op(A, lambda: nc.scalar.activation(
                                out=gv[ts_][:, c * 512:(c + 1) * 512], in_=bank[:], func=gelu_f),
                               reads=[rb], writes=[res("gv%d" % ts_)])
                            op(V, lambda: nc.vector.bn_stats(
                                out=bst[ts_][:, c, :], in_=gv[ts_][:, c * 512:(c + 1) * 512]),
                               reads=[res("gv%d" % ts_)], writes=[res("bst%d" % ts_)])
                        tokmaj([wsc_in[l, 12 + 4 * c + h] for h in range(4)], "wsc%d" % l,
                               lambda k, ts_: hT[:, k, ts_ * 128:(ts_ + 1) * 128], [res("hT%d" % hi)], ev_v,
                               after=lambda: next(s5it, None))
                        if c == 0:
                            pool_mm()
                    for ts_ in range(NS):
                        op(V, lambda ts_=ts_: nc.vector.bn_aggr(out=mv[:, ts_, :], in_=bst[ts_][:].rearrange("p a b -> p (a b)")),
                           reads=[res("bst%d" % ts_)], writes=[res("mv")])
                    op(V, lambda: nc.vector.tensor_scalar(
                        out=mv[:, :, 1], in0=mv[:, :, 1], scalar1=1e-5, scalar2=0.0, op0=ALU.add, op1=ALU.add),
                       reads=[res("mv")], writes=[res("mv")])
                    op(A, lambda: nc.scalar.activation(out=mv[:, :, 1], in_=mv[:, :, 1], func=AF.Sqrt),
                       reads=[res("mv")], writes=[res("mv")])
                    op(V, lambda: nc.vector.reciprocal(out=mv[:, :, 1], in_=mv[:, :, 1]),
                       reads=[res("mv")], writes=[res("mv")])
                    for ts_ in range(NS):
                        op(V, lambda ts_=ts_: nc.vector.tensor_scalar(
                            out=vn[ts_][:], in0=gv[ts_][:], scalar1=mv[:, ts_, 0:1], scalar2=mv[:, ts_, 1:2],
                            op0=ALU.subtract, op1=ALU.mult),
                           reads=[res("gv%d" % ts_), res("mv")], writes=[res("vn%d" % ts_)])

                    next(pf, None)
                    for hh in range(8):
                        e = hh % 2
                        def pre_s(hh=hh, e=e):
                            grp(P, [(lambda ts_=ts_, hh=hh: nc.tensor.matmul(
                                pY[:, ts_ * 128:(ts_ + 1) * 128], vn[ts_][:, hh * 128:(hh + 1) * 128], wsT[:, hh, :],
                                start=True, stop=True)) for ts_ in range(NS)],
                                reads=[res("vn0"), res("vn1"), RL], writes=[res("pY")])
                            op(A, lambda hh=hh, e=e: nc.scalar.activation(
                                out=tS[e][:], in_=pY[:, 0:256], func=AF.Copy, scale=lng[:, hh:hh + 1]),
                               reads=[res("pY"), res("lsmall")], writes=[res("tS%d" % e)])
                            op(G, lambda hh=hh, e=e: nc.gpsimd.tensor_tensor(
                                out=tS[e][:].rearrange("p (a b) -> p a b", a=2), in0=tS[e][:].rearrange("p (a b) -> p a b", a=2),
                                in1=Bt[:, hh:hh + 1, :].broadcast_to([128, 2, 128]), op=ALU.add),
                               reads=[res("tS%d" % e), res("lsmall")], writes=[res("tS%d" % e)])

                        def ev_sgu(bank, rb, hh=hh, e=e):
                            op(A, lambda: nc.scalar.activation(out=tA[e][:], in_=bank[:, 0:TT], func=gelu_f),
                               reads=[rb], writes=[res("tA%d" % e)])
                            op(A, lambda: nc.scalar.activation(out=tG[e][:], in_=bank[:, TT:2 * TT], func=AF.Tanh, scale=0.5),
                               reads=[rb], writes=[res("tG%d" % e)])
                            op(A, lambda: nc.scalar.copy(out=tW[:, e * TT:(e + 1) * TT], in_=bank[:, TT:2 * TT]),
                               reads=[rb], writes=[res("tW%d" % e)])
                            op(V, lambda: nc.vector.scalar_tensor_tensor(
                                out=tG[e][:], in0=tG[e][:], scalar=1.0, in1=tW[:, e * TT:(e + 1) * TT], op0=ALU.add, op1=ALU.mult),
                               reads=[res("tW%d" % e), res("tG%d" % e)], writes=[res("tG%d" % e)])
                            op(G, lambda: nc.gpsimd.tensor_tensor(out=tP[e][:], in0=tA[e][:], in1=tG[e][:], op=ALU.mult),
                               reads=[res("tA%d" % e), res("tG%d" % e)], writes=[res("tP%d" % e)])
                            op(V, lambda: nc.vector.scalar_tensor_tensor(
                                out=yT[:, 4 + hh, :], in0=tP[e][:], scalar=0.5, in1=tS[e][:], op0=ALU.mult, op1=ALU.mult),
                               reads=[res("tP%d" % e), res("tS%d" % e)], writes=[res("yT")])
                        def pre_all(pre_s=pre_s):
                            next(s5it, None)
                            next(pf, None)
                            pre_s()
                        zcols(l, [4 + hh, 28 + hh], ev_sgu, pre=pre_all)
                    for _ in s5it:
                        pass
                    for _ in pf:
                        pass

                    for ts_ in range(NS):
                        dma(G, sl_x[ts_], [lambda ts_=ts_: nc.gpsimd.dma_start(
                            out=xt[ts_][:], in_=usrc(it, ts_))],
                            reads=[res("U%d" % par)], writes=[res("xt%d" % ts_)])
                    if it + 1 < nstep:
                        CUR["hT"] = hTs[1 - hi]
                        CUR["hi"] = 1 - hi
                        emit_xa()
                        CUR["hT"] = hTs[hi]
                        CUR["hi"] = hi
                        s5it = s5_gen()
                    else:
                        s5it = iter(())
                    wo_cnt = [0]

                    def after_o():
                        wo_cnt[0] += 1
                        if wo_cnt[0] % 2 == 0:
                            next(s5it, None)
                    for c in range(4):
                        def ev_o(ts_, bank, rb, c=c):
                            op(V, lambda: nc.vector.tensor_tensor(
                                out=xt[ts_][:, c * 512:(c + 1) * 512], in0=xt[ts_][:, c * 512:(c + 1) * 512],
                                in1=bank[:], op=ALU.add),
                               reads=[rb], writes=[res("xt%d" % ts_)])
                        tokmaj([wsc_out[l, 4 * c + h] for h in range(4)], "wsc%d" % l,
                               lambda k, ts_: yT[:, k, ts_ * 128:(ts_ + 1) * 128], [res("yT")], ev_o, after=after_o)
                    if it < ntile:
                        for ts_ in range(NS):
                            dma(G, sl_sd[ts_], [lambda ts_=ts_: nc.gpsimd.dma_start(
                                out=sendb[par].ap()[ts_ * 128:(ts_ + 1) * 128, :], in_=xt[ts_][:])],
                                reads=[res("xt%d" % ts_)], writes=[res("send%d" % par)])
                        _deps(G, [res("send%d" % par)], [res("U%d" % par)])
                        ins_ = nc.gpsimd.collective_compute(
                            "AllGather", ALU.bypass, replica_groups=[[0, 1], [2, 3], [4, 5], [6, 7]],
                            ins=[sendb[par].ap().opt()], outs=[Ub[par].ap().rearrange("a t d -> (a t) d").opt()])
                        ins_.then_inc(sl_cc.sem)
                        sl_cc.n += 1
                        _upd((sl_cc.key, sl_cc.sem, sl_cc.n), [res("send%d" % par)], [res("U%d" % par)])
                        tn_ = min(it + 2, ntile - 1)
                        dma(G, sl_u, [lambda tn_=tn_: nc.gpsimd.dma_start(
                            out=Ub[par].ap()[1], in_=x_in[tn_ * TT:(tn_ + 1) * TT, :])],
                            writes=[res("U%d" % par)])
                    for ts_ in range(NS):
                        op(A, lambda ts_=ts_: nc.scalar.activation(
                            out=hp[:], in_=xt[ts_][:], func=AF.Square, accum_out=ss[:, 2 + ts_:3 + ts_]),
                           reads=[res("xt%d" % ts_)], writes=[res("hp"), res("ss2")])
                    op(V, lambda: nc.vector.tensor_scalar(
                        out=rstd[:, 2:4], in0=ss[:, 2:4], scalar1=1.0 / D, scalar2=1e-6,
                        op0=ALU.mult, op1=ALU.add), reads=[res("ss2")], writes=[res("rstd2")])
                    op(A, lambda: nc.scalar.activation(out=rstd[:, 2:4], in_=rstd[:, 2:4], func=AF.Sqrt),
                       reads=[res("rstd2")], writes=[res("rstd2")])
                    op(V, lambda: nc.vector.reciprocal(out=rstd[:, 2:4], in_=rstd[:, 2:4]),
                       reads=[res("rstd2")], writes=[res("rstd2")])
                    for ts_ in range(NS):
                        rx = res("xt%d" % ts_)
                        op(A, lambda ts_=ts_: nc.scalar.activation(
                            out=xt[ts_][:], in_=xt[ts_][:], func=AF.Copy, scale=rstd[:, 2 + ts_:3 + ts_]),
                           reads=[rx, res("rstd2")], writes=[rx])
                        op(V, lambda ts_=ts_: nc.vector.tensor_tensor(
                            out=xt[ts_][:], in0=xt[ts_][:], in1=fg_rep[:], op=ALU.mult),
                           reads=[rx] + RC, writes=[rx])
                        if it >= 2:
                            dma(G, sl_st[ts_], [lambda ts_=ts_: nc.gpsimd.dma_start(
                                out=out[t0 + ts_ * 128:t0 + (ts_ + 1) * 128, :], in_=xt[ts_][:])],
                                reads=[rx], writes=[res("outd")])
        except _Stop:
            pass
        for s in sl_st + sl_sd:
            if s.n:
                nc.gpsimd.wait_ge(s.sem, s.n)
    return nc


vslots = {}
_CACHE = {}


def _consts():
    w = [2, 4, 8, 16]
    pos = np.arange(1, 17, dtype=np.float32)
    pdiv0 = np.stack([1.0 / np.minimum(pos, float(wi)) for wi in w]).astype(np.float32).reshape(1, -1)
    pdivc = np.stack([np.full(256, 1.0 / wi, np.float32) for wi in w]).reshape(1, -1)
    pm = np.zeros((128, 4), np.float32)
    pm[:64, 0] = 1.0
    pm[64:, 1] = 1.0
    for p in range(128):
        pm[p, 2 + ((p // 32) % 2)] = 1.0
    return {
        "c_identb": np.eye(128, dtype=np.float32).astype(ml_dtypes.bfloat16),
        "c_identf": np.eye(128, dtype=np.float32),
        "c_tril": np.tril(np.ones((128, 128), np.float32)),
        "c_tio": np.arange(256, dtype=np.float32).reshape(1, 256),
        "c_pdiv0": pdiv0, "c_pdiv2": np.ascontiguousarray(pdivc.reshape(4, 256)[:, :16]).reshape(1, -1), "c_pm": pm,
    }


def kernel(**inputs):
    seq, batch = CFG["seq"], CFG["batch"]
    key = (seq, CFG["gelu"], CFG.get("stop"))
    if key not in _CACHE:
        _CACHE[key] = build(seq)
    nc = _CACHE[key]
    x = np.ascontiguousarray(np.asarray(inputs["x"], dtype=np.float32))
    cst = _consts()
    per_layer = {}
    for k, v in inputs.items():
        if k in ("x", "final_g"):
            continue
        v = np.asarray(v, dtype=np.float32)
        per_layer[k] = [np.ascontiguousarray(v[r:r + 1]) for r in range(2)]
    fg = np.ascontiguousarray(np.asarray(inputs["final_g"], dtype=np.float32).reshape(1, D))
    zeros_x = np.zeros((seq, D), np.float32)
    in_maps = []
    for c in range(2 * batch):
        b, role = c // 2, c % 2
        m = {k: v[role] for k, v in per_layer.items()}
        m["final_g"] = fg
        m.update(cst)
        m["x"] = x[b] if role == 0 else zeros_x
        m["c_sel"] = np.array([[1 - role]], np.int32)
        if role == 1:
            m["c_pdiv0"], m["c_pdiv2"] = cst["c_pdiv2"], cst["c_pdiv0"]
        in_maps.append(m)
    res = run_bass_kernel_spmd(nc, in_maps, core_ids=list(range(2 * batch)))
    return np.stack([np.asarray(res.results[2 * b + 1]["out"]) for b in range(batch)], axis=0).astype(np.float32)
```

```python
import numpy as np
import ml_dtypes
from contextlib import ExitStack
import concourse.bass as bass
import concourse.mybir as mybir
from concourse.bass_utils import run_bass_kernel_spmd

F32 = mybir.dt.float32
BF16 = mybir.dt.bfloat16
AF = mybir.ActivationFunctionType
ALU = mybir.AluOpType

D = 2048
NK = 16
INC = 5120
TT = 256
NS = 2
PI = float(np.pi)
CFG = {"seq": 8192, "depth": 2, "batch": 4, "gelu": "tanh"}
I32 = mybir.dt.int32


class _Stop(Exception):
    pass


def stop_at(n):
    if CFG.get("stop") == n:
        raise _Stop()


class Res:
    __slots__ = ("w", "r", "excl")

    def __init__(self, excl=False):
        self.w = None
        self.r = {}
        self.excl = excl


class Eng:
    def __init__(self, e, sem, key, is_pe=False):
        self.e = e
        self.sem = sem
        self.key = key
        self.is_pe = is_pe
        self.n = 0
        self.seen = {}

    def need(self, st):
        if st is None:
            return
        key, sem, val = st
        if key == self.key and self.is_pe:
            return
        if self.seen.get(key, 0) >= val:
            return
        self.e.wait_ge(sem, val)
        self.seen[key] = val


class Slot:
    def __init__(self, sem, key):
        self.sem = sem
        self.key = key
        self.n = 0


def _split(reads, writes):
    ex = [r for r in reads if r.excl]
    if ex:
        reads = [r for r in reads if not r.excl]
        writes = list(writes) + ex
    return reads, writes


def _deps(E, reads, writes):
    reads, writes = _split(reads, writes)
    for r in reads:
        E.need(r.w)
    for w in writes:
        E.need(w.w)
        for st in w.r.values():
            E.need(st)


def _upd(st, reads, writes):
    reads, writes = _split(reads, writes)
    for w in writes:
        w.w = st
        w.r = {}
    for r in reads:
        r.r[st[0]] = st


def op(E, fn, reads=(), writes=()):
    _deps(E, reads, writes)
    ins = fn()
    E.n += 1
    ins.then_inc(E.sem, 1)
    _upd((E.key, E.sem, E.n), reads, writes)


def grp(E, fns, reads=(), writes=()):
    _deps(E, reads, writes)
    ins = None
    for fn in fns:
        ins = fn()
    E.n += 1
    ins.then_inc(E.sem, 1)
    _upd((E.key, E.sem, E.n), reads, writes)


def dma(Q, slot, fns, reads=(), writes=()):
    _deps(Q, reads, writes)
    if slot.n:
        Q.need((slot.key, slot.sem, slot.n))
    for fn in fns:
        fn().then_inc(slot.sem, 16)
        slot.n += 16
    _upd((slot.key, slot.sem, slot.n), reads, writes)


def build(seq, depth_unused=1):
    depth = 1
    ntile = seq // TT
    nstep = ntile + 2
    nc = bass.Bass("TRN2", target_bir_lowering=False)

    def din(name, shape, dt=F32):
        return nc.dram_tensor(name, list(shape), dt, kind="ExternalInput").ap()

    x_in = din("x", [seq, D])
    norm_g = din("norm_g", [depth, D])
    w_in = din("w_in", [depth, D, INC])
    lam_re = din("lam_re", [depth, 32, 64])
    lam_im = din("lam_im", [depth, 32, 64])
    b_re = din("b_re", [depth, 32, 64, 16])
    b_im = din("b_im", [depth, 32, 64, 16])
    c_re = din("c_re", [depth, 32, 16, 64])
    c_im = din("c_im", [depth, 32, 16, 64])
    d_skip = din("d_skip", [depth, 32, 16])
    log_dt = din("log_dt", [depth, 32])
    w_glu = din("w_glu", [depth, 512, 512])
    b_glu = din("b_glu", [depth, 512])
    ln_g = din("ln_g", [depth, 1024])
    ln_b = din("ln_b", [depth, 1024])
    w_s = din("w_s", [depth, 8, 128, 128])
    b_s = din("b_s", [depth, 8, 128])
    w_pool = din("w_pool", [depth, 4, 128, 128])
    pool_scale = din("pool_scale", [depth, 512])
    w_out = din("w_out", [depth, D, D])
    final_g = din("final_g", [1, D])
    c_identb = din("c_identb", [128, 128], BF16)
    c_identf = din("c_identf", [128, 128])
    c_tril = din("c_tril", [128, 128])
    c_tio = din("c_tio", [1, 256])
    c_pdiv0 = din("c_pdiv0", [1, 4 * 16])
    c_pm = din("c_pm", [128, 4])
    c_pdiv2 = din("c_pdiv2", [1, 4 * 16])
    c_sel = din("c_sel", [1, 1], I32)
    out = nc.dram_tensor("out", [seq, D], F32, kind="ExternalOutput").ap()
    wsc_in = nc.dram_tensor("wsc_in", [depth, 40, 128, NK, 128], BF16, kind="Internal").ap()
    wsc_out = nc.dram_tensor("wsc_out", [depth, 16, 128, NK, 128], BF16, kind="Internal").ap()
    sendb = [nc.dram_tensor("sendb%d" % i, [TT, D], F32, kind="Internal") for i in range(2)]
    Ub = [nc.dram_tensor("Ub%d" % i, [2, TT, D], F32, kind="Internal") for i in range(2)]

    es = ExitStack()
    with es:
        def sb(name, shape, dt=F32):
            return es.enter_context(nc.sbuf_tensor(name, list(shape), dt))

        def pst(name, shape, dt=F32):
            return es.enter_context(nc.psum_tensor(name, list(shape), dt))

        def sem(name):
            return es.enter_context(nc.semaphore(name))

        P = Eng(nc.tensor, sem("s_pe"), "pe", True)
        A = Eng(nc.scalar, sem("s_act"), "act")
        V = Eng(nc.vector, sem("s_dve"), "dve")
        G = Eng(nc.gpsimd, sem("s_pool"), "pool")
        SY = Eng(nc.sync, sem("s_sy"), "sy")
        nslot = [0]

        def slot():
            nslot[0] += 1
            return Slot(sem("s_d%d" % nslot[0]), "d%d" % nslot[0])

        identb = sb("identb", [128, 128], BF16)
        identf = sb("identf", [128, 128])
        tril = sb("tril", [128, 128])
        tio = sb("tio", [128, 256])
        pdiv0 = sb("pdiv0", [128, 4, 16])
        pdiv2 = sb("pdiv2", [128, 4, 16])
        selt = sb("selt", [1, 1], I32)
        pm = sb("pm", [128, 4])
        fg_rep = sb("fg_rep", [128, D])
        xt = [sb("xt%d" % i, [128, D]) for i in range(NS)]
        hp = sb("hp", [128, D], BF16)
        hTs = [sb("hT%d" % i, [128, NK, TT], BF16) for i in range(2)]
        NW = 5
        W = [sb("w%d" % i, [128, NK, 128], BF16) for i in range(NW)]
        gv = [sb("gv%d" % i, [128, 1024]) for i in range(NS)]
        vn = [sb("vn%d" % i, [128, 1024], BF16) for i in range(NS)]
        tA = [sb("tA%d" % i, [128, TT]) for i in range(2)]
        tG = [sb("tG%d" % i, [128, TT]) for i in range(2)]
        tS = [sb("tS%d" % i, [128, TT]) for i in range(2)]
        tP = [sb("tP%d" % i, [128, TT]) for i in range(2)]
        tW = sb("tW", [128, 2 * TT])
        bglh = sb("bglh", [128, 4])
        yT = sb("yT", [128, NK, TT], BF16)
        xaf = sb("xaf", [128, 4, TT])
        xab = sb("xab", [128, 4, TT], BF16)
        sga = sb("sga", [128, 4, TT], BF16)
        sgc = sb("sgc", [128, 4, TT], BF16)
        xc = sb("xc", [128, 4, 16 + TT])
        pa = sb("pa", [128, 16 + TT])
        pb = sb("pb", [128, 16 + TT])
        pp = sb("pp", [128, 4, TT], BF16)
        cosT = sb("cosT", [128, 16, TT])
        sinT = sb("sinT", [128, 16, TT])
        qa = [sb("qa%d" % i, [128, 2, TT]) for i in range(2)]
        qb = [sb("qb%d" % i, [128, 2, TT]) for i in range(2)]
        qu = [sb("qu%d" % i, [128, 2, TT]) for i in range(2)]
        qt = [sb("qt%d" % i, [128, 2, TT]) for i in range(2)]
        t1, t2, t3, t4, Tre, Tim = qa[0], qa[1], qb[0], qb[1], qu[0], qu[1]
        Sl = sb("Sl", [128, 2, 16])
        kit = qt[1][:].bitcast(mybir.dt.int32)
        Sb = sb("Sb", [128, 4, 4, TT], BF16)
        Cpad = sb("Cpad", [128, 16, 3, 128], BF16)
        BTr = sb("BTr", [128, 4, 2, 128], BF16)
        BTi = sb("BTi", [128, 4, 2, 128], BF16)
        ypre = sb("ypre", [128, TT])
        yg = sb("yg", [128, 4, TT])
        ygb = sb("ygb", [128, 4, TT], BF16)
        wglu = sb("wglu", [128, 4, 512], BF16)
        wpool = sb("wpool", [128, 4, 128], BF16)
        wsT = sb("wsT", [128, 8, 128], BF16)
        Bt = sb("Bt", [128, 8, 128])
        lng = sb("lng", [128, 8])
        lnb = sb("lnb", [128, 8])
        psc = sb("psc", [128, 4])
        bgl = sb("bgl", [128, 4])
        dsk = sb("dsk", [128, 4])
        gk = sb("gk", [128, NK])
        Tin_re = sb("Tin_re", [128, 16])
        Tin_im = sb("Tin_im", [128, 16])
        rr = sb("rr", [128, 16])
        th = sb("th", [128, 16])
        sp = [sb("sp%d" % i, [128, 16]) for i in range(12)]
        Bre = t3[:].rearrange("p a t -> p (a t)")[:, 0:256].rearrange("p (g h) -> p g h", g=16)
        Bim = t4[:].rearrange("p a t -> p (a t)")[:, 0:256].rearrange("p (g h) -> p g h", g=16)
        BBr = Tre[:].rearrange("p a t -> p (a t)")[:, 0:256].rearrange("p (g h) -> p g h", g=16)
        BBi = Tim[:].rearrange("p a t -> p (a t)")[:, 0:256].rearrange("p (g h) -> p g h", g=16)
        in3r = t1[:].rearrange("p a t -> p (a t)")
        in3i = t2[:].rearrange("p a t -> p (a t)")
        Cre = gv[0][:, 0:256].rearrange("p (g h) -> p g h", g=16)
        Cim = gv[0][:, 256:512].rearrange("p (g h) -> p g h", g=16)
        ss = sb("ss", [128, 4])
        rstd = sb("rstd", [128, 4])
        bst = [sb("bst%d" % i, [128, 2, 6]) for i in range(NS)]
        mv = sb("mv", [128, NS, 2])
        cr = sb("cr", [128, 4, 16])
        cthk = sb("cthk", [128, 16])
        sthk = sb("sthk", [128, 16])
        ki16 = sb("ki16", [128, 16], mybir.dt.int32)
        cvi = [xt[i][:].rearrange("p (k c) -> p k c", k=NK) for i in range(2)]
        cvo = [yT[:, 8 * i:8 * i + 8, :].rearrange("p a (b c) -> p (a b) c", c=128) for i in range(2)]
        cvi4 = [xt[i][:].rearrange("p (kk c) -> p kk c", kk=4) for i in range(2)]
        cvo4 = [yT[:, 8 * i:8 * i + 8, :].rearrange("p (kk a) t -> p kk (a t)", a=2) for i in range(2)]
        ones_b = sb("ones_b", [128, 128], BF16)

        pA = [pst("pA%d" % i, [128, 512]) for i in range(2)]
        pV = [pst("pV%d" % i, [128, 512]) for i in range(2)]
        pS = [pst("pS%d" % i, [128, 512]) for i in range(2)]
        pY = pst("pY", [128, 512])
        pT = pst("pT", [128, 1024], BF16)

        blk = es.enter_context(nc.Block())

        R = {}

        def res(name):
            if name not in R:
                R[name] = Res(excl=name in ("pA0", "pA1", "pV0", "pV1", "pS0", "pS1", "pY", "pT"))
            return R[name]

        sl_c = slot()
        sl_x = [slot() for _ in range(NS)]
        sl_w = [slot() for _ in range(NW)]
        sl_st = [slot() for _ in range(NS)]
        sl_sd = [slot() for _ in range(NS)]
        sl_cv = [slot() for _ in range(2)]
        sl_cvo = [slot() for _ in range(2)]
        sl_p = slot()

        gelu_f = AF.Gelu_apprx_tanh if CFG["gelu"] == "tanh" else AF.Gelu

        try:
            dma(SY, sl_c, [
                lambda: nc.sync.dma_start(out=identb[:], in_=c_identb[:, :]),
                lambda: nc.sync.dma_start(out=identf[:], in_=c_identf[:, :]),
                lambda: nc.sync.dma_start(out=tril[:], in_=c_tril[:, :]),
                lambda: nc.sync.dma_start(out=tio[:], in_=c_tio[0:1, :].broadcast_to([128, 256])),
                lambda: nc.sync.dma_start(out=pdiv0[:].rearrange("p g t -> p (g t)"),
                                          in_=c_pdiv0[0:1, :].broadcast_to([128, 64])),
                lambda: nc.sync.dma_start(out=pm[:], in_=c_pm[:, :]),
                lambda: nc.sync.dma_start(out=pdiv2[:].rearrange("p g t -> p (g t)"),
                                          in_=c_pdiv2[0:1, :].broadcast_to([128, 64])),
                lambda: nc.sync.dma_start(out=selt[:], in_=c_sel[:, :]),
                lambda: nc.sync.dma_start(out=fg_rep[:], in_=final_g[0:1, :].broadcast_to([128, D])),
            ], writes=[res("const")])
            RC = [res("const")]

            selr = es.enter_context(nc.gpsimd.register("selr"))
            G.need(res("const").w)
            nc.gpsimd.reg_load(selr, selt[:1, :1])
            SELIDX = nc.gpsimd.snap(selr, min_val=0, max_val=1)
            sl_cc = Slot(sem("s_cc"), "cc")
            sl_u = slot()
            op(G, lambda: nc.gpsimd.memset(xt[0][:], 0.0), writes=[res("xt0")])
            for par in range(2):
                dma(G, sl_u, [
                    (lambda par=par, r=r: nc.gpsimd.dma_start(out=Ub[par].ap()[0][r * 128:(r + 1) * 128, :], in_=xt[0][:]))
                    for r in range(2)] + [
                    lambda par=par: nc.gpsimd.dma_start(out=Ub[par].ap()[1], in_=x_in[par * TT:(par + 1) * TT, :])],
                    reads=[res("xt0")], writes=[res("U%d" % par)])

            def usrc(step, ts_):
                return Ub[step % 2].ap()[SELIDX][ts_ * 128:(ts_ + 1) * 128, :]

            cnt = 0
            for l in range(depth):
                dma(SY, sl_p, [lambda l=l: nc.sync.dma_start(
                    out=gk[:], in_=norm_g[l].rearrange("(k p) -> p k", p=128),
                    allow_slow_non_contiguous=True)], writes=[res("gk")])
                for m in range(40 + 16):
                    s = cnt % 2
                    cnt += 1
                    wide = (12 <= m < 20) or m >= 40
                    kg = 0
                    if m < 40:
                        if wide:
                            c0 = (12 + 4 * ((m - 12) // 4)) * 128
                            kg = (m - 12) % 4
                            src = w_in[l][kg * 512:(kg + 1) * 512, c0:c0 + 512].rearrange("(kk p) c -> p kk c", p=128)
                        else:
                            src = w_in[l][:, m * 128:(m + 1) * 128].rearrange("(k p) c -> p k c", p=128)
                        dst = wsc_in[l, m]
                    else:
                        mo = m - 40
                        c0 = (mo // 4) * 512
                        kg = mo % 4
                        src = w_out[l][kg * 512:(kg + 1) * 512, c0:c0 + 512].rearrange("(kk p) c -> p kk c", p=128)
                        dst = wsc_out[l, mo]
                    ci = cvi4[s] if wide else cvi[s]
                    dma(SY, sl_cv[s], [lambda ci=ci, src=src: nc.sync.dma_start(out=ci[:], in_=src)],
                        writes=[res("xt%d" % s)])
                    E = V if (cnt % 2 == 0) else G
                    if m < 40 and wide:
                        op(E, lambda E=E, s=s, kg=kg: E.e.tensor_tensor(
                            out=cvo4[s][:], in0=cvi4[s][:],
                            in1=gk[:, 4 * kg:4 * kg + 4].unsqueeze(2).broadcast_to([128, 4, 512]), op=ALU.mult),
                           reads=[res("xt%d" % s), res("gk")], writes=[res("cvo%d" % s)])
                    elif m < 40:
                        op(E, lambda E=E, s=s: E.e.tensor_tensor(
                            out=cvo[s][:], in0=cvi[s][:],
                            in1=gk[:].unsqueeze(2).broadcast_to([128, NK, 128]), op=ALU.mult),
                           reads=[res("xt%d" % s), res("gk")], writes=[res("cvo%d" % s)])
                    else:
                        op(E, lambda E=E, s=s: E.e.tensor_copy(out=cvo[s][:], in_=cvi[s][:]),
                           reads=[res("xt%d" % s)], writes=[res("cvo%d" % s)])
                    dma(SY, sl_cvo[s], [lambda s=s, dst=dst: nc.sync.dma_start(out=dst, in_=cvo[s][:])],
                        reads=[res("cvo%d" % s)], writes=[res("wsc%d" % l)])

            for nm_ in ("cvo0", "cvo1"):
                R_ = res(nm_)
                res("yT").r.update(R_.r)
                if R_.w is not None:
                    res("yT").r[R_.w[0]] = R_.w
            stop_at(1)
            w_i = [0]

            def load_w(src_ap, rname):
                s = w_i[0] % NW
                w_i[0] += 1
                dma(SY, sl_w[s], [lambda: nc.sync.dma_start(out=W[s][:], in_=src_ap)],
                    reads=[res(rname)], writes=[res("w%d" % s)])
                return s

            pA_i = [0]
            pV_i = [0]
            CUR = {"hT": hTs[0], "hi": 0}

            def zcols(l, ms, evac, pre=None):
                b = pA_i[0] % 2
                pA_i[0] += 1
                hT = CUR["hT"]
                rh = res("hT%d" % CUR["hi"])
                for h, m in enumerate(ms):
                    s = load_w(wsc_in[l, m], "wsc%d" % l)
                    grp(P, [(lambda k=k, s=s, h=h, b=b: nc.tensor.matmul(
                        pA[b][:, h * TT:(h + 1) * TT], W[s][:, k, :], hT[:, k, :],
                        start=(k == 0), stop=(k == NK - 1))) for k in range(NK)],
                        reads=[res("w%d" % s), rh], writes=[res("pA%d" % b)])
                if pre is not None:
                    pre()
                evac(pA[b], res("pA%d" % b))

            def tokmaj(srcs, rname, lhs_fn, rlhs, evac, after=None):
                slots = [load_w(a_, rname) for a_ in srcs]
                for ts_ in range(NS):
                    b = pV_i[0] % 2
                    pV_i[0] += 1
                    for h, s in enumerate(slots):
                        grp(P, [(lambda kk=kk, s=s, h=h, b=b, ts_=ts_: nc.tensor.matmul(
                            pV[b][:, 0:512], lhs_fn(4 * h + kk, ts_),
                            W[s][:].rearrange("p (kk a) c -> p kk (a c)", a=4)[:, kk, :],
                            start=(4 * h + kk == 0), stop=(4 * h + kk == NK - 1))) for kk in range(4)],
                            reads=[res("w%d" % s)] + rlhs, writes=[res("pV%d" % b)])
                    if after is not None:
                        after()
                    evac(ts_, pV[b], res("pV%d" % b))

            def prefetch_gen(srcx, tn, hi):
                hTn = hTs[hi]
                for ts_ in range(NS):
                    dma(G, sl_x[ts_], [lambda ts_=ts_: nc.gpsimd.dma_start(
                        out=xt[ts_][:], in_=usrc(tn, ts_))],
                        reads=[res("U%d" % (tn % 2))], writes=[res("xt%d" % ts_)])
                yield
                for ts_ in range(NS):
                    op(A, lambda ts_=ts_: nc.scalar.activation(
                        out=hp[:], in_=xt[ts_][:], func=AF.Square, accum_out=ss[:, ts_:ts_ + 1]),
                       reads=[res("xt%d" % ts_)], writes=[res("hp"), res("ss")])
                op(V, lambda: nc.vector.tensor_scalar(
                    out=rstd[:, 0:2], in0=ss[:, 0:2], scalar1=1.0 / D, scalar2=1e-6,
                    op0=ALU.mult, op1=ALU.add), reads=[res("ss")], writes=[res("rstd")])
                op(A, lambda: nc.scalar.activation(out=rstd[:, 0:2], in_=rstd[:, 0:2], func=AF.Sqrt),
                   reads=[res("rstd")], writes=[res("rstd")])
                op(V, lambda: nc.vector.reciprocal(out=rstd[:, 0:2], in_=rstd[:, 0:2]),
                   reads=[res("rstd")], writes=[res("rstd")])
                yield
                for ts_ in range(NS):
                    op(A, lambda ts_=ts_: nc.scalar.activation(
                        out=hp[:], in_=xt[ts_][:], func=AF.Copy, scale=rstd[:, ts_:ts_ + 1]),
                       reads=[res("xt%d" % ts_), res("rstd")], writes=[res("hp")])
                    yield
                    for kb in range(2):
                        grp(P, [(lambda k=k, kb=kb: nc.tensor.transpose(
                            out=pT[:, (k - kb * 8) * 128:(k - kb * 8 + 1) * 128],
                            in_=hp[:, k * 128:(k + 1) * 128], identity=identb[:]))
                            for k in range(kb * 8, kb * 8 + 8)],
                            reads=[res("hp")] + RC, writes=[res("pT")])
                        if kb == 0:
                            op(V, lambda ts_=ts_, kb=kb: nc.vector.tensor_copy(
                                out=hTn[:, kb * 8:kb * 8 + 8, ts_ * 128:(ts_ + 1) * 128],
                                in_=pT[:].rearrange("p (a b) -> p a b", a=8)),
                               reads=[res("pT")], writes=[res("hT%d" % hi)])
                        else:
                            op(A, lambda ts_=ts_, kb=kb: nc.scalar.copy(
                                out=hTn[:, kb * 8:kb * 8 + 8, ts_ * 128:(ts_ + 1) * 128],
                                in_=pT[:].rearrange("p (a b) -> p a b", a=8)),
                               reads=[res("pT")], writes=[res("hT%d" % hi)])
                    yield

            gtile = [0]
            for l in range(depth):
                src = None
                last = (l == depth - 1)
                RL = res("layerw")
                cv0f = cvi[0][:].rearrange("p k c -> p (k c)")
                cv1f = cvi[1][:].rearrange("p k c -> p (k c)")
                dma(SY, sl_p, [lambda: nc.sync.dma_start(
                    out=cv0f[:, 0:512].rearrange("p (g d) -> p g d", g=4),
                    in_=w_pool[l].rearrange("g c d -> c g d"))], writes=[res("xt0")])
                op(V, lambda: nc.vector.tensor_copy(
                    out=wpool[:], in_=cv0f[:, 0:512].rearrange("p (g d) -> p g d", g=4)),
                   reads=[res("xt0")], writes=[RL])
                dma(SY, sl_p, [lambda: nc.sync.dma_start(
                    out=cv1f[:, 0:2048].rearrange("p (k n) -> p k n", k=4),
                    in_=w_glu[l].rearrange("(k p) n -> p k n", p=128))], writes=[res("xt1")])
                op(V, lambda: nc.vector.tensor_copy(
                    out=wglu[:], in_=cv1f[:, 0:2048].rearrange("p (k n) -> p k n", k=4)),
                   reads=[res("xt1")], writes=[RL])
                dma(SY, sl_p, [lambda: nc.sync.dma_start(
                    out=cvi[0][:, 0:8, :], in_=w_s[l].rearrange("h t s -> t h s"))], writes=[res("xt0")])
                op(V, lambda: nc.vector.tensor_tensor(
                    out=cvi[0][:, 0:8, :], in0=cvi[0][:, 0:8, :],
                    in1=tril[:].unsqueeze(1).broadcast_to([128, 8, 128]), op=ALU.mult),
                   reads=[res("xt0")] + RC, writes=[res("xt0")])
                for hh in range(8):
                    half = hh % 4
                    if half == 0:
                        pass
                    grp(P, [lambda hh=hh, half=half: nc.tensor.transpose(
                        out=pY[:, half * 128:(half + 1) * 128], in_=cvi[0][:, hh, :], identity=identf[:])],
                        reads=[res("xt0")] + RC, writes=[res("pY")])
                    if half == 3:
                        op(V, lambda hh=hh: nc.vector.tensor_copy(
                            out=wsT[:, hh - 3:hh + 1, :], in_=pY[:].rearrange("p (a b) -> p a b", a=4)),
                           reads=[res("pY")], writes=[RL])
                op(G, lambda: nc.gpsimd.memset(ones_b[:], 1.0), writes=[res("ones")])
                dma(SY, sl_p, [
                    lambda: nc.sync.dma_start(out=Bt[:].rearrange("p h t -> p (h t)"),
                                              in_=b_s[l:l + 1].rearrange("o h t -> o (h t)").broadcast_to([128, 1024])),
                    lambda: nc.sync.dma_start(out=lng[:], in_=ln_g[l].rearrange("(h d) -> d h", d=128),
                                              allow_slow_non_contiguous=True),
                    lambda: nc.sync.dma_start(out=lnb[:], in_=ln_b[l].rearrange("(h d) -> d h", d=128),
                                              allow_slow_non_contiguous=True),
                    lambda: nc.sync.dma_start(out=psc[:], in_=pool_scale[l].rearrange("(g d) -> d g", d=128),
                                              allow_slow_non_contiguous=True),
                    lambda: nc.sync.dma_start(out=bgl[:], in_=b_glu[l].rearrange("(g d) -> d g", d=128),
                                              allow_slow_non_contiguous=True),
                    lambda: nc.sync.dma_start(out=dsk[:], in_=d_skip[l].rearrange("(c g) h -> (g h) c", g=8),
                                              allow_slow_non_contiguous=True),
                ], writes=[res("lsmall")])
                op(V, lambda: nc.vector.tensor_scalar(out=bglh[:], in0=bgl[:], scalar1=0.5, scalar2=0.0, op0=ALU.mult, op1=ALU.add),
                   reads=[res("lsmall")], writes=[res("lsmall")])
                for hh in range(8):
                    grp(P, [lambda hh=hh: nc.tensor.matmul(
                        pY[:, 0:128], ones_b[:], wsT[:, hh, :], start=True, stop=True)],
                        reads=[res("ones"), RL], writes=[res("pY")])
                    op(V, lambda hh=hh: nc.vector.scalar_tensor_tensor(
                        out=Bt[:, hh, :], in0=pY[:, 0:128], scalar=lnb[:, hh:hh + 1], in1=Bt[:, hh, :],
                        op0=ALU.mult, op1=ALU.add),
                       reads=[res("pY"), res("lsmall")], writes=[res("lsmall")])

                stop_at(2)
                RS = res("s5w")
                ALIAS = [res("qa0"), res("qa1"), res("qb0"), res("qb1"), res("qu0"), res("qu1")]
                fns = []
                for gl in range(2):
                    ps_ = slice(gl * 64, (gl + 1) * 64)
                    fns += [
                        lambda gl=gl, ps_=ps_: nc.sync.dma_start(out=sp[0][ps_, :], in_=lam_re[l].rearrange("(gh gl) p -> gl p gh", gl=2)[gl], allow_slow_non_contiguous=True),
                        lambda gl=gl, ps_=ps_: nc.sync.dma_start(out=sp[1][ps_, :], in_=lam_im[l].rearrange("(gh gl) p -> gl p gh", gl=2)[gl], allow_slow_non_contiguous=True),
                        lambda gl=gl, ps_=ps_: nc.sync.dma_start(out=sp[2][ps_, :], in_=log_dt[l:l + 1].rearrange("o (gh gl) -> gl o gh", gl=2)[gl].broadcast_to([64, 16]), allow_slow_non_contiguous=True),
                        lambda gl=gl, ps_=ps_: nc.sync.dma_start(out=Bre[ps_], in_=b_re[l].rearrange("(gh gl) p h -> gl p gh h", gl=2)[gl], allow_slow_non_contiguous=True),
                        lambda gl=gl, ps_=ps_: nc.sync.dma_start(out=Bim[ps_], in_=b_im[l].rearrange("(gh gl) p h -> gl p gh h", gl=2)[gl], allow_slow_non_contiguous=True),
                    ]
                    for gh in range(16):
                        fns += [
                            lambda gl=gl, ps_=ps_, gh=gh: nc.sync.dma_start(out=Cre[ps_, gh, :], in_=c_re[l, 2 * gh + gl].rearrange("h p -> p h"), allow_slow_non_contiguous=True),
                            lambda gl=gl, ps_=ps_, gh=gh: nc.sync.dma_start(out=Cim[ps_, gh, :], in_=c_im[l, 2 * gh + gl].rearrange("h p -> p h"), allow_slow_non_contiguous=True),
                        ]
                dma(SY, sl_p, fns, writes=[res("s5raw"), res("gv0")] + ALIAS)
                RR = [res("s5raw")]
                lre, lim, ldt = sp[0], sp[1], sp[2]
                dt_, are, a1, sth, cth, lbr, lbi, den, qr, qi = sp[3], sp[4], sp[5], sp[6], sp[7], sp[8], sp[9], sp[10], sp[11], sp[2]
                RP = res("s5p")
                op(A, lambda: nc.scalar.activation(out=dt_[:], in_=ldt[:], func=AF.Exp), reads=RR, writes=[RP])
                op(V, lambda: nc.vector.tensor_tensor(out=are[:], in0=lre[:], in1=dt_[:], op=ALU.mult), reads=RR + [RP], writes=[RP])
                op(V, lambda: nc.vector.tensor_tensor(out=th[:], in0=lim[:], in1=dt_[:], op=ALU.mult), reads=RR + [RP], writes=[RP, RS])
                op(A, lambda: nc.scalar.activation(out=rr[:], in_=are[:], func=AF.Exp), reads=[RP], writes=[RP, RS])
                def sincos_small(dst, off, mul=1.0):
                    op(V, lambda: nc.vector.tensor_scalar(out=a1[:], in0=th[:], scalar1=mul / (2 * PI), scalar2=off / (2 * PI), op0=ALU.mult, op1=ALU.add), reads=[RP], writes=[RP])
                    op(V, lambda: nc.vector.tensor_copy(out=ki16[:], in_=a1[:]), reads=[RP], writes=[RP])
                    op(V, lambda: nc.vector.tensor_copy(out=den[:], in_=ki16[:]), reads=[RP], writes=[RP])
                    op(V, lambda: nc.vector.tensor_tensor(out=a1[:], in0=a1[:], in1=den[:], op=ALU.subtract), reads=[RP], writes=[RP])
                    op(V, lambda: nc.vector.tensor_scalar(out=den[:], in0=a1[:], scalar1=0.5, scalar2=1.0, op0=ALU.is_gt, op1=ALU.mult), reads=[RP], writes=[RP])
                    op(V, lambda: nc.vector.tensor_tensor(out=a1[:], in0=a1[:], in1=den[:], op=ALU.subtract), reads=[RP], writes=[RP])
                    op(V, lambda: nc.vector.tensor_scalar(out=den[:], in0=a1[:], scalar1=-0.5, scalar2=1.0, op0=ALU.is_lt, op1=ALU.mult), reads=[RP], writes=[RP])
                    op(V, lambda: nc.vector.tensor_tensor(out=a1[:], in0=a1[:], in1=den[:], op=ALU.add), reads=[RP], writes=[RP])
                    op(A, lambda: nc.scalar.activation(out=dst[:], in_=a1[:], func=AF.Sin, scale=2 * PI), reads=[RP], writes=[RP])
                sincos_small(sth, 0.0)
                sincos_small(cth, 0.5 * PI)
                MARK_CARRY_TABLES = True
                op(V, lambda: nc.vector.tensor_tensor(out=lbr[:], in0=rr[:], in1=cth[:], op=ALU.mult), reads=[RP], writes=[RP])
                op(V, lambda: nc.vector.tensor_tensor(out=lbi[:], in0=rr[:], in1=sth[:], op=ALU.mult), reads=[RP], writes=[RP])
                op(V, lambda: nc.vector.tensor_scalar(out=lbr[:], in0=lbr[:], scalar1=-1.0, scalar2=0.0, op0=ALU.add, op1=ALU.add), reads=[RP], writes=[RP])
                op(V, lambda: nc.vector.tensor_tensor(out=den[:], in0=lre[:], in1=lre[:], op=ALU.mult), reads=RR + [RP], writes=[RP])
                op(V, lambda: nc.vector.tensor_tensor(out=a1[:], in0=lim[:], in1=lim[:], op=ALU.mult), reads=RR + [RP], writes=[RP])
                op(V, lambda: nc.vector.tensor_tensor(out=den[:], in0=den[:], in1=a1[:], op=ALU.add), reads=[RP], writes=[RP])
                op(V, lambda: nc.vector.reciprocal(out=den[:], in_=den[:]), reads=[RP], writes=[RP])
                op(V, lambda: nc.vector.tensor_tensor(out=qr[:], in0=lbr[:], in1=lre[:], op=ALU.mult), reads=RR + [RP], writes=[RP])
                op(V, lambda: nc.vector.tensor_tensor(out=a1[:], in0=lbi[:], in1=lim[:], op=ALU.mult), reads=RR + [RP], writes=[RP])
                op(V, lambda: nc.vector.tensor_tensor(out=qr[:], in0=qr[:], in1=a1[:], op=ALU.add), reads=[RP], writes=[RP])
                op(V, lambda: nc.vector.tensor_tensor(out=qr[:], in0=qr[:], in1=den[:], op=ALU.mult), reads=[RP], writes=[RP])
                op(V, lambda: nc.vector.tensor_tensor(out=qi[:], in0=lbi[:], in1=lre[:], op=ALU.mult), reads=RR + [RP], writes=[RP])
                op(V, lambda: nc.vector.tensor_tensor(out=a1[:], in0=lbr[:], in1=lim[:], op=ALU.mult), reads=RR + [RP], writes=[RP])
                op(V, lambda: nc.vector.tensor_tensor(out=qi[:], in0=qi[:], in1=a1[:], op=ALU.subtract), reads=[RP], writes=[RP])
                op(V, lambda: nc.vector.tensor_tensor(out=qi[:], in0=qi[:], in1=den[:], op=ALU.mult), reads=[RP], writes=[RP])
                sincos_small(sth, 0.0, float(TT))
                op(V, lambda: nc.vector.tensor_copy(out=sthk[:], in_=sth[:]), reads=[RP], writes=[RS])
                sincos_small(cth, 0.5 * PI, float(TT))
                op(V, lambda: nc.vector.tensor_copy(out=cthk[:], in_=cth[:]), reads=[RP], writes=[RS])
                qrb = qr[:].unsqueeze(2).broadcast_to([128, 16, 16])
                qib = qi[:].unsqueeze(2).broadcast_to([128, 16, 16])
                i3r = in3r[:].rearrange("p (g a h) -> p g a h", g=16, a=2)
                i3i = in3i[:].rearrange("p (g a h) -> p g a h", g=16, a=2)
                op(V, lambda: nc.vector.tensor_tensor(out=BBr[:], in0=Bre[:], in1=qrb, op=ALU.mult), reads=RR + [RP], writes=[RP] + ALIAS)
                op(V, lambda: nc.vector.tensor_tensor(out=i3r[:, :, 0, :], in0=Bim[:], in1=qib, op=ALU.mult), reads=RR + [RP], writes=[RP] + ALIAS)
                op(V, lambda: nc.vector.tensor_tensor(out=BBr[:], in0=BBr[:], in1=i3r[:, :, 0, :], op=ALU.subtract), reads=[RP], writes=[RP] + ALIAS)
                op(V, lambda: nc.vector.tensor_tensor(out=BBi[:], in0=Bim[:], in1=qrb, op=ALU.mult), reads=RR + [RP], writes=[RP] + ALIAS)
                op(V, lambda: nc.vector.tensor_tensor(out=i3r[:, :, 0, :], in0=Bre[:], in1=qib, op=ALU.mult), reads=RR + [RP], writes=[RP] + ALIAS)
                op(V, lambda: nc.vector.tensor_tensor(out=BBi[:], in0=BBi[:], in1=i3r[:, :, 0, :], op=ALU.add), reads=[RP], writes=[RP] + ALIAS)
                for a in range(2):
                    op(V, lambda a=a: nc.vector.tensor_scalar(out=i3r[:, :, a, :], in0=BBr[:], scalar1=pm[:, a:a + 1], scalar2=0.0, op0=ALU.mult, op1=ALU.add), reads=[RP] + RC, writes=[RP] + ALIAS)
                    op(V, lambda a=a: nc.vector.tensor_scalar(out=i3i[:, :, a, :], in0=BBi[:], scalar1=pm[:, a:a + 1], scalar2=0.0, op0=ALU.mult, op1=ALU.add), reads=[RP] + RC, writes=[RP] + ALIAS)
                for (i3, BT) in ((in3r, BTr), (in3i, BTi)):
                    for ct in range(4):
                        grp(P, [lambda ct=ct, i3=i3: nc.tensor.transpose(
                            out=pY[:, ct * 128:(ct + 1) * 128], in_=i3[:, ct * 128:(ct + 1) * 128], identity=identf[:])],
                            reads=[RP] + RC + ALIAS, writes=[res("pY")])
                    for jj in range(2):
                        op(V, lambda BT=BT, jj=jj: nc.vector.tensor_scalar(
                            out=BT[:, :, jj, :], in0=pY[:].rearrange("p (a b) -> p a b", a=4),
                            scalar1=pm[:, 2 + jj:3 + jj], scalar2=0.0, op0=ALU.mult, op1=ALU.add),
                           reads=[res("pY")] + RC, writes=[RS])
                op(G, lambda: nc.gpsimd.memset(Cpad[:], 0.0), writes=[RS])
                Cp6 = Cpad[:].rearrange("p (a q) r (qq g h) -> p a q r qq g h", q=4, qq=4, g=2)
                for q in range(4):
                    for ri, Cs, sgn in ((0, Cre, 1.0), (1, Cim, -1.0), (2, Cre, -1.0)):
                        Csv = Cs[:].rearrange("p (a q) h -> p a q h", q=4)
                        for a in range(2):
                            op(V, lambda q=q, ri=ri, Csv=Csv, a=a, sgn=sgn: nc.vector.tensor_scalar(
                                out=Cp6[:, :, q, ri, q, a, :], in0=Csv[:, :, q, :], scalar1=pm[:, a:a + 1],
                                scalar2=sgn, op0=ALU.mult, op1=ALU.mult),
                               reads=RR + RC, writes=[RS])
                for (tab, off) in ((sinT, 0.0), (cosT, 0.5 * PI)):
                    for c in range(8):
                        W3 = [res("qa0"), res("qb0"), res("qt1")]
                        op(V, lambda c=c: nc.vector.tensor_tensor(
                            out=t1[:], in0=th[:, 2 * c:2 * c + 2].unsqueeze(2).broadcast_to([128, 2, TT]),
                            in1=tio[:].unsqueeze(1).broadcast_to([128, 2, TT]), op=ALU.mult),
                           reads=[RP, RS] + RC, writes=W3)
                        op(V, lambda off=off: nc.vector.tensor_scalar(out=t1[:], in0=t1[:], scalar1=off, scalar2=1.0 / (2 * PI), op0=ALU.add, op1=ALU.mult), reads=[], writes=W3)
                        op(V, lambda: nc.vector.tensor_copy(out=kit[:], in_=t1[:]), writes=W3)
                        op(V, lambda: nc.vector.tensor_copy(out=t3[:], in_=kit[:]), writes=W3)
                        op(V, lambda: nc.vector.tensor_tensor(out=t1[:], in0=t1[:], in1=t3[:], op=ALU.subtract), writes=W3)
                        op(V, lambda: nc.vector.tensor_scalar(out=t3[:], in0=t1[:], scalar1=0.5, scalar2=1.0, op0=ALU.is_gt, op1=ALU.mult), writes=W3)
                        op(V, lambda: nc.vector.tensor_tensor(out=t1[:], in0=t1[:], in1=t3[:], op=ALU.subtract), writes=W3)
                        op(V, lambda: nc.vector.tensor_scalar(out=t3[:], in0=t1[:], scalar1=-0.5, scalar2=1.0, op0=ALU.is_lt, op1=ALU.mult), writes=W3)
                        op(V, lambda: nc.vector.tensor_tensor(out=t1[:], in0=t1[:], in1=t3[:], op=ALU.add), writes=W3)
                        op(A, lambda tab=tab, c=c: nc.scalar.activation(out=tab[:, 2 * c:2 * c + 2, :], in_=t1[:], func=AF.Sin, scale=2 * PI),
                           reads=W3, writes=[RS])
                op(G, lambda: nc.gpsimd.memset(Tin_re[:], 0.0), writes=[res("Tin")])
                op(G, lambda: nc.gpsimd.memset(Tin_im[:], 0.0), writes=[res("Tin")])
                op(G, lambda: nc.gpsimd.memset(xc[:], 0.0), writes=[res("xc")])

                stop_at(3)
                def emit_xa():
                    for j in range(2):
                        def ev_xa(bank, rb, j=j):
                            op(A, lambda: nc.scalar.copy(
                                out=xaf[:, 2 * j:2 * j + 2, :], in_=bank[:].rearrange("p (a b) -> p a b", a=2)),
                               reads=[rb], writes=[res("xaf")])
                            op(V, lambda: nc.vector.tensor_copy(
                                out=xab[:, 2 * j:2 * j + 2, :], in_=xaf[:, 2 * j:2 * j + 2, :]),
                               reads=[res("xaf")], writes=[res("xab")])
                        zcols(l, [2 * j, 2 * j + 1], ev_xa)

                def s5_gen():
                    for gh in range(16):
                        pr = gh % 2
                        ct = gh // 4
                        q = gh % 4
                        hb = 64 * (q // 2)
                        jz = q % 2
                        bank = pS[pr]
                        rbank = res("pS%d" % pr)
                        grp(P, [lambda bank=bank, hb=hb, ct=ct, jz=jz: nc.tensor.matmul(
                                    bank[:, 0:TT], BTr[hb:hb + 64, ct, jz, :], xab[hb:hb + 64, ct, :], start=True, stop=True),
                                lambda bank=bank, hb=hb, ct=ct, jz=jz: nc.tensor.matmul(
                                    bank[:, TT:2 * TT], BTi[hb:hb + 64, ct, jz, :], xab[hb:hb + 64, ct, :], start=True, stop=True)],
                            reads=[RS, res("xab")], writes=[rbank])
                        u2 = bank[:].rearrange("p (a b) -> p a b", a=2)
                        cs = cosT[:, gh:gh + 1, :].broadcast_to([128, 2, TT])
                        sn = sinT[:, gh:gh + 1, :].broadcast_to([128, 2, TT])
                        ra, rb_, ru, rt = (res("qa%d" % pr), res("qb%d" % pr), res("qu%d" % pr), res("qt%d" % pr))
                        op(V, lambda u2=u2, cs=cs, pr=pr: nc.vector.tensor_tensor(out=qa[pr][:], in0=u2, in1=cs, op=ALU.mult), reads=[rbank, RS], writes=[ra])
                        op(V, lambda u2=u2, sn=sn, pr=pr: nc.vector.tensor_tensor(out=qb[pr][:], in0=u2, in1=sn, op=ALU.mult), reads=[rbank, RS], writes=[rb_])
                        op(V, lambda pr=pr: nc.vector.tensor_tensor(out=qu[pr][:, 0, :], in0=qa[pr][:, 0, :], in1=qb[pr][:, 1, :], op=ALU.add), reads=[ra, rb_], writes=[ru])
                        op(V, lambda pr=pr: nc.vector.tensor_tensor(out=qu[pr][:, 1, :], in0=qa[pr][:, 1, :], in1=qb[pr][:, 0, :], op=ALU.subtract), reads=[ra, rb_], writes=[ru])
                        op(V, lambda gh=gh, pr=pr: nc.vector.tensor_tensor_scan(
                            out=qt[pr][:, 0, :], data0=rr[:, gh:gh + 1].broadcast_to([128, TT]), data1=qu[pr][:, 0, :],
                            initial=Tin_re[:, gh:gh + 1], op0=ALU.mult, op1=ALU.add), reads=[ru, RS, res("Tin")], writes=[rt])
                        op(V, lambda gh=gh, pr=pr: nc.vector.tensor_tensor_scan(
                            out=qt[pr][:, 1, :], data0=rr[:, gh:gh + 1].broadcast_to([128, TT]), data1=qu[pr][:, 1, :],
                            initial=Tin_im[:, gh:gh + 1], op0=ALU.mult, op1=ALU.add), reads=[ru, RS, res("Tin")], writes=[rt])
                        op(V, lambda cs=cs, pr=pr, q=q: nc.vector.tensor_tensor(out=Sb[:, 0:2, q, :], in0=qt[pr][:], in1=cs, op=ALU.mult), reads=[rt, RS], writes=[res("Sb")])
                        op(V, lambda sn=sn, pr=pr, q=q: nc.vector.tensor_tensor(out=Sb[:, 2:4, q, :], in0=qt[pr][:], in1=sn, op=ALU.mult), reads=[rt, RS], writes=[res("Sb")])
                        op(V, lambda gh=gh, pr=pr: nc.vector.tensor_copy(out=Sl[:, :, gh], in_=qt[pr][:, :, TT - 1]), reads=[rt], writes=[res("Sl")])
                        if q == 3:
                            fl = []
                            n = 0
                            for qq in range(4):
                                for prod, var in ((0, 0), (1, 1), (2, 1), (3, 2)):
                                    fl.append(lambda qq=qq, prod=prod, var=var, ct=ct, n=n: nc.tensor.matmul(
                                        pY[:, 0:TT], Cpad[:, 4 * ct + qq, var, :], Sb[:, prod, qq, :],
                                        start=(n == 0), stop=(n == 15)))
                                    n += 1
                            grp(P, fl, reads=[RS, res("Sb")], writes=[res("pY")])
                            op(V, lambda ct=ct: nc.vector.scalar_tensor_tensor(
                                out=ypre[:], in0=xaf[:, ct, :], scalar=dsk[:, ct:ct + 1], in1=pY[:, 0:TT],
                                op0=ALU.mult, op1=ALU.add),
                               reads=[res("pY"), res("xaf"), res("lsmall")], writes=[res("ypre")])
                            op(A, lambda ct=ct: nc.scalar.activation(out=yg[:, ct, :], in_=ypre[:], func=gelu_f),
                               reads=[res("ypre")], writes=[res("yg")])
                            op(G, lambda ct=ct: nc.gpsimd.tensor_copy(out=ygb[:, ct, :], in_=yg[:, ct, :]),
                               reads=[res("yg")], writes=[res("ygb")])
                        yield
                    op(G, lambda: nc.gpsimd.tensor_tensor(out=cr[:, 0, :], in0=Sl[:, 0, :], in1=cthk[:], op=ALU.mult), reads=[res("Sl"), RS], writes=[res("cr")])
                    op(G, lambda: nc.gpsimd.tensor_tensor(out=cr[:, 1, :], in0=Sl[:, 1, :], in1=sthk[:], op=ALU.mult), reads=[res("Sl"), RS], writes=[res("cr")])
                    op(G, lambda: nc.gpsimd.tensor_tensor(out=cr[:, 2, :], in0=Sl[:, 1, :], in1=cthk[:], op=ALU.mult), reads=[res("Sl"), RS], writes=[res("cr")])
                    op(G, lambda: nc.gpsimd.tensor_tensor(out=cr[:, 3, :], in0=Sl[:, 0, :], in1=sthk[:], op=ALU.mult), reads=[res("Sl"), RS], writes=[res("cr")])
                    op(G, lambda: nc.gpsimd.tensor_tensor(out=Tin_re[:], in0=cr[:, 0, :], in1=cr[:, 1, :], op=ALU.subtract), reads=[res("cr")], writes=[res("Tin")])
                    op(G, lambda: nc.gpsimd.tensor_tensor(out=Tin_im[:], in0=cr[:, 2, :], in1=cr[:, 3, :], op=ALU.add), reads=[res("cr")], writes=[res("Tin")])
                    for c2 in range(4):
                        grp(P, [(lambda k=k, c2=c2: nc.tensor.matmul(
                            pY[:, 0:TT], wglu[:, k, c2 * 128:(c2 + 1) * 128], ygb[:, k, :],
                            start=(k == 0), stop=(k == 3))) for k in range(4)],
                            reads=[RL, res("ygb")], writes=[res("pY")])
                        op(A, lambda c2=c2: nc.scalar.activation(
                            out=tA[0][:], in_=pY[:, 0:TT], func=AF.Tanh, bias=bglh[:, c2:c2 + 1], scale=0.5),
                           reads=[res("pY"), res("lsmall")], writes=[res("tA0")])
                        op(V, lambda c2=c2: nc.vector.scalar_tensor_tensor(
                            out=tP[0][:], in0=tA[0][:], scalar=1.0, in1=yg[:, c2, :], op0=ALU.add, op1=ALU.mult),
                           reads=[res("tA0"), res("yg")], writes=[res("tP0")])
                        op(V, lambda c2=c2: nc.vector.scalar_tensor_tensor(
                            out=yT[:, c2, :], in0=tP[0][:], scalar=0.5, in1=sga[:, c2, :], op0=ALU.mult, op1=ALU.mult),
                           reads=[res("tP0"), res("sga")], writes=[res("yT")])
                        yield


                for it in range(nstep):
                    t0 = (it - 2) * TT
                    par = it % 2
                    gi = gtile[0]
                    gtile[0] += 1
                    hi = gi % 2
                    if gi == 0:
                        for _ in prefetch_gen(None, 0, 0):
                            pass
                    CUR["hT"] = hTs[hi]
                    CUR["hi"] = hi
                    hT = hTs[hi]
                    if it == 0:
                        emit_xa()
                        s5it = s5_gen()
                    next(s5it, None)
                    if it + 1 < nstep:
                        pf = prefetch_gen(None, it + 1, 1 - hi)
                    else:
                        pf = iter(())

                    for j in range(2):
                        def ev_xc(bank, rb, j=j):
                            op(A, lambda: nc.scalar.copy(
                                out=xc[:, 2 * j:2 * j + 2, 16:16 + TT], in_=bank[:].rearrange("p (a b) -> p a b", a=2)),
                               reads=[rb], writes=[res("xc")])
                        zcols(l, [20 + 2 * j, 21 + 2 * j], ev_xc)
                    next(s5it, None)

                    def ev_silu(dst, rname):
                        def f(bank, rb):
                            op(A, lambda: nc.scalar.activation(
                                out=dst, in_=bank[:].rearrange("p (a b) -> p a b", a=2), func=AF.Silu),
                               reads=[rb], writes=[res(rname)])
                        return f
                    for j in range(2):
                        zcols(l, [36 + 2 * j, 37 + 2 * j], ev_silu(sgc[:, 2 * j:2 * j + 2, :], "sgc"))
                    for g in range(4):
                        srcb = xc[:, g, :]
                        cur = None
                        bufs = [pa, pb]
                        rn = ["pa", "pb"]
                        sh = 1
                        for lev in range(g + 1):
                            o = bufs[lev % 2]
                            lo = 2 * sh - 1
                            i_ap = srcb if cur is None else cur[:]
                            rsrc = res("xc") if cur is None else res(rn[(lev - 1) % 2])
                            op(G, lambda o=o, i_ap=i_ap, lo=lo, sh=sh: nc.gpsimd.tensor_tensor(
                                out=o[:, lo:16 + TT], in0=i_ap[:, lo:16 + TT], in1=i_ap[:, lo - sh:16 + TT - sh], op=ALU.add),
                               reads=[rsrc], writes=[res(rn[lev % 2])])
                            cur = o
                            sh *= 2
                        rcur = res(rn[g % 2])
                        op(V, lambda cur=cur, g=g: nc.vector.scalar_tensor_tensor(
                            out=pp[:, g, :], in0=cur[:, 16:16 + TT], scalar=1.0 / (2 ** (g + 1)),
                            in1=xc[:, g, 16:16 + TT], op0=ALU.mult, op1=ALU.subtract),
                           reads=[rcur, res("xc")], writes=[res("pp")])
                        if it in (0, 2):
                            pdv_ = pdiv0 if it == 0 else pdiv2
                            op(V, lambda cur=cur, g=g, pdv_=pdv_: nc.vector.tensor_tensor(
                                out=cur[:, 16:32], in0=cur[:, 16:32], in1=pdv_[:, g, :], op=ALU.mult),
                               reads=[rcur] + RC, writes=[rcur])
                            op(V, lambda cur=cur, g=g: nc.vector.tensor_tensor(
                                out=pp[:, g, 0:16], in0=cur[:, 16:32], in1=xc[:, g, 16:32], op=ALU.subtract),
                               reads=[rcur, res("xc")], writes=[res("pp")])
                    op(G, lambda: nc.gpsimd.tensor_copy(out=xc[:, :, 0:16], in_=xc[:, :, TT:TT + 16]),
                       reads=[res("xc")], writes=[res("xc")])
                    def pool_mm():
                        for j in range(2):
                            b = pA_i[0] % 2
                            pA_i[0] += 1
                            for h in range(2):
                                g = 2 * j + h
                                grp(P, [lambda g=g, h=h, b=b: nc.tensor.matmul(
                                    pA[b][:, h * TT:(h + 1) * TT], wpool[:, g, :], pp[:, g, :], start=True, stop=True)],
                                    reads=[RL, res("pp")], writes=[res("pA%d" % b)])
                                op(A, lambda g=g, h=h, b=b: nc.scalar.activation(
                                    out=tA[g % 2][:], in_=pA[b][:, h * TT:(h + 1) * TT], func=AF.Copy, scale=psc[:, g:g + 1]),
                                   reads=[res("pA%d" % b), res("lsmall")], writes=[res("tA%d" % (g % 2))])
                                op(G, lambda g=g: nc.gpsimd.tensor_tensor(
                                    out=yT[:, 12 + g, :], in0=tA[g % 2][:], in1=sgc[:, g, :], op=ALU.mult),
                                   reads=[res("tA%d" % (g % 2)), res("sgc")], writes=[res("yT")])


                    next(s5it, None)
                    for j in range(2):
                        zcols(l, [24 + 2 * j, 25 + 2 * j], ev_silu(sga[:, 2 * j:2 * j + 2, :], "sga"))
                    next(s5it, None)

                    for c in range(2):
                        def ev_v(ts_, bank, rb, c=c):
                            op(A, lambda: nc.scalar.activation(
                                out=gv[ts_][:, c * 512:(c + 1) * 512], in_=bank[:], func=gelu_f),
                               reads=[rb], writes=[res("gv%d" % ts_)])
                            op(V, lambda: nc.vector.bn_stats(
                                out=bst[ts_][:, c, :], in_=gv[ts_][:, c * 512:(c + 1) * 512]),
                               reads=[res("gv%d" % ts_)], writes=[res("bst%d" % ts_)])
                        tokmaj([wsc_in[l, 12 + 4 * c + h] for h in range(4)], "wsc%d" % l,
                               lambda k, ts_: hT[:, k, ts_ * 128:(ts_ + 1) * 128], [res("hT%d" % hi)], ev_v,
                               after=lambda: next(s5it, None))
                        if c == 0:
                            pool_mm()
                    for ts_ in range(NS):
                        op(V, lambda ts_=ts_: nc.vector.bn_aggr(out=mv[:, ts_, :], in_=bst[ts_][:].rearrange("p a b -> p (a b)")),
                           reads=[res("bst%d" % ts_)], writes=[res("mv")])
                    op(V, lambda: nc.vector.tensor_scalar(
                        out=mv[:, :, 1], in0=mv[:, :, 1], scalar1=1e-5, scalar2=0.0, op0=ALU.add, op1=ALU.add),
                       reads=[res("mv")], writes=[res("mv")])
                    op(A, lambda: nc.scalar.activation(out=mv[:, :, 1], in_=mv[:, :, 1], func=AF.Sqrt),
                       reads=[res("mv")], writes=[res("mv")])
                    op(V, lambda: nc.vector.reciprocal(out=mv[:, :, 1], in_=mv[:, :, 1]),
                       reads=[res("mv")], writes=[res("mv")])
                    for ts_ in range(NS):
                        op(V, lambda ts_=ts_: nc.vector.tensor_scalar(
                            out=vn[ts_][:], in0=gv[ts_][:], scalar1=mv[:, ts_, 0:1], scalar2=mv[:, ts_, 1:2],
                            op0=ALU.subtract, op1=ALU.mult),
                           reads=[res("gv%d" % ts_), res("mv")], writes=[res("vn%d" % ts_)])

                    next(pf, None)
                    for hh in range(8):
                        e = hh % 2
                        def pre_s(hh=hh, e=e):
                            grp(P, [(lambda ts_=ts_, hh=hh: nc.tensor.matmul(
                                pY[:, ts_ * 128:(ts_ + 1) * 128], vn[ts_][:, hh * 128:(hh + 1) * 128], wsT[:, hh, :],
                                start=True, stop=True)) for ts_ in range(NS)],
                                reads=[res("vn0"), res("vn1"), RL], writes=[res("pY")])
                            op(A, lambda hh=hh, e=e: nc.scalar.activation(
                                out=tS[e][:], in_=pY[:, 0:256], func=AF.Copy, scale=lng[:, hh:hh + 1]),
                               reads=[res("pY"), res("lsmall")], writes=[res("tS%d" % e)])
                            op(G, lambda hh=hh, e=e: nc.gpsimd.tensor_tensor(
                                out=tS[e][:].rearrange("p (a b) -> p a b", a=2), in0=tS[e][:].rearrange("p (a b) -> p a b", a=2),
                                in1=Bt[:, hh:hh + 1, :].broadcast_to([128, 2, 128]), op=ALU.add),
                               reads=[res("tS%d" % e), res("lsmall")], writes=[res("tS%d" % e)])

                        def ev_sgu(bank, rb, hh=hh, e=e):
                            op(A, lambda: nc.scalar.activation(out=tA[e][:], in_=bank[:, 0:TT], func=gelu_f),
                               reads=[rb], writes=[res("tA%d" % e)])
                            op(A, lambda: nc.scalar.activation(out=tG[e][:], in_=bank[:, TT:2 * TT], func=AF.Tanh, scale=0.5),
                               reads=[rb], writes=[res("tG%d" % e)])
                            op(A, lambda: nc.scalar.copy(out=tW[:, e * TT:(e + 1) * TT], in_=bank[:, TT:2 * TT]),
                               reads=[rb], writes=[res("tW%d" % e)])
                            op(V, lambda: nc.vector.scalar_tensor_tensor(
                                out=tG[e][:], in0=tG[e][:], scalar=1.0, in1=tW[:, e * TT:(e + 1) * TT], op0=ALU.add, op1=ALU.mult),
                               reads=[res("tW%d" % e), res("tG%d" % e)], writes=[res("tG%d" % e)])
                            op(G, lambda: nc.gpsimd.tensor_tensor(out=tP[e][:], in0=tA[e][:], in1=tG[e][:], op=ALU.mult),
                               reads=[res("tA%d" % e), res("tG%d" % e)], writes=[res("tP%d" % e)])
                            op(V, lambda: nc.vector.scalar_tensor_tensor(
                                out=yT[:, 4 + hh, :], in0=tP[e][:], scalar=0.5, in1=tS[e][:], op0=ALU.mult, op1=ALU.mult),
                               reads=[res("tP%d" % e), res("tS%d" % e)], writes=[res("yT")])
                        def pre_all(pre_s=pre_s):
                            next(s5it, None)
                            next(pf, None)
                            pre_s()
                        zcols(l, [4 + hh, 28 + hh], ev_sgu, pre=pre_all)
                    for _ in s5it:
                        pass
                    for _ in pf:
                        pass

                    for ts_ in range(NS):
                        dma(G, sl_x[ts_], [lambda ts_=ts_: nc.gpsimd.dma_start(
                            out=xt[ts_][:], in_=usrc(it, ts_))],
                            reads=[res("U%d" % par)], writes=[res("xt%d" % ts_)])
                    if it + 1 < nstep:
                        CUR["hT"] = hTs[1 - hi]
                        CUR["hi"] = 1 - hi
                        emit_xa()
                        CUR["hT"] = hTs[hi]
                        CUR["hi"] = hi
                        s5it = s5_gen()
                    else:
                        s5it = iter(())
                    wo_cnt = [0]

                    def after_o():
                        wo_cnt[0] += 1
                        next(s5it, None)
                    for c in range(4):
                        def ev_o(ts_, bank, rb, c=c):
                            op(V, lambda: nc.vector.tensor_tensor(
                                out=xt[ts_][:, c * 512:(c + 1) * 512], in0=xt[ts_][:, c * 512:(c + 1) * 512],
                                in1=bank[:], op=ALU.add),
                               reads=[rb], writes=[res("xt%d" % ts_)])
                        tokmaj([wsc_out[l, 4 * c + h] for h in range(4)], "wsc%d" % l,
                               lambda k, ts_: yT[:, k, ts_ * 128:(ts_ + 1) * 128], [res("yT")], ev_o, after=after_o)
                    if it < ntile:
                        for ts_ in range(NS):
                            dma(G, sl_sd[ts_], [lambda ts_=ts_: nc.gpsimd.dma_start(
                                out=sendb[par].ap()[ts_ * 128:(ts_ + 1) * 128, :], in_=xt[ts_][:])],
                                reads=[res("xt%d" % ts_)], writes=[res("send%d" % par)])
                        _deps(G, [res("send%d" % par)], [res("U%d" % par)])
                        ins_ = nc.gpsimd.collective_compute(
                            "AllGather", ALU.bypass, replica_groups=[[0, 1], [2, 3], [4, 5], [6, 7]],
                            ins=[sendb[par].ap().opt()], outs=[Ub[par].ap().rearrange("a t d -> (a t) d").opt()])
                        ins_.then_inc(sl_cc.sem)
                        sl_cc.n += 1
                        _upd((sl_cc.key, sl_cc.sem, sl_cc.n), [res("send%d" % par)], [res("U%d" % par)])
                        tn_ = min(it + 2, ntile - 1)
                        dma(G, sl_u, [lambda tn_=tn_: nc.gpsimd.dma_start(
                            out=Ub[par].ap()[1], in_=x_in[tn_ * TT:(tn_ + 1) * TT, :])],
                            writes=[res("U%d" % par)])
                    for ts_ in range(NS):
                        op(A, lambda ts_=ts_: nc.scalar.activation(
                            out=hp[:], in_=xt[ts_][:], func=AF.Square, accum_out=ss[:, 2 + ts_:3 + ts_]),
                           reads=[res("xt%d" % ts_)], writes=[res("hp"), res("ss2")])
                    op(V, lambda: nc.vector.tensor_scalar(
                        out=rstd[:, 2:4], in0=ss[:, 2:4], scalar1=1.0 / D, scalar2=1e-6,
                        op0=ALU.mult, op1=ALU.add), reads=[res("ss2")], writes=[res("rstd2")])
                    op(A, lambda: nc.scalar.activation(out=rstd[:, 2:4], in_=rstd[:, 2:4], func=AF.Sqrt),
                       reads=[res("rstd2")], writes=[res("rstd2")])
                    op(V, lambda: nc.vector.reciprocal(out=rstd[:, 2:4], in_=rstd[:, 2:4]),
                       reads=[res("rstd2")], writes=[res("rstd2")])
                    for ts_ in range(NS):
                        rx = res("xt%d" % ts_)
                        op(A, lambda ts_=ts_: nc.scalar.activation(
                            out=xt[ts_][:], in_=xt[ts_][:], func=AF.Copy, scale=rstd[:, 2 + ts_:3 + ts_]),
                           reads=[rx, res("rstd2")], writes=[rx])
                        op(V, lambda ts_=ts_: nc.vector.tensor_tensor(
                            out=xt[ts_][:], in0=xt[ts_][:], in1=fg_rep[:], op=ALU.mult),
                           reads=[rx] + RC, writes=[rx])
                        if it >= 2:
                            dma(G, sl_st[ts_], [lambda ts_=ts_: nc.gpsimd.dma_start(
                                out=out[t0 + ts_ * 128:t0 + (ts_ + 1) * 128, :], in_=xt[ts_][:])],
                                reads=[rx], writes=[res("outd")])
        except _Stop:
            pass
        for s in sl_st + sl_sd:
            if s.n:
                nc.gpsimd.wait_ge(s.sem, s.n)
    return nc


vslots = {}
_CACHE = {}


def _consts():
    w = [2, 4, 8, 16]
    pos = np.arange(1, 17, dtype=np.float32)
    pdiv0 = np.stack([1.0 / np.minimum(pos, float(wi)) for wi in w]).astype(np.float32).reshape(1, -1)
    pdivc = np.stack([np.full(256, 1.0 / wi, np.float32) for wi in w]).reshape(1, -1)
    pm = np.zeros((128, 4), np.float32)
    pm[:64, 0] = 1.0
    pm[64:, 1] = 1.0
    for p in range(128):
        pm[p, 2 + ((p // 32) % 2)] = 1.0
    return {
        "c_identb": np.eye(128, dtype=np.float32).astype(ml_dtypes.bfloat16),
        "c_identf": np.eye(128, dtype=np.float32),
        "c_tril": np.tril(np.ones((128, 128), np.float32)),
        "c_tio": np.arange(256, dtype=np.float32).reshape(1, 256),
        "c_pdiv0": pdiv0, "c_pdiv2": np.ascontiguousarray(pdivc.reshape(4, 256)[:, :16]).reshape(1, -1), "c_pm": pm,
    }


def kernel(**inputs):
    seq, batch = CFG["seq"], CFG["batch"]
    key = (seq, CFG["gelu"], CFG.get("stop"))
    if key not in _CACHE:
        _CACHE[key] = build(seq)
    nc = _CACHE[key]
    x = np.ascontiguousarray(np.asarray(inputs["x"], dtype=np.float32))
    cst = _consts()
    per_layer = {}
    for k, v in inputs.items():
        if k in ("x", "final_g"):
            continue
        v = np.asarray(v, dtype=np.float32)
        per_layer[k] = [np.ascontiguousarray(v[r:r + 1]) for r in range(2)]
    fg = np.ascontiguousarray(np.asarray(inputs["final_g"], dtype=np.float32).reshape(1, D))
    zeros_x = np.zeros((seq, D), np.float32)
    in_maps = []
    for c in range(2 * batch):
        b, role = c // 2, c % 2
        m = {k: v[role] for k, v in per_layer.items()}
        m["final_g"] = fg
        m.update(cst)
        m["x"] = x[b] if role == 0 else zeros_x
        m["c_sel"] = np.array([[1 - role]], np.int32)
        if role == 1:
            m["c_pdiv0"], m["c_pdiv2"] = cst["c_pdiv2"], cst["c_pdiv0"]
        in_maps.append(m)
    res = run_bass_kernel_spmd(nc, in_maps, core_ids=list(range(2 * batch)))
    return np.stack([np.asarray(res.results[2 * b + 1]["out"]) for b in range(batch)], axis=0).astype(np.float32)
```

```python
import numpy as np
import ml_dtypes
from contextlib import ExitStack
import concourse.bass as bass
import concourse.mybir as mybir
from concourse.bass_utils import run_bass_kernel_spmd

F32 = mybir.dt.float32
BF16 = mybir.dt.bfloat16
AF = mybir.ActivationFunctionType
ALU = mybir.AluOpType

D = 2048
NK = 16
INC = 5120
TT = 256
NS = 2
PI = float(np.pi)
CFG = {"seq": 8192, "depth": 2, "batch": 4, "gelu": "tanh"}
I32 = mybir.dt.int32


class _Stop(Exception):
    pass


def stop_at(n):
    if CFG.get("stop") == n:
        raise _Stop()


class Res:
    __slots__ = ("w", "r", "excl")

    def __init__(self, excl=False):
        self.w = None
        self.r = {}
        self.excl = excl


class Eng:
    def __init__(self, e, sem, key, is_pe=False):
        self.e = e
        self.sem = sem
        self.key = key
        self.is_pe = is_pe
        self.n = 0
        self.seen = {}

    def need(self, st):
        if st is None:
            return
        key, sem, val = st
        if key == self.key and self.is_pe:
            return
        if self.seen.get(key, 0) >= val:
            return
        self.e.wait_ge(sem, val)
        self.seen[key] = val


class Slot:
    def __init__(self, sem, key):
        self.sem = sem
        self.key = key
        self.n = 0


def _split(reads, writes):
    ex = [r for r in reads if r.excl]
    if ex:
        reads = [r for r in reads if not r.excl]
        writes = list(writes) + ex
    return reads, writes


def _deps(E, reads, writes):
    reads, writes = _split(reads, writes)
    for r in reads:
        E.need(r.w)
    for w in writes:
        E.need(w.w)
        for st in w.r.values():
            E.need(st)


def _upd(st, reads, writes):
    reads, writes = _split(reads, writes)
    for w in writes:
        w.w = st
        w.r = {}
    for r in reads:
        r.r[st[0]] = st


def op(E, fn, reads=(), writes=()):
    _deps(E, reads, writes)
    ins = fn()
    E.n += 1
    ins.then_inc(E.sem, 1)
    _upd((E.key, E.sem, E.n), reads, writes)


def grp(E, fns, reads=(), writes=()):
    _deps(E, reads, writes)
    ins = None
    for fn in fns:
        ins = fn()
    E.n += 1
    ins.then_inc(E.sem, 1)
    _upd((E.key, E.sem, E.n), reads, writes)


def dma(Q, slot, fns, reads=(), writes=()):
    _deps(Q, reads, writes)
    if slot.n:
        Q.need((slot.key, slot.sem, slot.n))
    for fn in fns:
        fn().then_inc(slot.sem, 16)
        slot.n += 16
    _upd((slot.key, slot.sem, slot.n), reads, writes)


def build(seq, depth_unused=1):
    depth = 1
    ntile = seq // TT
    nstep = ntile + 2
    nc = bass.Bass("TRN2", target_bir_lowering=False)

    def din(name, shape, dt=F32):
        return nc.dram_tensor(name, list(shape), dt, kind="ExternalInput").ap()

    x_in = din("x", [seq, D])
    norm_g = din("norm_g", [depth, D])
    w_in = din("w_in", [depth, D, INC])
    lam_re = din("lam_re", [depth, 32, 64])
    lam_im = din("lam_im", [depth, 32, 64])
    b_re = din("b_re", [depth, 32, 64, 16])
    b_im = din("b_im", [depth, 32, 64, 16])
    c_re = din("c_re", [depth, 32, 16, 64])
    c_im = din("c_im", [depth, 32, 16, 64])
    d_skip = din("d_skip", [depth, 32, 16])
    log_dt = din("log_dt", [depth, 32])
    w_glu = din("w_glu", [depth, 512, 512])
    b_glu = din("b_glu", [depth, 512])
    ln_g = din("ln_g", [depth, 1024])
    ln_b = din("ln_b", [depth, 1024])
    w_s = din("w_s", [depth, 8, 128, 128])
    b_s = din("b_s", [depth, 8, 128])
    w_pool = din("w_pool", [depth, 4, 128, 128])
    pool_scale = din("pool_scale", [depth, 512])
    w_out = din("w_out", [depth, D, D])
    final_g = din("final_g", [1, D])
    c_identb = din("c_identb", [128, 128], BF16)
    c_identf = din("c_identf", [128, 128])
    c_tril = din("c_tril", [128, 128])
    c_tio = din("c_tio", [1, 256])
    c_pdiv0 = din("c_pdiv0", [1, 4 * 16])
    c_pm = din("c_pm", [128, 4])
    c_pdiv2 = din("c_pdiv2", [1, 4 * 16])
    c_sel = din("c_sel", [1, 1], I32)
    out = nc.dram_tensor("out", [seq, D], F32, kind="ExternalOutput").ap()
    wsc_in = nc.dram_tensor("wsc_in", [depth, 40, 128, NK, 128], BF16, kind="Internal").ap()
    wsc_out = nc.dram_tensor("wsc_out", [depth, 16, 128, NK, 128], BF16, kind="Internal").ap()
    sendb = [nc.dram_tensor("sendb%d" % i, [TT, D], F32, kind="Internal") for i in range(2)]
    Ub = [nc.dram_tensor("Ub%d" % i, [2, TT, D], F32, kind="Internal") for i in range(2)]

    es = ExitStack()
    with es:
        def sb(name, shape, dt=F32):
            return es.enter_context(nc.sbuf_tensor(name, list(shape), dt))

        def pst(name, shape, dt=F32):
            return es.enter_context(nc.psum_tensor(name, list(shape), dt))

        def sem(name):
            return es.enter_context(nc.semaphore(name))

        P = Eng(nc.tensor, sem("s_pe"), "pe", True)
        A = Eng(nc.scalar, sem("s_act"), "act")
        V = Eng(nc.vector, sem("s_dve"), "dve")
        G = Eng(nc.gpsimd, sem("s_pool"), "pool")
        SY = Eng(nc.sync, sem("s_sy"), "sy")
        nslot = [0]

        def slot():
            nslot[0] += 1
            return Slot(sem("s_d%d" % nslot[0]), "d%d" % nslot[0])

        identb = sb("identb", [128, 128], BF16)
        identf = sb("identf", [128, 128])
        tril = sb("tril", [128, 128])
        tio = sb("tio", [128, 256])
        pdiv0 = sb("pdiv0", [128, 4, 16])
        pdiv2 = sb("pdiv2", [128, 4, 16])
        selt = sb("selt", [1, 1], I32)
        pm = sb("pm", [128, 4])
        fg_rep = sb("fg_rep", [128, D])
        xt = [sb("xt%d" % i, [128, D]) for i in range(NS)]
        hp = sb("hp", [128, D], BF16)
        hTs = [sb("hT%d" % i, [128, NK, TT], BF16) for i in range(2)]
        NW = 5
        W = [sb("w%d" % i, [128, NK, 128], BF16) for i in range(NW)]
        gv = [sb("gv%d" % i, [128, 1024]) for i in range(NS)]
        vn = [sb("vn%d" % i, [128, 1024], BF16) for i in range(NS)]
        tA = [sb("tA%d" % i, [128, TT]) for i in range(2)]
        tG = [sb("tG%d" % i, [128, TT]) for i in range(2)]
        tS = [sb("tS%d" % i, [128, TT]) for i in range(2)]
        tP = [sb("tP%d" % i, [128, TT]) for i in range(2)]
        tW = sb("tW", [128, 2 * TT])
        bglh = sb("bglh", [128, 4])
        yT = sb("yT", [128, NK, TT], BF16)
        xaf = sb("xaf", [128, 4, TT])
        xab = sb("xab", [128, 4, TT], BF16)
        sga = sb("sga", [128, 4, TT], BF16)
        sgc = sb("sgc", [128, 4, TT], BF16)
        xc = sb("xc", [128, 4, 16 + TT])
        pa = sb("pa", [128, 16 + TT])
        pb = sb("pb", [128, 16 + TT])
        pp = sb("pp", [128, 4, TT], BF16)
        cosT = sb("cosT", [128, 16, TT])
        sinT = sb("sinT", [128, 16, TT])
        qa = [sb("qa%d" % i, [128, 2, TT]) for i in range(2)]
        qb = [sb("qb%d" % i, [128, 2, TT]) for i in range(2)]
        qu = [sb("qu%d" % i, [128, 2, TT]) for i in range(2)]
        qt = [sb("qt%d" % i, [128, 2, TT]) for i in range(2)]
        t1, t2, t3, t4, Tre, Tim = qa[0], qa[1], qb[0], qb[1], qu[0], qu[1]
        Sl = sb("Sl", [128, 2, 16])
        kit = qt[1][:].bitcast(mybir.dt.int32)
        Sb = sb("Sb", [128, 4, 4, TT], BF16)
        Cpad = sb("Cpad", [128, 16, 3, 128], BF16)
        BTr = sb("BTr", [128, 4, 2, 128], BF16)
        BTi = sb("BTi", [128, 4, 2, 128], BF16)
        ypre = sb("ypre", [128, TT])
        yg = sb("yg", [128, 4, TT])
        ygb = sb("ygb", [128, 4, TT], BF16)
        wglu = sb("wglu", [128, 4, 512], BF16)
        wpool = sb("wpool", [128, 4, 128], BF16)
        wsT = sb("wsT", [128, 8, 128], BF16)
        Bt = sb("Bt", [128, 8, 128])
        lng = sb("lng", [128, 8])
        lnb = sb("lnb", [128, 8])
        psc = sb("psc", [128, 4])
        bgl = sb("bgl", [128, 4])
        dsk = sb("dsk", [128, 4])
        gk = sb("gk", [128, NK])
        Tin_re = sb("Tin_re", [128, 16])
        Tin_im = sb("Tin_im", [128, 16])
        rr = sb("rr", [128, 16])
        th = sb("th", [128, 16])
        sp = [sb("sp%d" % i, [128, 16]) for i in range(12)]
        Bre = t3[:].rearrange("p a t -> p (a t)")[:, 0:256].rearrange("p (g h) -> p g h", g=16)
        Bim = t4[:].rearrange("p a t -> p (a t)")[:, 0:256].rearrange("p (g h) -> p g h", g=16)
        BBr = Tre[:].rearrange("p a t -> p (a t)")[:, 0:256].rearrange("p (g h) -> p g h", g=16)
        BBi = Tim[:].rearrange("p a t -> p (a t)")[:, 0:256].rearrange("p (g h) -> p g h", g=16)
        in3r = t1[:].rearrange("p a t -> p (a t)")
        in3i = t2[:].rearrange("p a t -> p (a t)")
        Cre = gv[0][:, 0:256].rearrange("p (g h) -> p g h", g=16)
        Cim = gv[0][:, 256:512].rearrange("p (g h) -> p g h", g=16)
        ss = sb("ss", [128, 4])
        rstd = sb("rstd", [128, 4])
        bst = [sb("bst%d" % i, [128, 2, 6]) for i in range(NS)]
        mv = sb("mv", [128, NS, 2])
        cr = sb("cr", [128, 4, 16])
        cthk = sb("cthk", [128, 16])
        sthk = sb("sthk", [128, 16])
        ki16 = sb("ki16", [128, 16], mybir.dt.int32)
        cvi = [xt[i][:].rearrange("p (k c) -> p k c", k=NK) for i in range(2)]
        cvo = [yT[:, 8 * i:8 * i + 8, :].rearrange("p a (b c) -> p (a b) c", c=128) for i in range(2)]
        cvi4 = [xt[i][:].rearrange("p (kk c) -> p kk c", kk=4) for i in range(2)]
        cvo4 = [yT[:, 8 * i:8 * i + 8, :].rearrange("p (kk a) t -> p kk (a t)", a=2) for i in range(2)]
        ones_b = sb("ones_b", [128, 128], BF16)

        pA = [pst("pA%d" % i, [128, 512]) for i in range(2)]
        pV = [pst("pV%d" % i, [128, 512]) for i in range(2)]
        pS = [pst("pS%d" % i, [128, 512]) for i in range(2)]
        pY = pst("pY", [128, 512])
        pT = pst("pT", [128, 1024], BF16)

        blk = es.enter_context(nc.Block())

        R = {}

        def res(name):
            if name not in R:
                R[name] = Res(excl=name in ("pA0", "pA1", "pV0", "pV1", "pS0", "pS1", "pY", "pT"))
            return R[name]

        sl_c = slot()
        sl_x = [slot() for _ in range(NS)]
        sl_w = [slot() for _ in range(NW)]
        sl_st = [slot() for _ in range(NS)]
        sl_sd = [slot() for _ in range(NS)]
        sl_cv = [slot() for _ in range(2)]
        sl_cvo = [slot() for _ in range(2)]
        sl_p = slot()

        gelu_f = AF.Gelu_apprx_tanh if CFG["gelu"] == "tanh" else AF.Gelu

        try:
            dma(SY, sl_c, [
                lambda: nc.sync.dma_start(out=identb[:], in_=c_identb[:, :]),
                lambda: nc.sync.dma_start(out=identf[:], in_=c_identf[:, :]),
                lambda: nc.sync.dma_start(out=tril[:], in_=c_tril[:, :]),
                lambda: nc.sync.dma_start(out=tio[:], in_=c_tio[0:1, :].broadcast_to([128, 256])),
                lambda: nc.sync.dma_start(out=pdiv0[:].rearrange("p g t -> p (g t)"),
                                          in_=c_pdiv0[0:1, :].broadcast_to([128, 64])),
                lambda: nc.sync.dma_start(out=pm[:], in_=c_pm[:, :]),
                lambda: nc.sync.dma_start(out=pdiv2[:].rearrange("p g t -> p (g t)"),
                                          in_=c_pdiv2[0:1, :].broadcast_to([128, 64])),
                lambda: nc.sync.dma_start(out=selt[:], in_=c_sel[:, :]),
                lambda: nc.sync.dma_start(out=fg_rep[:], in_=final_g[0:1, :].broadcast_to([128, D])),
            ], writes=[res("const")])
            RC = [res("const")]

            selr = es.enter_context(nc.gpsimd.register("selr"))
            G.need(res("const").w)
            nc.gpsimd.reg_load(selr, selt[:1, :1])
            SELIDX = nc.gpsimd.snap(selr, min_val=0, max_val=1)
            sl_cc = Slot(sem("s_cc"), "cc")
            sl_u = slot()
            op(G, lambda: nc.gpsimd.memset(xt[0][:], 0.0), writes=[res("xt0")])
            for par in range(2):
                dma(G, sl_u, [
                    (lambda par=par, r=r: nc.gpsimd.dma_start(out=Ub[par].ap()[0][r * 128:(r + 1) * 128, :], in_=xt[0][:]))
                    for r in range(2)] + [
                    lambda par=par: nc.gpsimd.dma_start(out=Ub[par].ap()[1], in_=x_in[par * TT:(par + 1) * TT, :])],
                    reads=[res("xt0")], writes=[res("U%d" % par)])

            def usrc(step, ts_):
                return Ub[step % 2].ap()[SELIDX][ts_ * 128:(ts_ + 1) * 128, :]

            cnt = 0
            for l in range(depth):
                dma(SY, sl_p, [lambda l=l: nc.sync.dma_start(
                    out=gk[:], in_=norm_g[l].rearrange("(k p) -> p k", p=128),
                    allow_slow_non_contiguous=True)], writes=[res("gk")])
                for m in range(40 + 16):
                    s = cnt % 2
                    cnt += 1
                    wide = (12 <= m < 20) or m >= 40
                    kg = 0
                    if m < 40:
                        if wide:
                            c0 = (12 + 4 * ((m - 12) // 4)) * 128
                            kg = (m - 12) % 4
                            src = w_in[l][kg * 512:(kg + 1) * 512, c0:c0 + 512].rearrange("(kk p) c -> p kk c", p=128)
                        else:
                            src = w_in[l][:, m * 128:(m + 1) * 128].rearrange("(k p) c -> p k c", p=128)
                        dst = wsc_in[l, m]
                    else:
                        mo = m - 40
                        c0 = (mo // 4) * 512
                        kg = mo % 4
                        src = w_out[l][kg * 512:(kg + 1) * 512, c0:c0 + 512].rearrange("(kk p) c -> p kk c", p=128)
                        dst = wsc_out[l, mo]
                    ci = cvi4[s] if wide else cvi[s]
                    dma(SY, sl_cv[s], [lambda ci=ci, src=src: nc.sync.dma_start(out=ci[:], in_=src)],
                        writes=[res("xt%d" % s)])
                    E = V if (cnt % 2 == 0) else G
                    if m < 40 and wide:
                        op(E, lambda E=E, s=s, kg=kg: E.e.tensor_tensor(
                            out=cvo4[s][:], in0=cvi4[s][:],
                            in1=gk[:, 4 * kg:4 * kg + 4].unsqueeze(2).broadcast_to([128, 4, 512]), op=ALU.mult),
                           reads=[res("xt%d" % s), res("gk")], writes=[res("cvo%d" % s)])
                    elif m < 40:
                        op(E, lambda E=E, s=s: E.e.tensor_tensor(
                            out=cvo[s][:], in0=cvi[s][:],
                            in1=gk[:].unsqueeze(2).broadcast_to([128, NK, 128]), op=ALU.mult),
                           reads=[res("xt%d" % s), res("gk")], writes=[res("cvo%d" % s)])
                    else:
                        op(E, lambda E=E, s=s: E.e.tensor_copy(out=cvo[s][:], in_=cvi[s][:]),
                           reads=[res("xt%d" % s)], writes=[res("cvo%d" % s)])
                    dma(A, sl_cvo[s], [lambda s=s, dst=dst: nc.scalar.dma_start(out=dst, in_=cvo[s][:])],
                        reads=[res("cvo%d" % s)], writes=[res("wsc%d" % l)])

            for nm_ in ("cvo0", "cvo1"):
                R_ = res(nm_)
                res("yT").r.update(R_.r)
                if R_.w is not None:
                    res("yT").r[R_.w[0]] = R_.w
            stop_at(1)
            w_i = [0]

            def load_w(src_ap, rname):
                s = w_i[0] % NW
                w_i[0] += 1
                dma(SY, sl_w[s], [lambda: nc.sync.dma_start(out=W[s][:], in_=src_ap)],
                    reads=[res(rname)], writes=[res("w%d" % s)])
                return s

            pA_i = [0]
            pV_i = [0]
            CUR = {"hT": hTs[0], "hi": 0}

            def zcols(l, ms, evac, pre=None):
                b = pA_i[0] % 2
                pA_i[0] += 1
                hT = CUR["hT"]
                rh = res("hT%d" % CUR["hi"])
                for h, m in enumerate(ms):
                    s = load_w(wsc_in[l, m], "wsc%d" % l)
                    grp(P, [(lambda k=k, s=s, h=h, b=b: nc.tensor.matmul(
                        pA[b][:, h * TT:(h + 1) * TT], W[s][:, k, :], hT[:, k, :],
                        start=(k == 0), stop=(k == NK - 1))) for k in range(NK)],
                        reads=[res("w%d" % s), rh], writes=[res("pA%d" % b)])
                if pre is not None:
                    pre()
                evac(pA[b], res("pA%d" % b))

            def tokmaj(srcs, rname, lhs_fn, rlhs, evac, after=None):
                slots = [load_w(a_, rname) for a_ in srcs]
                for ts_ in range(NS):
                    b = pV_i[0] % 2
                    pV_i[0] += 1
                    for h, s in enumerate(slots):
                        grp(P, [(lambda kk=kk, s=s, h=h, b=b, ts_=ts_: nc.tensor.matmul(
                            pV[b][:, 0:512], lhs_fn(4 * h + kk, ts_),
                            W[s][:].rearrange("p (kk a) c -> p kk (a c)", a=4)[:, kk, :],
                            start=(4 * h + kk == 0), stop=(4 * h + kk == NK - 1))) for kk in range(4)],
                            reads=[res("w%d" % s)] + rlhs, writes=[res("pV%d" % b)])
                    if after is not None:
                        after()
                    evac(ts_, pV[b], res("pV%d" % b))

            def prefetch_gen(srcx, tn, hi):
                hTn = hTs[hi]
                for ts_ in range(NS):
                    dma(G, sl_x[ts_], [lambda ts_=ts_: nc.gpsimd.dma_start(
                        out=xt[ts_][:], in_=usrc(tn, ts_))],
                        reads=[res("U%d" % (tn % 2))], writes=[res("xt%d" % ts_)])
                yield
                for ts_ in range(NS):
                    op(A, lambda ts_=ts_: nc.scalar.activation(
                        out=hp[:], in_=xt[ts_][:], func=AF.Square, accum_out=ss[:, ts_:ts_ + 1]),
                       reads=[res("xt%d" % ts_)], writes=[res("hp"), res("ss")])
                op(V, lambda: nc.vector.tensor_scalar(
                    out=rstd[:, 0:2], in0=ss[:, 0:2], scalar1=1.0 / D, scalar2=1e-6,
                    op0=ALU.mult, op1=ALU.add), reads=[res("ss")], writes=[res("rstd")])
                op(A, lambda: nc.scalar.activation(out=rstd[:, 0:2], in_=rstd[:, 0:2], func=AF.Sqrt),
                   reads=[res("rstd")], writes=[res("rstd")])
                op(V, lambda: nc.vector.reciprocal(out=rstd[:, 0:2], in_=rstd[:, 0:2]),
                   reads=[res("rstd")], writes=[res("rstd")])
                yield
                for ts_ in range(NS):
                    op(A, lambda ts_=ts_: nc.scalar.activation(
                        out=hp[:], in_=xt[ts_][:], func=AF.Copy, scale=rstd[:, ts_:ts_ + 1]),
                       reads=[res("xt%d" % ts_), res("rstd")], writes=[res("hp")])
                    yield
                    for kb in range(2):
                        grp(P, [(lambda k=k, kb=kb: nc.tensor.transpose(
                            out=pT[:, (k - kb * 8) * 128:(k - kb * 8 + 1) * 128],
                            in_=hp[:, k * 128:(k + 1) * 128], identity=identb[:]))
                            for k in range(kb * 8, kb * 8 + 8)],
                            reads=[res("hp")] + RC, writes=[res("pT")])
                        if kb == 0:
                            op(V, lambda ts_=ts_, kb=kb: nc.vector.tensor_copy(
                                out=hTn[:, kb * 8:kb * 8 + 8, ts_ * 128:(ts_ + 1) * 128],
                                in_=pT[:].rearrange("p (a b) -> p a b", a=8)),
                               reads=[res("pT")], writes=[res("hT%d" % hi)])
                        else:
                            op(A, lambda ts_=ts_, kb=kb: nc.scalar.copy(
                                out=hTn[:, kb * 8:kb * 8 + 8, ts_ * 128:(ts_ + 1) * 128],
                                in_=pT[:].rearrange("p (a b) -> p a b", a=8)),
                               reads=[res("pT")], writes=[res("hT%d" % hi)])
                    yield

            gtile = [0]
            for l in range(depth):
                src = None
                last = (l == depth - 1)
                RL = res("layerw")
                cv0f = cvi[0][:].rearrange("p k c -> p (k c)")
                cv1f = cvi[1][:].rearrange("p k c -> p (k c)")
                dma(SY, sl_p, [lambda: nc.sync.dma_start(
                    out=cv0f[:, 0:512].rearrange("p (g d) -> p g d", g=4),
                    in_=w_pool[l].rearrange("g c d -> c g d"))], writes=[res("xt0")])
                op(V, lambda: nc.vector.tensor_copy(
                    out=wpool[:], in_=cv0f[:, 0:512].rearrange("p (g d) -> p g d", g=4)),
                   reads=[res("xt0")], writes=[RL])
                dma(SY, sl_p, [lambda: nc.sync.dma_start(
                    out=cv1f[:, 0:2048].rearrange("p (k n) -> p k n", k=4),
                    in_=w_glu[l].rearrange("(k p) n -> p k n", p=128))], writes=[res("xt1")])
                op(V, lambda: nc.vector.tensor_copy(
                    out=wglu[:], in_=cv1f[:, 0:2048].rearrange("p (k n) -> p k n", k=4)),
                   reads=[res("xt1")], writes=[RL])
                dma(SY, sl_p, [lambda: nc.sync.dma_start(
                    out=cvi[0][:, 0:8, :], in_=w_s[l].rearrange("h t s -> t h s"))], writes=[res("xt0")])
                op(V, lambda: nc.vector.tensor_tensor(
                    out=cvi[0][:, 0:8, :], in0=cvi[0][:, 0:8, :],
                    in1=tril[:].unsqueeze(1).broadcast_to([128, 8, 128]), op=ALU.mult),
                   reads=[res("xt0")] + RC, writes=[res("xt0")])
                for hh in range(8):
                    half = hh % 4
                    if half == 0:
                        pass
                    grp(P, [lambda hh=hh, half=half: nc.tensor.transpose(
                        out=pY[:, half * 128:(half + 1) * 128], in_=cvi[0][:, hh, :], identity=identf[:])],
                        reads=[res("xt0")] + RC, writes=[res("pY")])
                    if half == 3:
                        op(V, lambda hh=hh: nc.vector.tensor_copy(
                            out=wsT[:, hh - 3:hh + 1, :], in_=pY[:].rearrange("p (a b) -> p a b", a=4)),
                           reads=[res("pY")], writes=[RL])
                op(G, lambda: nc.gpsimd.memset(ones_b[:], 1.0), writes=[res("ones")])
                dma(SY, sl_p, [
                    lambda: nc.sync.dma_start(out=Bt[:].rearrange("p h t -> p (h t)"),
                                              in_=b_s[l:l + 1].rearrange("o h t -> o (h t)").broadcast_to([128, 1024])),
                    lambda: nc.sync.dma_start(out=lng[:], in_=ln_g[l].rearrange("(h d) -> d h", d=128),
                                              allow_slow_non_contiguous=True),
                    lambda: nc.sync.dma_start(out=lnb[:], in_=ln_b[l].rearrange("(h d) -> d h", d=128),
                                              allow_slow_non_contiguous=True),
                    lambda: nc.sync.dma_start(out=psc[:], in_=pool_scale[l].rearrange("(g d) -> d g", d=128),
                                              allow_slow_non_contiguous=True),
                    lambda: nc.sync.dma_start(out=bgl[:], in_=b_glu[l].rearrange("(g d) -> d g", d=128),
                                              allow_slow_non_contiguous=True),
                    lambda: nc.sync.dma_start(out=dsk[:], in_=d_skip[l].rearrange("(c g) h -> (g h) c", g=8),
                                              allow_slow_non_contiguous=True),
                ], writes=[res("lsmall")])
                op(V, lambda: nc.vector.tensor_scalar(out=bglh[:], in0=bgl[:], scalar1=0.5, scalar2=0.0, op0=ALU.mult, op1=ALU.add),
                   reads=[res("lsmall")], writes=[res("lsmall")])
                for hh in range(8):
                    grp(P, [lambda hh=hh: nc.tensor.matmul(
                        pY[:, 0:128], ones_b[:], wsT[:, hh, :], start=True, stop=True)],
                        reads=[res("ones"), RL], writes=[res("pY")])
                    op(V, lambda hh=hh: nc.vector.scalar_tensor_tensor(
                        out=Bt[:, hh, :], in0=pY[:, 0:128], scalar=lnb[:, hh:hh + 1], in1=Bt[:, hh, :],
                        op0=ALU.mult, op1=ALU.add),
                       reads=[res("pY"), res("lsmall")], writes=[res("lsmall")])

                stop_at(2)
                RS = res("s5w")
                ALIAS = [res("qa0"), res("qa1"), res("qb0"), res("qb1"), res("qu0"), res("qu1")]
                fns = []
                for gl in range(2):
                    ps_ = slice(gl * 64, (gl + 1) * 64)
                    fns += [
                        lambda gl=gl, ps_=ps_: nc.sync.dma_start(out=sp[0][ps_, :], in_=lam_re[l].rearrange("(gh gl) p -> gl p gh", gl=2)[gl], allow_slow_non_contiguous=True),
                        lambda gl=gl, ps_=ps_: nc.sync.dma_start(out=sp[1][ps_, :], in_=lam_im[l].rearrange("(gh gl) p -> gl p gh", gl=2)[gl], allow_slow_non_contiguous=True),
                        lambda gl=gl, ps_=ps_: nc.sync.dma_start(out=sp[2][ps_, :], in_=log_dt[l:l + 1].rearrange("o (gh gl) -> gl o gh", gl=2)[gl].broadcast_to([64, 16]), allow_slow_non_contiguous=True),
                        lambda gl=gl, ps_=ps_: nc.sync.dma_start(out=Bre[ps_], in_=b_re[l].rearrange("(gh gl) p h -> gl p gh h", gl=2)[gl], allow_slow_non_contiguous=True),
                        lambda gl=gl, ps_=ps_: nc.sync.dma_start(out=Bim[ps_], in_=b_im[l].rearrange("(gh gl) p h -> gl p gh h", gl=2)[gl], allow_slow_non_contiguous=True),
                    ]
                    for gh in range(16):
                        fns += [
                            lambda gl=gl, ps_=ps_, gh=gh: nc.sync.dma_start(out=Cre[ps_, gh, :], in_=c_re[l, 2 * gh + gl].rearrange("h p -> p h"), allow_slow_non_contiguous=True),
                            lambda gl=gl, ps_=ps_, gh=gh: nc.sync.dma_start(out=Cim[ps_, gh, :], in_=c_im[l, 2 * gh + gl].rearrange("h p -> p h"), allow_slow_non_contiguous=True),
                        ]
                dma(SY, sl_p, fns, writes=[res("s5raw"), res("gv0")] + ALIAS)
                RR = [res("s5raw")]
                lre, lim, ldt = sp[0], sp[1], sp[2]
                dt_, are, a1, sth, cth, lbr, lbi, den, qr, qi = sp[3], sp[4], sp[5], sp[6], sp[7], sp[8], sp[9], sp[10], sp[11], sp[2]
                RP = res("s5p")
                op(A, lambda: nc.scalar.activation(out=dt_[:], in_=ldt[:], func=AF.Exp), reads=RR, writes=[RP])
                op(V, lambda: nc.vector.tensor_tensor(out=are[:], in0=lre[:], in1=dt_[:], op=ALU.mult), reads=RR + [RP], writes=[RP])
                op(V, lambda: nc.vector.tensor_tensor(out=th[:], in0=lim[:], in1=dt_[:], op=ALU.mult), reads=RR + [RP], writes=[RP, RS])
                op(A, lambda: nc.scalar.activation(out=rr[:], in_=are[:], func=AF.Exp), reads=[RP], writes=[RP, RS])
                def sincos_small(dst, off, mul=1.0):
                    op(V, lambda: nc.vector.tensor_scalar(out=a1[:], in0=th[:], scalar1=mul / (2 * PI), scalar2=off / (2 * PI), op0=ALU.mult, op1=ALU.add), reads=[RP], writes=[RP])
                    op(V, lambda: nc.vector.tensor_copy(out=ki16[:], in_=a1[:]), reads=[RP], writes=[RP])
                    op(V, lambda: nc.vector.tensor_copy(out=den[:], in_=ki16[:]), reads=[RP], writes=[RP])
                    op(V, lambda: nc.vector.tensor_tensor(out=a1[:], in0=a1[:], in1=den[:], op=ALU.subtract), reads=[RP], writes=[RP])
                    op(V, lambda: nc.vector.tensor_scalar(out=den[:], in0=a1[:], scalar1=0.5, scalar2=1.0, op0=ALU.is_gt, op1=ALU.mult), reads=[RP], writes=[RP])
                    op(V, lambda: nc.vector.tensor_tensor(out=a1[:], in0=a1[:], in1=den[:], op=ALU.subtract), reads=[RP], writes=[RP])
                    op(V, lambda: nc.vector.tensor_scalar(out=den[:], in0=a1[:], scalar1=-0.5, scalar2=1.0, op0=ALU.is_lt, op1=ALU.mult), reads=[RP], writes=[RP])
                    op(V, lambda: nc.vector.tensor_tensor(out=a1[:], in0=a1[:], in1=den[:], op=ALU.add), reads=[RP], writes=[RP])
                    op(A, lambda: nc.scalar.activation(out=dst[:], in_=a1[:], func=AF.Sin, scale=2 * PI), reads=[RP], writes=[RP])
                sincos_small(sth, 0.0)
                sincos_small(cth, 0.5 * PI)
                MARK_CARRY_TABLES = True
                op(V, lambda: nc.vector.tensor_tensor(out=lbr[:], in0=rr[:], in1=cth[:], op=ALU.mult), reads=[RP], writes=[RP])
                op(V, lambda: nc.vector.tensor_tensor(out=lbi[:], in0=rr[:], in1=sth[:], op=ALU.mult), reads=[RP], writes=[RP])
                op(V, lambda: nc.vector.tensor_scalar(out=lbr[:], in0=lbr[:], scalar1=-1.0, scalar2=0.0, op0=ALU.add, op1=ALU.add), reads=[RP], writes=[RP])
                op(V, lambda: nc.vector.tensor_tensor(out=den[:], in0=lre[:], in1=lre[:], op=ALU.mult), reads=RR + [RP], writes=[RP])
                op(V, lambda: nc.vector.tensor_tensor(out=a1[:], in0=lim[:], in1=lim[:], op=ALU.mult), reads=RR + [RP], writes=[RP])
                op(V, lambda: nc.vector.tensor_tensor(out=den[:], in0=den[:], in1=a1[:], op=ALU.add), reads=[RP], writes=[RP])
                op(V, lambda: nc.vector.reciprocal(out=den[:], in_=den[:]), reads=[RP], writes=[RP])
                op(V, lambda: nc.vector.tensor_tensor(out=qr[:], in0=lbr[:], in1=lre[:], op=ALU.mult), reads=RR + [RP], writes=[RP])
                op(V, lambda: nc.vector.tensor_tensor(out=a1[:], in0=lbi[:], in1=lim[:], op=ALU.mult), reads=RR + [RP], writes=[RP])
                op(V, lambda: nc.vector.tensor_tensor(out=qr[:], in0=qr[:], in1=a1[:], op=ALU.add), reads=[RP], writes=[RP])
                op(V, lambda: nc.vector.tensor_tensor(out=qr[:], in0=qr[:], in1=den[:], op=ALU.mult), reads=[RP], writes=[RP])
                op(V, lambda: nc.vector.tensor_tensor(out=qi[:], in0=lbi[:], in1=lre[:], op=ALU.mult), reads=RR + [RP], writes=[RP])
                op(V, lambda: nc.vector.tensor_tensor(out=a1[:], in0=lbr[:], in1=lim[:], op=ALU.mult), reads=RR + [RP], writes=[RP])
                op(V, lambda: nc.vector.tensor_tensor(out=qi[:], in0=qi[:], in1=a1[:], op=ALU.subtract), reads=[RP], writes=[RP])
                op(V, lambda: nc.vector.tensor_tensor(out=qi[:], in0=qi[:], in1=den[:], op=ALU.mult), reads=[RP], writes=[RP])
                sincos_small(sth, 0.0, float(TT))
                op(V, lambda: nc.vector.tensor_copy(out=sthk[:], in_=sth[:]), reads=[RP], writes=[RS])
                sincos_small(cth, 0.5 * PI, float(TT))
                op(V, lambda: nc.vector.tensor_copy(out=cthk[:], in_=cth[:]), reads=[RP], writes=[RS])
                qrb = qr[:].unsqueeze(2).broadcast_to([128, 16, 16])
                qib = qi[:].unsqueeze(2).broadcast_to([128, 16, 16])
                i3r = in3r[:].rearrange("p (g a h) -> p g a h", g=16, a=2)
                i3i = in3i[:].rearrange("p (g a h) -> p g a h", g=16, a=2)
                op(V, lambda: nc.vector.tensor_tensor(out=BBr[:], in0=Bre[:], in1=qrb, op=ALU.mult), reads=RR + [RP], writes=[RP] + ALIAS)
                op(V, lambda: nc.vector.tensor_tensor(out=i3r[:, :, 0, :], in0=Bim[:], in1=qib, op=ALU.mult), reads=RR + [RP], writes=[RP] + ALIAS)
                op(V, lambda: nc.vector.tensor_tensor(out=BBr[:], in0=BBr[:], in1=i3r[:, :, 0, :], op=ALU.subtract), reads=[RP], writes=[RP] + ALIAS)
                op(V, lambda: nc.vector.tensor_tensor(out=BBi[:], in0=Bim[:], in1=qrb, op=ALU.mult), reads=RR + [RP], writes=[RP] + ALIAS)
                op(V, lambda: nc.vector.tensor_tensor(out=i3r[:, :, 0, :], in0=Bre[:], in1=qib, op=ALU.mult), reads=RR + [RP], writes=[RP] + ALIAS)
                op(V, lambda: nc.vector.tensor_tensor(out=BBi[:], in0=BBi[:], in1=i3r[:, :, 0, :], op=ALU.add), reads=[RP], writes=[RP] + ALIAS)
                for a in range(2):
                    op(V, lambda a=a: nc.vector.tensor_scalar(out=i3r[:, :, a, :], in0=BBr[:], scalar1=pm[:, a:a + 1], scalar2=0.0, op0=ALU.mult, op1=ALU.add), reads=[RP] + RC, writes=[RP] + ALIAS)
                    op(V, lambda a=a: nc.vector.tensor_scalar(out=i3i[:, :, a, :], in0=BBi[:], scalar1=pm[:, a:a + 1], scalar2=0.0, op0=ALU.mult, op1=ALU.add), reads=[RP] + RC, writes=[RP] + ALIAS)
                for (i3, BT) in ((in3r, BTr), (in3i, BTi)):
                    for ct in range(4):
                        grp(P, [lambda ct=ct, i3=i3: nc.tensor.transpose(
                            out=pY[:, ct * 128:(ct + 1) * 128], in_=i3[:, ct * 128:(ct + 1) * 128], identity=identf[:])],
                            reads=[RP] + RC + ALIAS, writes=[res("pY")])
                    for jj in range(2):
                        op(V, lambda BT=BT, jj=jj: nc.vector.tensor_scalar(
                            out=BT[:, :, jj, :], in0=pY[:].rearrange("p (a b) -> p a b", a=4),
                            scalar1=pm[:, 2 + jj:3 + jj], scalar2=0.0, op0=ALU.mult, op1=ALU.add),
                           reads=[res("pY")] + RC, writes=[RS])
                op(G, lambda: nc.gpsimd.memset(Cpad[:], 0.0), writes=[RS])
                Cp6 = Cpad[:].rearrange("p (a q) r (qq g h) -> p a q r qq g h", q=4, qq=4, g=2)
                for q in range(4):
                    for ri, Cs, sgn in ((0, Cre, 1.0), (1, Cim, -1.0), (2, Cre, -1.0)):
                        Csv = Cs[:].rearrange("p (a q) h -> p a q h", q=4)
                        for a in range(2):
                            op(V, lambda q=q, ri=ri, Csv=Csv, a=a, sgn=sgn: nc.vector.tensor_scalar(
                                out=Cp6[:, :, q, ri, q, a, :], in0=Csv[:, :, q, :], scalar1=pm[:, a:a + 1],
                                scalar2=sgn, op0=ALU.mult, op1=ALU.mult),
                               reads=RR + RC, writes=[RS])
                for (tab, off) in ((sinT, 0.0), (cosT, 0.5 * PI)):
                    for c in range(8):
                        W3 = [res("qa0"), res("qb0"), res("qt1")]
                        op(V, lambda c=c: nc.vector.tensor_tensor(
                            out=t1[:], in0=th[:, 2 * c:2 * c + 2].unsqueeze(2).broadcast_to([128, 2, TT]),
                            in1=tio[:].unsqueeze(1).broadcast_to([128, 2, TT]), op=ALU.mult),
                           reads=[RP, RS] + RC, writes=W3)
                        op(V, lambda off=off: nc.vector.tensor_scalar(out=t1[:], in0=t1[:], scalar1=off, scalar2=1.0 / (2 * PI), op0=ALU.add, op1=ALU.mult), reads=[], writes=W3)
                        op(V, lambda: nc.vector.tensor_copy(out=kit[:], in_=t1[:]), writes=W3)
                        op(V, lambda: nc.vector.tensor_copy(out=t3[:], in_=kit[:]), writes=W3)
                        op(V, lambda: nc.vector.tensor_tensor(out=t1[:], in0=t1[:], in1=t3[:], op=ALU.subtract), writes=W3)
                        op(V, lambda: nc.vector.tensor_scalar(out=t3[:], in0=t1[:], scalar1=0.5, scalar2=1.0, op0=ALU.is_gt, op1=ALU.mult), writes=W3)
                        op(V, lambda: nc.vector.tensor_tensor(out=t1[:], in0=t1[:], in1=t3[:], op=ALU.subtract), writes=W3)
                        op(V, lambda: nc.vector.tensor_scalar(out=t3[:], in0=t1[:], scalar1=-0.5, scalar2=1.0, op0=ALU.is_lt, op1=ALU.mult), writes=W3)
                        op(V, lambda: nc.vector.tensor_tensor(out=t1[:], in0=t1[:], in1=t3[:], op=ALU.add), writes=W3)
                        op(A, lambda tab=tab, c=c: nc.scalar.activation(out=tab[:, 2 * c:2 * c + 2, :], in_=t1[:], func=AF.Sin, scale=2 * PI),
                           reads=W3, writes=[RS])
                op(G, lambda: nc.gpsimd.memset(Tin_re[:], 0.0), writes=[res("Tin")])
                op(G, lambda: nc.gpsimd.memset(Tin_im[:], 0.0), writes=[res("Tin")])
                op(G, lambda: nc.gpsimd.memset(xc[:], 0.0), writes=[res("xc")])

                stop_at(3)
                def emit_xa():
                    for j in range(2):
                        def ev_xa(bank, rb, j=j):
                            op(A, lambda: nc.scalar.copy(
                                out=xaf[:, 2 * j:2 * j + 2, :], in_=bank[:].rearrange("p (a b) -> p a b", a=2)),
                               reads=[rb], writes=[res("xaf")])
                            op(V, lambda: nc.vector.tensor_copy(
                                out=xab[:, 2 * j:2 * j + 2, :], in_=xaf[:, 2 * j:2 * j + 2, :]),
                               reads=[res("xaf")], writes=[res("xab")])
                        zcols(l, [2 * j, 2 * j + 1], ev_xa)

                def s5_gen():
                    for gh in range(16):
                        pr = gh % 2
                        ct = gh // 4
                        q = gh % 4
                        hb = 64 * (q // 2)
                        jz = q % 2
                        bank = pS[pr]
                        rbank = res("pS%d" % pr)
                        grp(P, [lambda bank=bank, hb=hb, ct=ct, jz=jz: nc.tensor.matmul(
                                    bank[:, 0:TT], BTr[hb:hb + 64, ct, jz, :], xab[hb:hb + 64, ct, :], start=True, stop=True),
                                lambda bank=bank, hb=hb, ct=ct, jz=jz: nc.tensor.matmul(
                                    bank[:, TT:2 * TT], BTi[hb:hb + 64, ct, jz, :], xab[hb:hb + 64, ct, :], start=True, stop=True)],
                            reads=[RS, res("xab")], writes=[rbank])
                        u2 = bank[:].rearrange("p (a b) -> p a b", a=2)
                        cs = cosT[:, gh:gh + 1, :].broadcast_to([128, 2, TT])
                        sn = sinT[:, gh:gh + 1, :].broadcast_to([128, 2, TT])
                        ra, rb_, ru, rt = (res("qa%d" % pr), res("qb%d" % pr), res("qu%d" % pr), res("qt%d" % pr))
                        op(V, lambda u2=u2, cs=cs, pr=pr: nc.vector.tensor_tensor(out=qa[pr][:], in0=u2, in1=cs, op=ALU.mult), reads=[rbank, RS], writes=[ra])
                        op(V, lambda u2=u2, sn=sn, pr=pr: nc.vector.tensor_tensor(out=qb[pr][:], in0=u2, in1=sn, op=ALU.mult), reads=[rbank, RS], writes=[rb_])
                        op(V, lambda pr=pr: nc.vector.tensor_tensor(out=qu[pr][:, 0, :], in0=qa[pr][:, 0, :], in1=qb[pr][:, 1, :], op=ALU.add), reads=[ra, rb_], writes=[ru])
                        op(V, lambda pr=pr: nc.vector.tensor_tensor(out=qu[pr][:, 1, :], in0=qa[pr][:, 1, :], in1=qb[pr][:, 0, :], op=ALU.subtract), reads=[ra, rb_], writes=[ru])
                        op(V, lambda gh=gh, pr=pr: nc.vector.tensor_tensor_scan(
                            out=qt[pr][:, 0, :], data0=rr[:, gh:gh + 1].broadcast_to([128, TT]), data1=qu[pr][:, 0, :],
                            initial=Tin_re[:, gh:gh + 1], op0=ALU.mult, op1=ALU.add), reads=[ru, RS, res("Tin")], writes=[rt])
                        op(V, lambda gh=gh, pr=pr: nc.vector.tensor_tensor_scan(
                            out=qt[pr][:, 1, :], data0=rr[:, gh:gh + 1].broadcast_to([128, TT]), data1=qu[pr][:, 1, :],
                            initial=Tin_im[:, gh:gh + 1], op0=ALU.mult, op1=ALU.add), reads=[ru, RS, res("Tin")], writes=[rt])
                        op(V, lambda cs=cs, pr=pr, q=q: nc.vector.tensor_tensor(out=Sb[:, 0:2, q, :], in0=qt[pr][:], in1=cs, op=ALU.mult), reads=[rt, RS], writes=[res("Sb")])
                        op(V, lambda sn=sn, pr=pr, q=q: nc.vector.tensor_tensor(out=Sb[:, 2:4, q, :], in0=qt[pr][:], in1=sn, op=ALU.mult), reads=[rt, RS], writes=[res("Sb")])
                        op(V, lambda gh=gh, pr=pr: nc.vector.tensor_copy(out=Sl[:, :, gh], in_=qt[pr][:, :, TT - 1]), reads=[rt], writes=[res("Sl")])
                        if q == 3:
                            fl = []
                            n = 0
                            for qq in range(4):
                                for prod, var in ((0, 0), (1, 1), (2, 1), (3, 2)):
                                    fl.append(lambda qq=qq, prod=prod, var=var, ct=ct, n=n: nc.tensor.matmul(
                                        pY[:, 0:TT], Cpad[:, 4 * ct + qq, var, :], Sb[:, prod, qq, :],
                                        start=(n == 0), stop=(n == 15)))
                                    n += 1
                            grp(P, fl, reads=[RS, res("Sb")], writes=[res("pY")])
                            op(V, lambda ct=ct: nc.vector.scalar_tensor_tensor(
                                out=ypre[:], in0=xaf[:, ct, :], scalar=dsk[:, ct:ct + 1], in1=pY[:, 0:TT],
                                op0=ALU.mult, op1=ALU.add),
                               reads=[res("pY"), res("xaf"), res("lsmall")], writes=[res("ypre")])
                            op(A, lambda ct=ct: nc.scalar.activation(out=yg[:, ct, :], in_=ypre[:], func=gelu_f),
                               reads=[res("ypre")], writes=[res("yg")])
                            op(G, lambda ct=ct: nc.gpsimd.tensor_copy(out=ygb[:, ct, :], in_=yg[:, ct, :]),
                               reads=[res("yg")], writes=[res("ygb")])
                        yield
                    op(G, lambda: nc.gpsimd.tensor_tensor(out=cr[:, 0, :], in0=Sl[:, 0, :], in1=cthk[:], op=ALU.mult), reads=[res("Sl"), RS], writes=[res("cr")])
                    op(G, lambda: nc.gpsimd.tensor_tensor(out=cr[:, 1, :], in0=Sl[:, 1, :], in1=sthk[:], op=ALU.mult), reads=[res("Sl"), RS], writes=[res("cr")])
                    op(G, lambda: nc.gpsimd.tensor_tensor(out=cr[:, 2, :], in0=Sl[:, 1, :], in1=cthk[:], op=ALU.mult), reads=[res("Sl"), RS], writes=[res("cr")])
                    op(G, lambda: nc.gpsimd.tensor_tensor(out=cr[:, 3, :], in0=Sl[:, 0, :], in1=sthk[:], op=ALU.mult), reads=[res("Sl"), RS], writes=[res("cr")])
                    op(G, lambda: nc.gpsimd.tensor_tensor(out=Tin_re[:], in0=cr[:, 0, :], in1=cr[:, 1, :], op=ALU.subtract), reads=[res("cr")], writes=[res("Tin")])
                    op(G, lambda: nc.gpsimd.tensor_tensor(out=Tin_im[:], in0=cr[:, 2, :], in1=cr[:, 3, :], op=ALU.add), reads=[res("cr")], writes=[res("Tin")])
                    for c2 in range(4):
                        grp(P, [(lambda k=k, c2=c2: nc.tensor.matmul(
                            pY[:, 0:TT], wglu[:, k, c2 * 128:(c2 + 1) * 128], ygb[:, k, :],
                            start=(k == 0), stop=(k == 3))) for k in range(4)],
                            reads=[RL, res("ygb")], writes=[res("pY")])
                        op(A, lambda c2=c2: nc.scalar.activation(
                            out=tA[0][:], in_=pY[:, 0:TT], func=AF.Tanh, bias=bglh[:, c2:c2 + 1], scale=0.5),
                           reads=[res("pY"), res("lsmall")], writes=[res("tA0")])
                        op(V, lambda c2=c2: nc.vector.scalar_tensor_tensor(
                            out=tP[0][:], in0=tA[0][:], scalar=1.0, in1=yg[:, c2, :], op0=ALU.add, op1=ALU.mult),
                           reads=[res("tA0"), res("yg")], writes=[res("tP0")])
                        op(V, lambda c2=c2: nc.vector.scalar_tensor_tensor(
                            out=yT[:, c2, :], in0=tP[0][:], scalar=0.5, in1=sga[:, c2, :], op0=ALU.mult, op1=ALU.mult),
                           reads=[res("tP0"), res("sga")], writes=[res("yT")])
                        yield


                for it in range(nstep):
                    t0 = (it - 2) * TT
                    par = it % 2
                    gi = gtile[0]
                    gtile[0] += 1
                    hi = gi % 2
                    if gi == 0:
                        for _ in prefetch_gen(None, 0, 0):
                            pass
                    CUR["hT"] = hTs[hi]
                    CUR["hi"] = hi
                    hT = hTs[hi]
                    if it == 0:
                        emit_xa()
                        s5it = s5_gen()
                    next(s5it, None)
                    if it + 1 < nstep:
                        pf = prefetch_gen(None, it + 1, 1 - hi)
                    else:
                        pf = iter(())

                    for j in range(2):
                        def ev_xc(bank, rb, j=j):
                            op(A, lambda: nc.scalar.copy(
                                out=xc[:, 2 * j:2 * j + 2, 16:16 + TT], in_=bank[:].rearrange("p (a b) -> p a b", a=2)),
                               reads=[rb], writes=[res("xc")])
                        zcols(l, [20 + 2 * j, 21 + 2 * j], ev_xc)
                    next(s5it, None)

                    def ev_silu(dst, rname):
                        def f(bank, rb):
                            op(A, lambda: nc.scalar.activation(
                                out=dst, in_=bank[:].rearrange("p (a b) -> p a b", a=2), func=AF.Silu),
                               reads=[rb], writes=[res(rname)])
                        return f
                    for j in range(2):
                        zcols(l, [36 + 2 * j, 37 + 2 * j], ev_silu(sgc[:, 2 * j:2 * j + 2, :], "sgc"))
                    for g in range(4):
                        srcb = xc[:, g, :]
                        cur = None
                        bufs = [pa, pb]
                        rn = ["pa", "pb"]
                        sh = 1
                        for lev in range(g + 1):
                            o = bufs[lev % 2]
                            lo = 2 * sh - 1
                            i_ap = srcb if cur is None else cur[:]
                            rsrc = res("xc") if cur is None else res(rn[(lev - 1) % 2])
                            op(G, lambda o=o, i_ap=i_ap, lo=lo, sh=sh: nc.gpsimd.tensor_tensor(
                                out=o[:, lo:16 + TT], in0=i_ap[:, lo:16 + TT], in1=i_ap[:, lo - sh:16 + TT - sh], op=ALU.add),
                               reads=[rsrc], writes=[res(rn[lev % 2])])
                            cur = o
                            sh *= 2
                        rcur = res(rn[g % 2])
                        op(V, lambda cur=cur, g=g: nc.vector.scalar_tensor_tensor(
                            out=pp[:, g, :], in0=cur[:, 16:16 + TT], scalar=1.0 / (2 ** (g + 1)),
                            in1=xc[:, g, 16:16 + TT], op0=ALU.mult, op1=ALU.subtract),
                           reads=[rcur, res("xc")], writes=[res("pp")])
                        if it in (0, 2):
                            pdv_ = pdiv0 if it == 0 else pdiv2
                            op(V, lambda cur=cur, g=g, pdv_=pdv_: nc.vector.tensor_tensor(
                                out=cur[:, 16:32], in0=cur[:, 16:32], in1=pdv_[:, g, :], op=ALU.mult),
                               reads=[rcur] + RC, writes=[rcur])
                            op(V, lambda cur=cur, g=g: nc.vector.tensor_tensor(
                                out=pp[:, g, 0:16], in0=cur[:, 16:32], in1=xc[:, g, 16:32], op=ALU.subtract),
                               reads=[rcur, res("xc")], writes=[res("pp")])
                    op(G, lambda: nc.gpsimd.tensor_copy(out=xc[:, :, 0:16], in_=xc[:, :, TT:TT + 16]),
                       reads=[res("xc")], writes=[res("xc")])
                    def pool_mm():
                        for j in range(2):
                            b = pA_i[0] % 2
                            pA_i[0] += 1
                            for h in range(2):
                                g = 2 * j + h
                                grp(P, [lambda g=g, h=h, b=b: nc.tensor.matmul(
                                    pA[b][:, h * TT:(h + 1) * TT], wpool[:, g, :], pp[:, g, :], start=True, stop=True)],
                                    reads=[RL, res("pp")], writes=[res("pA%d" % b)])
                                op(A, lambda g=g, h=h, b=b: nc.scalar.activation(
                                    out=tA[g % 2][:], in_=pA[b][:, h * TT:(h + 1) * TT], func=AF.Copy, scale=psc[:, g:g + 1]),
                                   reads=[res("pA%d" % b), res("lsmall")], writes=[res("tA%d" % (g % 2))])
                                op(G, lambda g=g: nc.gpsimd.tensor_tensor(
                                    out=yT[:, 12 + g, :], in0=tA[g % 2][:], in1=sgc[:, g, :], op=ALU.mult),
                                   reads=[res("tA%d" % (g % 2)), res("sgc")], writes=[res("yT")])


                    next(s5it, None)
                    for j in range(2):
                        zcols(l, [24 + 2 * j, 25 + 2 * j], ev_silu(sga[:, 2 * j:2 * j + 2, :], "sga"))
                    next(s5it, None)

                    for c in range(2):
                        def ev_v(ts_, bank, rb, c=c):
                            op(A, lambda: nc.scalar.activation(
                                out=gv[ts_][:, c * 512:(c + 1) * 512], in_=bank[:], func=gelu_f),
                               reads=[rb], writes=[res("gv%d" % ts_)])
                            op(V, lambda: nc.vector.bn_stats(
                                out=bst[ts_][:, c, :], in_=gv[ts_][:, c * 512:(c + 1) * 512]),
                               reads=[res("gv%d" % ts_)], writes=[res("bst%d" % ts_)])
                        tokmaj([wsc_in[l, 12 + 4 * c + h] for h in range(4)], "wsc%d" % l,
                               lambda k, ts_: hT[:, k, ts_ * 128:(ts_ + 1) * 128], [res("hT%d" % hi)], ev_v,
                               after=lambda: next(s5it, None))
                        if c == 0:
                            pool_mm()
                    for ts_ in range(NS):
                        op(V, lambda ts_=ts_: nc.vector.bn_aggr(out=mv[:, ts_, :], in_=bst[ts_][:].rearrange("p a b -> p (a b)")),
                           reads=[res("bst%d" % ts_)], writes=[res("mv")])
                    op(V, lambda: nc.vector.tensor_scalar(
                        out=mv[:, :, 1], in0=mv[:, :, 1], scalar1=1e-5, scalar2=0.0, op0=ALU.add, op1=ALU.add),
                       reads=[res("mv")], writes=[res("mv")])
                    op(A, lambda: nc.scalar.activation(out=mv[:, :, 1], in_=mv[:, :, 1], func=AF.Sqrt),
                       reads=[res("mv")], writes=[res("mv")])
                    op(V, lambda: nc.vector.reciprocal(out=mv[:, :, 1], in_=mv[:, :, 1]),
                       reads=[res("mv")], writes=[res("mv")])
                    for ts_ in range(NS):
                        op(V, lambda ts_=ts_: nc.vector.tensor_scalar(
                            out=vn[ts_][:], in0=gv[ts_][:], scalar1=mv[:, ts_, 0:1], scalar2=mv[:, ts_, 1:2],
                            op0=ALU.subtract, op1=ALU.mult),
                           reads=[res("gv%d" % ts_), res("mv")], writes=[res("vn%d" % ts_)])

                    next(pf, None)
                    for hh in range(8):
                        e = hh % 2
                        def pre_s(hh=hh, e=e):
                            grp(P, [(lambda ts_=ts_, hh=hh: nc.tensor.matmul(
                                pY[:, ts_ * 128:(ts_ + 1) * 128], vn[ts_][:, hh * 128:(hh + 1) * 128], wsT[:, hh, :],
                                start=True, stop=True)) for ts_ in range(NS)],
                                reads=[res("vn0"), res("vn1"), RL], writes=[res("pY")])
                            op(A, lambda hh=hh, e=e: nc.scalar.activation(
                                out=tS[e][:], in_=pY[:, 0:256], func=AF.Copy, scale=lng[:, hh:hh + 1]),
                               reads=[res("pY"), res("lsmall")], writes=[res("tS%d" % e)])
                            op(G, lambda hh=hh, e=e: nc.gpsimd.tensor_tensor(
                                out=tS[e][:].rearrange("p (a b) -> p a b", a=2), in0=tS[e][:].rearrange("p (a b) -> p a b", a=2),
                                in1=Bt[:, hh:hh + 1, :].broadcast_to([128, 2, 128]), op=ALU.add),
                               reads=[res("tS%d" % e), res("lsmall")], writes=[res("tS%d" % e)])

                        def ev_sgu(bank, rb, hh=hh, e=e):
                            op(A, lambda: nc.scalar.activation(out=tA[e][:], in_=bank[:, 0:TT], func=gelu_f),
                               reads=[rb], writes=[res("tA%d" % e)])
                            op(A, lambda: nc.scalar.activation(out=tG[e][:], in_=bank[:, TT:2 * TT], func=AF.Tanh, scale=0.5),
                               reads=[rb], writes=[res("tG%d" % e)])
                            op(A, lambda: nc.scalar.copy(out=tW[:, e * TT:(e + 1) * TT], in_=bank[:, TT:2 * TT]),
                               reads=[rb], writes=[res("tW%d" % e)])
                            op(V, lambda: nc.vector.scalar_tensor_tensor(
                                out=tG[e][:], in0=tG[e][:], scalar=1.0, in1=tW[:, e * TT:(e + 1) * TT], op0=ALU.add, op1=ALU.mult),
                               reads=[res("tW%d" % e), res("tG%d" % e)], writes=[res("tG%d" % e)])
                            op(G, lambda: nc.gpsimd.tensor_tensor(out=tP[e][:], in0=tA[e][:], in1=tG[e][:], op=ALU.mult),
                               reads=[res("tA%d" % e), res("tG%d" % e)], writes=[res("tP%d" % e)])
                            op(V, lambda: nc.vector.scalar_tensor_tensor(
                                out=yT[:, 4 + hh, :], in0=tP[e][:], scalar=0.5, in1=tS[e][:], op0=ALU.mult, op1=ALU.mult),
                               reads=[res("tP%d" % e), res("tS%d" % e)], writes=[res("yT")])
                        def pre_all(pre_s=pre_s):
                            next(s5it, None)
                            next(pf, None)
                            pre_s()
                        zcols(l, [4 + hh, 28 + hh], ev_sgu, pre=pre_all)
                    for _ in s5it:
                        pass
                    for _ in pf:
                        pass

                    for ts_ in range(NS):
                        dma(G, sl_x[ts_], [lambda ts_=ts_: nc.gpsimd.dma_start(
                            out=xt[ts_][:], in_=usrc(it, ts_))],
                            reads=[res("U%d" % par)], writes=[res("xt%d" % ts_)])
                    if it + 1 < nstep:
                        CUR["hT"] = hTs[1 - hi]
                        CUR["hi"] = 1 - hi
                        emit_xa()
                        CUR["hT"] = hTs[hi]
                        CUR["hi"] = hi
                        s5it = s5_gen()
                    else:
                        s5it = iter(())
                    wo_cnt = [0]

                    def after_o():
                        wo_cnt[0] += 1
                        next(s5it, None)
                    for c in range(4):
                        def ev_o(ts_, bank, rb, c=c):
                            op(V, lambda: nc.vector.tensor_tensor(
                                out=xt[ts_][:, c * 512:(c + 1) * 512], in0=xt[ts_][:, c * 512:(c + 1) * 512],
                                in1=bank[:], op=ALU.add),
                               reads=[rb], writes=[res("xt%d" % ts_)])
                        tokmaj([wsc_out[l, 4 * c + h] for h in range(4)], "wsc%d" % l,
                               lambda k, ts_: yT[:, k, ts_ * 128:(ts_ + 1) * 128], [res("yT")], ev_o, after=after_o)
                    if it < ntile:
                        for ts_ in range(NS):
                            dma(G, sl_sd[ts_], [lambda ts_=ts_: nc.gpsimd.dma_start(
                                out=sendb[par].ap()[ts_ * 128:(ts_ + 1) * 128, :], in_=xt[ts_][:])],
                                reads=[res("xt%d" % ts_)], writes=[res("send%d" % par)])
                        _deps(G, [res("send%d" % par)], [res("U%d" % par)])
                        ins_ = nc.gpsimd.collective_compute(
                            "AllGather", ALU.bypass, replica_groups=[[0, 1], [2, 3], [4, 5], [6, 7]],
                            ins=[sendb[par].ap().opt()], outs=[Ub[par].ap().rearrange("a t d -> (a t) d").opt()])
                        ins_.then_inc(sl_cc.sem)
                        sl_cc.n += 1
                        _upd((sl_cc.key, sl_cc.sem, sl_cc.n), [res("send%d" % par)], [res("U%d" % par)])
                        tn_ = min(it + 2, ntile - 1)
                        dma(G, sl_u, [lambda tn_=tn_: nc.gpsimd.dma_start(
                            out=Ub[par].ap()[1], in_=x_in[tn_ * TT:(tn_ + 1) * TT, :])],
                            writes=[res("U%d" % par)])
                    for ts_ in range(NS):
                        op(A, lambda ts_=ts_: nc.scalar.activation(
                            out=hp[:], in_=xt[ts_][:], func=AF.Square, accum_out=ss[:, 2 + ts_:3 + ts_]),
                           reads=[res("xt%d" % ts_)], writes=[res("hp"), res("ss2")])
                    op(V, lambda: nc.vector.tensor_scalar(
                        out=rstd[:, 2:4], in0=ss[:, 2:4], scalar1=1.0 / D, scalar2=1e-6,
                        op0=ALU.mult, op1=ALU.add), reads=[res("ss2")], writes=[res("rstd2")])
                    op(A, lambda: nc.scalar.activation(out=rstd[:, 2:4], in_=rstd[:, 2:4], func=AF.Sqrt),
                       reads=[res("rstd2")], writes=[res("rstd2")])
                    op(V, lambda: nc.vector.reciprocal(out=rstd[:, 2:4], in_=rstd[:, 2:4]),
                       reads=[res("rstd2")], writes=[res("rstd2")])
                    for ts_ in range(NS):
                        rx = res("xt%d" % ts_)
                        op(A, lambda ts_=ts_: nc.scalar.activation(
                            out=xt[ts_][:], in_=xt[ts_][:], func=AF.Copy, scale=rstd[:, 2 + ts_:3 + ts_]),
                           reads=[rx, res("rstd2")], writes=[rx])
                        op(V, lambda ts_=ts_: nc.vector.tensor_tensor(
                            out=xt[ts_][:], in0=xt[ts_][:], in1=fg_rep[:], op=ALU.mult),
                           reads=[rx] + RC, writes=[rx])
                        if it >= 2:
                            dma(G, sl_st[ts_], [lambda ts_=ts_: nc.gpsimd.dma_start(
                                out=out[t0 + ts_ * 128:t0 + (ts_ + 1) * 128, :], in_=xt[ts_][:])],
                                reads=[rx], writes=[res("outd")])
        except _Stop:
            pass
        for s in sl_st + sl_sd:
            if s.n:
                nc.gpsimd.wait_ge(s.sem, s.n)
    return nc


vslots = {}
_CACHE = {}


def _consts():
    w = [2, 4, 8, 16]
    pos = np.arange(1, 17, dtype=np.float32)
    pdiv0 = np.stack([1.0 / np.minimum(pos, float(wi)) for wi in w]).astype(np.float32).reshape(1, -1)
    pdivc = np.stack([np.full(256, 1.0 / wi, np.float32) for wi in w]).reshape(1, -1)
    pm = np.zeros((128, 4), np.float32)
    pm[:64, 0] = 1.0
    pm[64:, 1] = 1.0
    for p in range(128):
        pm[p, 2 + ((p // 32) % 2)] = 1.0
    return {
        "c_identb": np.eye(128, dtype=np.float32).astype(ml_dtypes.bfloat16),
        "c_identf": np.eye(128, dtype=np.float32),
        "c_tril": np.tril(np.ones((128, 128), np.float32)),
        "c_tio": np.arange(256, dtype=np.float32).reshape(1, 256),
        "c_pdiv0": pdiv0, "c_pdiv2": np.ascontiguousarray(pdivc.reshape(4, 256)[:, :16]).reshape(1, -1), "c_pm": pm,
    }


def kernel(**inputs):
    seq, batch = CFG["seq"], CFG["batch"]
    key = (seq, CFG["gelu"], CFG.get("stop"))
    if key not in _CACHE:
        _CACHE[key] = build(seq)
    nc = _CACHE[key]
    x = np.ascontiguousarray(np.asarray(inputs["x"], dtype=np.float32))
    cst = _consts()
    per_layer = {}
    for k, v in inputs.items():
        if k in ("x", "final_g"):
            continue
        v = np.asarray(v, dtype=np.float32)
        per_layer[k] = [np.ascontiguousarray(v[r:r + 1]) for r in range(2)]
    fg = np.ascontiguousarray(np.asarray(inputs["final_g"], dtype=np.float32).reshape(1, D))
    zeros_x = np.zeros((seq, D), np.float32)
    in_maps = []
    for c in range(2 * batch):
        b, role = c // 2, c % 2
        m = {k: v[role] for k, v in per_layer.items()}
        m["final_g"] = fg
        m.update(cst)
        m["x"] = x[b] if role == 0 else zeros_x
        m["c_sel"] = np.array([[1 - role]], np.int32)
        if role == 1:
            m["c_pdiv0"], m["c_pdiv2"] = cst["c_pdiv2"], cst["c_pdiv0"]
        in_maps.append(m)
    res = run_bass_kernel_spmd(nc, in_maps, core_ids=list(range(2 * batch)))
    return np.stack([np.asarray(res.results[2 * b + 1]["out"]) for b in range(batch)], axis=0).astype(np.float32)
```
